# Optimizing a Trainium2 kernel written in Bass

```python
import math
import jax, jax.numpy as jnp
from jax import lax
import numpy as np

D_MODEL = 4096
BATCH = 2
SEQ = 8192
DEPTH = 2

CHUNK = 64
EPS = 1e-6
N_EVEN = (DEPTH + 1) // 2
N_ODD = DEPTH // 2

A_WIDTH = D_MODEL // 2
A_KEY = 128
A_HEADS = A_WIDTH // A_KEY
A_VAL = A_WIDTH // A_HEADS
B_WIDTH = D_MODEL // 2
B_GROUP = 16
B_GROUPS = B_WIDTH // B_GROUP
B_STATE = 64
S5_DT_MIN = 1e-3
S5_DT_MAX = 1e-1
AB_IN = 4 * A_WIDTH + B_WIDTH
AB_OUT = A_WIDTH + B_WIDTH
C_HEADS = 16
C_QK = D_MODEL
C_V = 2 * D_MODEL
C_DK = C_QK // C_HEADS
C_DV = C_V // C_HEADS
C_IN = 2 * C_QK + 2 * C_V
ROPE_BASE = 10000.0
PEER_HEADS = 8
PEER_NKEYS = 128
PEER_EXPERTS = PEER_NKEYS * PEER_NKEYS
PEER_TOPK = 16
PEER_KEY_HALF = 128
PEER_QDIM = 2 * PEER_KEY_HALF
PEER_TOKEN_BLOCK = 128

kernel_name = "hybrid_hgrn2_s5_retention_peer"


def rmsnorm(x, g):
    x32 = x.astype(jnp.float32)
    y = x32 * lax.rsqrt(jnp.mean(x32 * x32, axis=-1, keepdims=True) + EPS)
    return (y * g.astype(jnp.float32)).astype(x.dtype)


def head_rms(o):
    return o * lax.rsqrt(jnp.mean(o * o, axis=-1, keepdims=True) + EPS)


def to_chunks(t):
    b, s, h, d = t.shape
    return t.reshape(b, s // CHUNK, CHUNK, h, d).transpose(1, 0, 3, 2, 4)


def from_chunks(t):
    nc, b, h, c, d = t.shape
    return t.transpose(1, 0, 3, 2, 4).reshape(b, nc * c, h, d)


def hgrn2_mixer(q_raw, f_raw, i_raw, g_raw, lb):
    b, s, _ = q_raw.shape
    f = lb + (1.0 - lb) * jax.nn.sigmoid(f_raw.astype(jnp.float32))
    logf = jnp.log(f)
    k = 1.0 - f
    q = jax.nn.silu(q_raw.astype(jnp.float32))
    v = i_raw.astype(jnp.float32)
    qc = to_chunks(q.reshape(b, s, A_HEADS, A_KEY))
    lc = to_chunks(logf.reshape(b, s, A_HEADS, A_KEY))
    kc = to_chunks(k.reshape(b, s, A_HEADS, A_KEY))
    vc = to_chunks(v.reshape(b, s, A_HEADS, A_VAL))
    causal = jnp.tril(jnp.ones((CHUNK, CHUNK), dtype=bool))[:, :, None]

    def step(state, xs):
        qt, lt, kt, vt = xs
        cum = jnp.cumsum(lt, axis=2)
        diff = cum[:, :, :, None, :] - cum[:, :, None, :, :]
        decay = jnp.exp(jnp.where(causal, diff, -jnp.inf))
        attn = jnp.einsum('bhtd,bhtsd,bhsd->bhts', qt, decay, kt)
        o = jnp.einsum('bhts,bhse->bhte', attn, vt) + \
            jnp.einsum('bhtd,bhde->bhte', qt * jnp.exp(cum), state)
        last = cum[:, :, -1:, :]
        state = jnp.exp(last[:, :, 0, :])[..., None] * state + \
            jnp.einsum('bhsd,bhse->bhde', kt * jnp.exp(last - cum), vt)
        return state, o

    s0 = jnp.zeros((b, A_HEADS, A_KEY, A_VAL), jnp.float32)
    _, o = lax.scan(step, s0, (qc, lc, kc, vc))
    o = head_rms(from_chunks(o)).reshape(b, s, A_WIDTH)
    return o * jax.nn.silu(g_raw.astype(jnp.float32))


def _complex_affine_combine(e1, e2):
    a1r, a1i, b1r, b1i = e1
    a2r, a2i, b2r, b2i = e2
    return (a2r * a1r - a2i * a1i,
            a2r * a1i + a2i * a1r,
            a2r * b1r - a2i * b1i + b2r,
            a2r * b1i + a2i * b1r + b2i)


def s5_mixer(u_raw, a_re, a_im, log_dt, b_re, b_im, c_re, c_im, d_skip, w_glu):
    b, s, _ = u_raw.shape
    u = u_raw.astype(jnp.float32).reshape(b, s, B_GROUPS, B_GROUP)
    dt = jnp.exp(log_dt.astype(jnp.float32))[:, None]
    lam_re = jnp.minimum(a_re.astype(jnp.float32), -1e-4)
    lam_im = a_im.astype(jnp.float32)
    mag = jnp.exp(lam_re * dt)
    ang = lam_im * dt
    abar_re = mag * jnp.cos(ang)
    abar_im = mag * jnp.sin(ang)
    nr = abar_re - 1.0
    ni = abar_im
    den = lam_re * lam_re + lam_im * lam_im
    z_re = (nr * lam_re + ni * lam_im) / den
    z_im = (ni * lam_re - nr * lam_im) / den
    br = b_re.astype(jnp.float32)
    bi = b_im.astype(jnp.float32)
    bbar_re = z_re[..., None] * br - z_im[..., None] * bi
    bbar_im = z_re[..., None] * bi + z_im[..., None] * br
    ut = jnp.swapaxes(u, 0, 1)
    bu_re = jnp.einsum('sbgh,gph->sbgp', ut, bbar_re)
    bu_im = jnp.einsum('sbgh,gph->sbgp', ut, bbar_im)
    at_re = jnp.broadcast_to(abar_re, (s, 1, B_GROUPS, B_STATE))
    at_im = jnp.broadcast_to(abar_im, (s, 1, B_GROUPS, B_STATE))
    _, _, x_re, x_im = lax.associative_scan(
        _complex_affine_combine, (at_re, at_im, bu_re, bu_im), axis=0)
    y = jnp.einsum('sbgp,ghp->bsgh', x_re, c_re.astype(jnp.float32)) - \
        jnp.einsum('sbgp,ghp->bsgh', x_im, c_im.astype(jnp.float32)) + \
        d_skip.astype(jnp.float32) * u
    y = jax.nn.gelu(y.reshape(b, s, B_WIDTH), approximate=False)
    return y * jax.nn.sigmoid(y @ w_glu.astype(jnp.float32))


def ab_layer(h, w_in, lb, a_re, a_im, log_dt, b_re, b_im, c_re, c_im, d_skip, w_glu, w_out):
    z = h @ w_in
    q_a = z[..., :A_WIDTH]
    f_a = z[..., A_WIDTH:2 * A_WIDTH]
    i_a = z[..., 2 * A_WIDTH:3 * A_WIDTH]
    g_a = z[..., 3 * A_WIDTH:4 * A_WIDTH]
    u_b = z[..., 4 * A_WIDTH:]
    o_a = hgrn2_mixer(q_a, f_a, i_a, g_a, lb)
    o_b = s5_mixer(u_b, a_re, a_im, log_dt, b_re, b_im, c_re, c_im, d_skip, w_glu)
    o = jnp.concatenate([o_a, o_b], axis=-1).astype(h.dtype)
    return o @ w_out


def rotary(t):
    s, d = t.shape[1], t.shape[3]
    pos = jnp.arange(s, dtype=jnp.float32)
    inv_freq = ROPE_BASE ** (-jnp.arange(0, d, 2, dtype=jnp.float32) / d)
    ang = pos[:, None] * inv_freq[None, :]
    cos = jnp.concatenate([jnp.cos(ang), jnp.cos(ang)], -1)[None, :, None, :]
    sin = jnp.concatenate([jnp.sin(ang), jnp.sin(ang)], -1)[None, :, None, :]
    t1, t2 = t[..., : d // 2], t[..., d // 2:]
    return t * cos + jnp.concatenate([-t2, t1], -1) * sin


def retention_layer(h, w_in, w_out):
    b, s, _ = h.shape
    z = (h @ w_in).astype(jnp.float32)
    q = rotary(z[..., :C_QK].reshape(b, s, C_HEADS, C_DK))
    k = rotary(z[..., C_QK:2 * C_QK].reshape(b, s, C_HEADS, C_DK)) * (C_DK ** -0.5)
    v = z[..., 2 * C_QK:2 * C_QK + C_V].reshape(b, s, C_HEADS, C_DV)
    g = z[..., 2 * C_QK + C_V:]
    log_gamma = jnp.log(1.0 - 2.0 ** (-5.0 - jnp.arange(C_HEADS, dtype=jnp.float32)))
    idx = jnp.arange(CHUNK, dtype=jnp.float32)
    intra_decay = jnp.exp(jnp.abs(idx[:, None] - idx[None, :])[None] * log_gamma[:, None, None])
    q_decay = jnp.exp((idx + 1.0)[None, :] * log_gamma[:, None])[:, :, None]
    k_decay = jnp.exp((CHUNK - 1.0 - idx)[None, :] * log_gamma[:, None])[:, :, None]
    chunk_decay = jnp.exp(CHUNK * log_gamma)[:, None, None]

    def step(r, xs):
        qt, kt, vt = xs
        scores = jnp.einsum('bhtd,bhsd->bhts', qt, kt) * intra_decay
        o = jnp.einsum('bhts,bhse->bhte', scores, vt) + \
            jnp.einsum('bhtd,bhde->bhte', qt * q_decay, r)
        r = chunk_decay * r + jnp.einsum('bhsd,bhse->bhde', kt * k_decay, vt)
        return r, o

    r0 = jnp.zeros((b, C_HEADS, C_DK, C_DV), jnp.float32)
    _, o = lax.scan(step, r0, (to_chunks(q), to_chunks(k), to_chunks(v)))
    o = head_rms(from_chunks(o)).reshape(b, s, C_V) * jax.nn.silu(g)
    return o.astype(h.dtype) @ w_out


def peer_ffn(xn, w_q, sub_keys, u_tab, v_tab):
    b, s, d = xn.shape
    blocks = xn.reshape(-1, PEER_TOKEN_BLOCK, d)
    keys32 = sub_keys.astype(jnp.float32)

    def one_block(xb):
        t = xb.shape[0]
        q = (xb @ w_q).astype(jnp.float32).reshape(t, PEER_HEADS, 2, PEER_KEY_HALF)
        sc = jnp.einsum('thpk,hpnk->thpn', q, keys32)
        s1, i1 = lax.top_k(sc[:, :, 0], PEER_TOPK)
        s2, i2 = lax.top_k(sc[:, :, 1], PEER_TOPK)
        cand = (s1[..., :, None] + s2[..., None, :]).reshape(t, PEER_HEADS, PEER_TOPK * PEER_TOPK)
        top_s, top_c = lax.top_k(cand, PEER_TOPK)
        e1 = jnp.take_along_axis(i1, top_c // PEER_TOPK, axis=-1)
        e2 = jnp.take_along_axis(i2, top_c % PEER_TOPK, axis=-1)
        experts = (e1 * PEER_NKEYS + e2).reshape(t, PEER_HEADS * PEER_TOPK)
        gate = jax.nn.softmax(top_s, axis=-1).reshape(t, PEER_HEADS * PEER_TOPK)
        u_sel = jnp.take(u_tab, experts, axis=0)
        v_sel = jnp.take(v_tab, experts, axis=0)
        hid = jax.nn.gelu(jnp.einsum('td,ted->te', xb, u_sel).astype(jnp.float32), approximate=False)
        w = (gate * hid).astype(xb.dtype)
        return jnp.einsum('te,ted->td', w, v_sel)

    return lax.map(one_block, blocks).reshape(b, s, d)


def setup_inputs(seed: int = 0) -> dict:
    key = jax.random.key(seed)
    ks = jax.random.split(key, 24)
    f32 = jnp.float32
    nrm = lambda k, shape, sc: jax.random.normal(k, shape, f32) * sc
    return {
        "x": nrm(ks[0], (BATCH, SEQ, D_MODEL), 1.0),
        "mix_norm_g": 1.0 + nrm(ks[1], (DEPTH, D_MODEL), 0.02),
        "ab_w_in": nrm(ks[2], (N_EVEN, D_MODEL, AB_IN), D_MODEL ** -0.5),
        "a_lb_param": nrm(ks[3], (N_EVEN + 1, A_WIDTH), 0.1),
        "b_a_re": -0.5 + nrm(ks[4], (N_EVEN, B_GROUPS, B_STATE), 0.01),
        "b_a_im": jnp.pi * jnp.arange(B_STATE, dtype=f32) + nrm(ks[5], (N_EVEN, B_GROUPS, B_STATE), 0.01),
        "b_log_dt": jax.random.uniform(ks[6], (N_EVEN, B_GROUPS), f32,
                                       math.log(S5_DT_MIN), math.log(S5_DT_MAX)),
        "b_b_re": nrm(ks[7], (N_EVEN, B_GROUPS, B_STATE, B_GROUP), (2 * B_GROUP) ** -0.5),
        "b_b_im": nrm(ks[8], (N_EVEN, B_GROUPS, B_STATE, B_GROUP), (2 * B_GROUP) ** -0.5),
        "b_c_re": nrm(ks[9], (N_EVEN, B_GROUPS, B_GROUP, B_STATE), (2 * B_STATE) ** -0.5),
        "b_c_im": nrm(ks[10], (N_EVEN, B_GROUPS, B_GROUP, B_STATE), (2 * B_STATE) ** -0.5),
        "b_d": nrm(ks[11], (N_EVEN, B_GROUPS, B_GROUP), 1.0),
        "b_w_glu": nrm(ks[12], (N_EVEN, B_WIDTH, B_WIDTH), B_WIDTH ** -0.5),
        "ab_w_out": nrm(ks[13], (N_EVEN, AB_OUT, D_MODEL), AB_OUT ** -0.5),
        "c_w_in": nrm(ks[14], (N_ODD, D_MODEL, C_IN), D_MODEL ** -0.5),
        "c_w_out": nrm(ks[15], (N_ODD, C_V, D_MODEL), C_V ** -0.5),
        "ffn_norm_g": 1.0 + nrm(ks[16], (DEPTH, D_MODEL), 0.02),
        "peer_w_q": nrm(ks[17], (DEPTH, D_MODEL, PEER_HEADS * PEER_QDIM), D_MODEL ** -0.5),
        "peer_sub_keys": nrm(ks[18], (DEPTH, PEER_HEADS, 2, PEER_NKEYS, PEER_KEY_HALF), PEER_KEY_HALF ** -0.5),
        "peer_u": nrm(ks[19], (DEPTH, PEER_EXPERTS, D_MODEL), D_MODEL ** -0.5),
        "peer_v": nrm(ks[20], (DEPTH, PEER_EXPERTS, D_MODEL), (PEER_HEADS * PEER_TOPK) ** -0.5),
        "final_norm_g": 1.0 + nrm(ks[21], (D_MODEL,), 0.02),
    }


def reference(x, mix_norm_g, ab_w_in, a_lb_param, b_a_re, b_a_im, b_log_dt, b_b_re, b_b_im,
              b_c_re, b_c_im, b_d, b_w_glu, ab_w_out, c_w_in, c_w_out, ffn_norm_g,
              peer_w_q, peer_sub_keys, peer_u, peer_v, final_norm_g):
    lbs = jnp.cumsum(jax.nn.softmax(a_lb_param.astype(jnp.float32), axis=0), axis=0)
    for layer in range(DEPTH):
        j = layer // 2
        h = rmsnorm(x, mix_norm_g[layer])
        if layer % 2 == 0:
            mix = ab_layer(h, ab_w_in[j], lbs[j], b_a_re[j], b_a_im[j], b_log_dt[j], b_b_re[j],
                           b_b_im[j], b_c_re[j], b_c_im[j], b_d[j], b_w_glu[j], ab_w_out[j])
        else:
            mix = retention_layer(h, c_w_in[j], c_w_out[j])
        x = x + mix.astype(x.dtype)
        hn = rmsnorm(x, ffn_norm_g[layer])
        x = x + peer_ffn(hn, peer_w_q[layer], peer_sub_keys[layer], peer_u[layer], peer_v[layer]).astype(x.dtype)
    return rmsnorm(x, final_norm_g)
```

```python
from contextlib import ExitStack
import math
import numpy as np
import concourse.bass as bass
import concourse.mybir as mybir
from concourse.bass_utils import run_bass_kernel_spmd

F32 = mybir.dt.float32
BF16 = mybir.dt.bfloat16
ALU = mybir.AluOpType
AF = mybir.ActivationFunctionType
AX = mybir.AxisListType

D = 4096
EPS = 1e-6
N_DMA_SEMS = 20
SB_WORDS = 53000
NEG = -1.0e30


class Buf:
    def __init__(self, ap, name):
        self.t = ap
        self.name = name
        self.w = None
        self.r = []

    def __getitem__(self, idx):
        return self.t[idx]


class Op:
    __slots__ = ("eng", "fn", "deps", "kind", "sem", "val", "need_inc", "prev_same_sem")

    def __init__(self, eng, fn, kind):
        self.eng = eng
        self.fn = fn
        self.kind = kind
        self.deps = []
        self.sem = None
        self.val = None
        self.need_inc = False
        self.prev_same_sem = None


class Prog:
    ENGS = ("pe", "act", "dve", "pool", "sp")

    def __init__(self, nc, stack):
        self.nc = nc
        self.stack = stack
        self.ops = []
        self.eobj = {"pe": nc.tensor, "act": nc.scalar, "dve": nc.vector,
                     "pool": nc.gpsimd, "sp": nc.sync}
        self.esets = [{e: stack.enter_context(nc.semaphore("s%d_%s" % (i, e))) for e in self.ENGS[:4]}
                      for i in range(4)]
        self.nds = {"sp": 76, "pool": 8}
        self.dsem = {q: [stack.enter_context(nc.semaphore("d_%s%d" % (q, i)))
                         for i in range(n)] for q, n in self.nds.items()}
        self.ccsem = stack.enter_context(nc.semaphore("s_cc"))
        self.cccount = 0
        self.dcount = {q: 0 for q in self.dsem}
        self.dlast = {}
        self.last = {}
        self.pending_dma = []
        self.big = stack.enter_context(nc.sbuf_tensor("arena", [128, SB_WORDS], F32))
        self.off = 0
        self.nb = 0
        self.ps = [Buf(stack.enter_context(nc.psum_tensor("psb%d" % i, [128, 512], F32)), "ps%d" % i)
                   for i in range(8)]

    def sb(self, free, dtype, name=None):
        n = int(np.prod(free))
        words = (n * (2 if dtype == BF16 else 4) + 3) // 4
        words = (words + 7) // 8 * 8
        assert self.off + words <= SB_WORDS, ("SBUF arena overflow", self.off, words)
        ap = self.big[:, self.off:self.off + words]
        self.off += words
        if dtype != F32:
            ap = ap.bitcast(dtype)
        ap = ap[:, 0:n]
        if len(free) == 2:
            ap = ap.rearrange("p (a b) -> p a b", a=free[0])
        elif len(free) == 3:
            ap = ap.rearrange("p (a b c) -> p a b c", a=free[0], b=free[1])
        self.nb += 1
        return Buf(ap, name or "b%d" % self.nb)

    def mark(self):
        return self.off

    def release(self, m):
        self.barrier()
        self.off = m

    def dram(self, name, shape, dtype):
        t = self.nc.dram_tensor(name, list(shape), dtype)
        b = Buf(t.ap(), name)
        b.handle = t
        return b

    def _track(self, op, r, w):
        deps = []
        for b in r:
            if b.w is not None:
                deps.append(b.w)
        for b in w:
            if b.w is not None:
                deps.append(b.w)
            for x in b.r:
                if x.eng == op.eng and x.kind == "c" and op.kind == "c":
                    continue
                deps.append(x)
        op.deps = [d for d in deps if not (d.eng == "pe" and op.eng == "pe"
                                           and d.kind == "c" and op.kind == "c")]
        for d in op.deps:
            d.need_inc = True
        for b in r:
            if op.kind == "c":
                b.r = [x for x in b.r if not (x.kind == "c" and x.eng == op.eng)]
            b.r.append(op)
        for b in w:
            b.w = op
            b.r = []

    def op(self, eng, fn, r=(), w=()):
        o = Op(eng, fn, "c")
        self._track(o, r, w)
        self.ops.append(o)
        self.last[eng] = o
        return o

    def dma(self, q, out, in_, r=(), w=()):
        o = Op(q, (out, in_), "d")
        k = self.dcount[q]
        self.dcount[q] += 1
        slot = k % self.nds[q]
        o.sem = self.dsem[q][slot]
        o.val = 16 * (k // self.nds[q] + 1)
        o.prev_same_sem = self.dlast.get((q, slot))
        self.dlast[(q, slot)] = o
        o.need_inc = True
        self._track(o, r, w)
        self.ops.append(o)
        self.pending_dma.append(o)
        return o

    def collective(self, kind, in_ap, out_ap, groups, r=(), w=()):
        o = Op("pool", (kind, in_ap, out_ap, groups), "cc")
        self.cccount += 1
        o.sem = self.ccsem
        o.val = self.cccount
        o.need_inc = True
        self._track(o, r, w)
        self.ops.append(o)
        self.pending_dma.append(o)
        return o

    def barrier(self):
        srcs = list(self.last.values()) + self.pending_dma
        for s in srcs:
            s.need_inc = True
        for e in self.ENGS:
            o = Op(e, None, "b")
            o.deps = list(srcs)
            self.ops.append(o)
        self.pending_dma = []

    def emit(self, final_wait_ops=()):
        cnt = {e: 0 for e in self.ENGS}
        epoch = 0
        tot = {e: 0 for e in self.ENGS}
        prev_b = False
        for o in self.ops:
            if o.kind == "b":
                prev_b = True
                continue
            if prev_b:
                prev_b = False
                if max(cnt.values()) > 14000 and epoch + 1 < len(self.esets):
                    epoch += 1
                    cnt = {e: 0 for e in self.ENGS}
            if o.kind == "c" and o.need_inc:
                cnt[o.eng] += 1
                tot[o.eng] += 1
                o.sem = self.esets[epoch][o.eng]
                o.val = cnt[o.eng]
        cnt = tot
        seen = {e: {} for e in self.ENGS}
        nwait = 0
        for o in self.ops:
            e = self.eobj[o.eng]
            sn = seen[o.eng]
            deps = list(o.deps)
            if o.kind == "d" and o.prev_same_sem is not None:
                deps.append(o.prev_same_sem)
            need = {}
            for d in deps:
                key = id(d.sem)
                if sn.get(key, 0) >= d.val:
                    continue
                if key not in need or need[key][1] < d.val:
                    need[key] = (d.sem, d.val)
            for key, (sem, val) in need.items():
                e.wait_ge(sem, val)
                sn[key] = val
                nwait += 1
            if o.kind == "d":
                out, in_ = o.fn
                e.dma_start(out=out, in_=in_).then_inc(o.sem, 16)
            elif o.kind == "cc":
                kind, in_ap, out_ap, groups = o.fn
                e.collective_compute(kind, ALU.bypass, replica_groups=groups,
                                     ins=[in_ap], outs=[out_ap]).then_inc(o.sem)
            elif o.kind == "c":
                ins = o.fn(e)
                if o.need_inc:
                    ins.then_inc(o.sem, 1)
        for o in final_wait_ops:
            e = self.eobj[o.eng]
            if seen[o.eng].get(id(o.sem), 0) < o.val:
                e.wait_ge(o.sem, o.val)
                seen[o.eng][id(o.sem)] = o.val
        return dict(n_ops=len(self.ops), n_wait=nwait, cnt=cnt, dma=dict(self.dcount), cc=self.cccount)


def piece_rows(K, N, ncores):
    kp = K
    while kp * N * 2 > (16 << 20) and kp % 2 == 0 and (kp // 2) % ncores == 0 and (kp // 2) >= ncores:
        kp //= 2
    return kp


def shard_rows(W, ncores):
    K, N = W.shape
    if ncores == 1:
        return [np.ascontiguousarray(W)]
    kp = piece_rows(K, N, ncores)
    npc = K // kp
    sub = kp // ncores
    Wr = W.reshape(npc, ncores, sub, N)
    return [np.ascontiguousarray(Wr[:, r].reshape(npc * sub, N)) for r in range(ncores)]


class Weight:
    def __init__(self, P, name, K, N, ncores):
        self.K, self.N, self.ncores, self.name = K, N, ncores, name
        nc = P.nc
        self.src = nc.dram_tensor(name, [K // ncores, N], F32, kind="ExternalInput").ap()
        self.full = P.dram(name + "_bf", [K, N], BF16)
        if ncores > 1:
            self.shard = P.dram(name + "_sh", [K // ncores, N], BF16)
        else:
            self.shard = self.full

    def distribute(self, P):
        K, N, nco = self.K, self.N, self.ncores
        rows = K // nco
        step = max(1, min(rows, (4 << 20) // (4 * N)))
        for r0 in range(0, rows, step):
            r1 = min(rows, r0 + step)
            P.dma("pool", self.shard[r0:r1, :], self.src[r0:r1, :], w=[self.shard])
        if nco > 1:
            kp = piece_rows(K, N, nco)
            sub = kp // nco
            for j in range(K // kp):
                P.collective("AllGather",
                             self.shard.handle.ap()[j * sub:(j + 1) * sub, :].opt(),
                             self.full.handle.ap()[j * kp:(j + 1) * kp, :].opt(),
                             [list(range(nco))], r=[self.shard], w=[self.full])


def load_wslab(P, dst, W, n0, n1, KT, r_extra=()):
    Wv = W.t.rearrange("(kt p) n -> p kt n", p=128)
    g = 8
    for k0 in range(0, KT, g):
        k1 = min(KT, k0 + g)
        P.dma("sp", dst[:, k0:k1, 0:n1 - n0], Wv[:, k0:k1, n0:n1], r=[W], w=[dst])


def norm_to_T(P, C, Y, r0, gam_sb, xt, hn, hnT, col0, acc=None, stat=None):
    P.dma("sp", xt[:], Y[r0:r0 + 128, :], r=[Y], w=[xt])
    ss, rs = stat
    P.op("dve", lambda e: e.memset(ss[:], 0.0), w=[ss])
    P.op("act", lambda e: e.activation(out=hn[:], in_=xt[:], func=AF.Square, accum_out=ss[:]),
         r=[xt, ss], w=[hn, ss])
    P.op("act", lambda e: e.activation(out=rs[:], in_=ss[:], func=AF.Sqrt, scale=1.0 / D, bias=C["eps"][:]),
         r=[ss, C["eps"]], w=[rs])
    P.op("dve", lambda e: e.reciprocal(out=rs[:], in_=rs[:]), r=[rs], w=[rs])
    if acc is not None:
        P.op("pool", lambda e: e.tensor_copy(out=acc[:], in_=xt[:]), r=[xt], w=[acc])
    P.op("dve", lambda e: e.tensor_scalar(out=hn[:], in0=xt[:], scalar1=rs[:], scalar2=None, op0=ALU.mult),
         r=[xt, rs], w=[hn])
    ident = C["ident_bf"]
    for kb in range(4):
        pb = P.ps[kb % 2]
        pv = pb.t.bitcast(BF16).rearrange("p (a b) -> p a b", a=8)
        for k in range(8):
            kt = kb * 8 + k
            P.op("pe", lambda e, kt=kt, k=k, pv=pv: e.transpose(pv[:, k, :], hn[:, kt * 128:(kt + 1) * 128], ident[:]),
                 r=[hn, ident], w=[pb])
        for k in range(8):
            kt = kb * 8 + k
            P.op("act", lambda e, kt=kt, k=k, pv=pv: e.activation(
                out=hnT[:, kt, col0:col0 + 128], in_=pv[:, k, :], func=AF.Copy, scale=gam_sb[:, kt:kt + 1]),
                r=[pb, gam_sb], w=[hnT])


def setup_consts(P):
    C = {}
    idf = P.sb([128], F32, "idf")
    P.op("pool", lambda e: e.memset(idf[:], 0.0), w=[idf])
    P.op("pool", lambda e: e.affine_select(out=idf[:], in_=idf[:], pattern=[[-1, 128]],
                                           compare_op=ALU.not_equal, fill=1.0, base=0, channel_multiplier=1),
         r=[idf], w=[idf])
    ib = P.sb([128], BF16, "ident_bf")
    P.op("dve", lambda e: e.tensor_copy(out=ib[:], in_=idf[:]), r=[idf], w=[ib])
    eps = P.sb([1], F32, "eps")
    P.op("dve", lambda e: e.memset(eps[:], EPS), w=[eps])
    hp = P.sb([1], F32, "halfpi")
    P.op("dve", lambda e: e.memset(hp[:], float(np.pi / 2)), w=[hp])
    C["halfpi"] = hp
    C["ident_f"] = idf
    C["ident_bf"] = ib
    C["eps"] = eps
    return C


def peer_stage(P, C, Y, T, gam_d, Wq, keys_d, UT, V):
    TG = min(T, 256)
    NT = TG // 128
    m0 = P.mark()
    hnT = P.sb([32, TG], BF16, "hnT")
    acc = [P.sb([4096], F32, "acc%d" % i) for i in range(NT)]
    wA = P.sb([32, 512], BF16, "wA")
    wB = P.sb([4, 4096], BF16, "wB")
    qT = P.sb([16, TG], BF16, "qT")
    keysT = P.sb([16, 128], BF16, "keysT")
    gam = P.sb([32], F32, "gam")
    sc = [P.sb([16, 128], F32, "sc%d" % i) for i in range(NT)]
    tau = [P.sb([8], F32, "tau%d" % i) for i in range(NT)]
    nbias = [P.sb([8], F32, "nb%d" % i) for i in range(NT)]
    a16 = P.sb([16], F32, "a16")
    b16 = P.sb([16], F32, "b16")
    c16 = P.sb([16], F32, "c16")
    j16 = P.sb([16], F32, "j16")
    tmp128 = P.sb([128], F32, "tmp128")
    cand = P.sb([16, 16], F32, "cand")
    cand2 = P.sb([256], F32, "cand2")
    negm = P.sb([1], F32, "negm")
    zz = P.sb([1], F32, "zz")
    ss = P.sb([1], F32, "ss")
    rs = P.sb([1], F32, "rs")
    m1 = P.mark()
    xt = P.sb([4096], F32, "xt")
    hn = P.sb([4096], BF16, "hn")
    P.off = m1
    S = [P.sb([8, 2, 128], F32, "S%d" % i) for i in range(2)]
    E = [P.sb([8, 256], F32, "E%d" % i) for i in range(2)]
    hidg = [P.sb([512], F32, "hidg%d" % i) for i in range(2)]
    G = [P.sb([256], F32, "G%d" % i) for i in range(2)]
    Wt = [P.sb([256], BF16, "W%d" % i) for i in range(2)]
    WT = [P.sb([2, 128], BF16, "WT%d" % i) for i in range(2)]

    P.dma("pool", keysT[:], keys_d, w=[keysT])
    P.dma("sp", gam[:], gam_d, w=[gam])
    ident = C["ident_bf"]
    psH = [P.ps[0], P.ps[1]]
    psT = P.ps[2]
    psT_v = psT.t.bitcast(BF16)[:, 0:256].rearrange("p (a b) -> p a b", a=2)
    psO = [[P.ps[3], P.ps[4]], [P.ps[5], P.ps[6]]]

    for g0 in range(0, T, TG):
        for tt in range(NT):
            norm_to_T(P, C, Y, g0 + tt * 128, gam, xt, hn, hnT, tt * 128, acc=acc[tt], stat=(ss, rs))
        for s in range(4):
            load_wslab(P, wA, Wq, s * 512, (s + 1) * 512, 32)
            for ft in range(4):
                pb = P.ps[3 + (s * 4 + ft) % 2]
                for kt in range(32):
                    P.op("pe", lambda e, kt=kt, ft=ft, pb=pb: e.matmul(
                        pb[:, 0:TG], wA[:, kt, ft * 128:(ft + 1) * 128], hnT[:, kt, :],
                        start=(kt == 0), stop=(kt == 31)), r=[wA, hnT], w=[pb])
                P.op("act", lambda e, s=s, ft=ft, pb=pb: e.activation(
                    out=qT[:, s * 4 + ft, :], in_=pb[:, 0:TG], func=AF.Copy), r=[pb], w=[qT])
        for tt in range(NT):
            for q4 in range(4):
                pb = P.ps[5 + q4 % 2]
                for k in range(4):
                    hp = q4 * 4 + k
                    P.op("pe", lambda e, hp=hp, k=k, pb=pb, tt=tt: e.matmul(
                        pb[:, k * 128:(k + 1) * 128], qT[:, hp, tt * 128:(tt + 1) * 128], keysT[:, hp, :],
                        start=True, stop=True), r=[qT, keysT], w=[pb])
                P.op("dve", lambda e, q4=q4, pb=pb, tt=tt: e.tensor_copy(
                    out=sc[tt][:, q4 * 4:(q4 + 1) * 4, :], in_=pb[:].rearrange("p (a b) -> p a b", a=4)),
                    r=[pb], w=[sc[tt]])
            for h in range(8):
                s1 = sc[tt][:, 2 * h, :]
                s2 = sc[tt][:, 2 * h + 1, :]
                for src, dst in ((s1, a16), (s2, b16)):
                    P.op("dve", lambda e, src=src, dst=dst: e.max(out=dst[:, 0:8], in_=src), r=[sc[tt]], w=[dst])
                    P.op("dve", lambda e, src=src, dst=dst: e.match_replace(
                        out=tmp128[:], in_to_replace=dst[:, 0:8], in_values=src, imm_value=NEG),
                        r=[sc[tt], dst], w=[tmp128])
                    P.op("dve", lambda e, dst=dst: e.max(out=dst[:, 8:16], in_=tmp128[:]), r=[tmp128], w=[dst])
                P.op("dve", lambda e: e.tensor_tensor(
                    out=cand[:], in0=a16[:].unsqueeze(2).to_broadcast([128, 16, 16]),
                    in1=b16[:].unsqueeze(1).to_broadcast([128, 16, 16]), op=ALU.add), r=[a16, b16], w=[cand])
                cf = cand[:].rearrange("p a b -> p (a b)")
                P.op("dve", lambda e, cf=cf: e.max(out=c16[:, 0:8], in_=cf), r=[cand], w=[c16])
                P.op("dve", lambda e, cf=cf: e.match_replace(out=cand2[:], in_to_replace=c16[:, 0:8],
                                                             in_values=cf, imm_value=NEG), r=[cand, c16], w=[cand2])
                P.op("dve", lambda e: e.max(out=c16[:, 8:16], in_=cand2[:]), r=[cand2], w=[c16])
                P.op("dve", lambda e, h=h, tt=tt: e.tensor_copy(out=tau[tt][:, h:h + 1], in_=c16[:, 15:16]),
                     r=[c16], w=[tau[tt]])
                P.op("dve", lambda e: e.tensor_scalar(out=negm[:], in0=c16[:, 0:1], scalar1=-1.0, scalar2=None,
                                                      op0=ALU.mult), r=[c16], w=[negm])
                P.op("dve", lambda e: e.memset(zz[:], 0.0), w=[zz])
                P.op("act", lambda e: e.activation(out=j16[:], in_=c16[:], func=AF.Exp, bias=negm[:], scale=1.0,
                                                   accum_out=zz[:]), r=[c16, negm, zz], w=[j16, zz])
                P.op("act", lambda e: e.activation(out=zz[:], in_=zz[:], func=AF.Ln), r=[zz], w=[zz])
                P.op("dve", lambda e, h=h, tt=tt: e.tensor_tensor(out=nbias[tt][:, h:h + 1], in0=negm[:], in1=zz[:],
                                                                  op=ALU.subtract), r=[negm, zz], w=[nbias[tt]])
        P.barrier()
        for c in range(32):
            e0 = c * 512
            load_wslab(P, wA, UT, e0, e0 + 512, 32)
            Vv = V.t.rearrange("(et p) d -> p et d", p=128)
            for et in range(4):
                P.dma("sp", wB[:, et, :], Vv[:, c * 4 + et, :], r=[V], w=[wB])
            for tt in range(NT):
                ph = psH[(c * NT + tt) % 2]
                hg = hidg[(c * NT + tt) % 2]
                for kt in range(32):
                    P.op("pe", lambda e, kt=kt, tt=tt, ph=ph: e.matmul(
                        ph[:], hnT[:, kt, tt * 128:(tt + 1) * 128], wA[:, kt, :],
                        start=(kt == 0), stop=(kt == 31)), r=[hnT, wA], w=[ph])
                P.op("act", lambda e, ph=ph, hg=hg: e.activation(out=hg[:], in_=ph[:], func=AF.Gelu), r=[ph], w=[hg])
                scv = sc[tt][:].rearrange("p (h two) n -> p h two n", two=2)
                for hf in range(2):
                    i0 = c * 4 + hf * 2
                    Sb, Eb, Gb, Wb, WTb = S[hf], E[hf], G[hf], Wt[hf], WT[hf]
                    P.op("dve", lambda e, Sb=Sb, i0=i0, scv=scv: e.tensor_tensor(
                        out=Sb[:],
                        in0=scv[:, :, 0, i0:i0 + 2].unsqueeze(3).to_broadcast([128, 8, 2, 128]),
                        in1=scv[:, :, 1, :].unsqueeze(2).to_broadcast([128, 8, 2, 128]),
                        op=ALU.add), r=[sc[tt]], w=[Sb])
                    for h in range(8):
                        sv = Sb[:, h, :, :].rearrange("p a b -> p (a b)")
                        P.op("act", lambda e, sv=sv, Eb=Eb, h=h, tt=tt: e.activation(
                            out=Eb[:, h, :], in_=sv, func=AF.Exp, bias=nbias[tt][:, h:h + 1], scale=1.0),
                            r=[Sb, nbias[tt]], w=[Eb])
                        P.op("dve", lambda e, sv=sv, Eb=Eb, h=h, tt=tt: e.scalar_tensor_tensor(
                            out=Eb[:, h, :], in0=sv, scalar=tau[tt][:, h:h + 1], in1=Eb[:, h, :],
                            op0=ALU.is_ge, op1=ALU.mult), r=[Sb, Eb, tau[tt]], w=[Eb])
                    P.op("pool", lambda e, Eb=Eb: e.tensor_tensor(
                        out=Eb[:, 0:4, :], in0=Eb[:, 0:4, :], in1=Eb[:, 4:8, :], op=ALU.add), r=[Eb], w=[Eb])
                    P.op("pool", lambda e, Eb=Eb: e.tensor_tensor(
                        out=Eb[:, 0:2, :], in0=Eb[:, 0:2, :], in1=Eb[:, 2:4, :], op=ALU.add), r=[Eb], w=[Eb])
                    P.op("pool", lambda e, Eb=Eb, Gb=Gb: e.tensor_tensor(
                        out=Gb[:], in0=Eb[:, 0, :], in1=Eb[:, 1, :], op=ALU.add), r=[Eb], w=[Gb])
                    P.op("dve", lambda e, Gb=Gb, Wb=Wb, hg=hg, hf=hf: e.tensor_tensor(
                        out=Wb[:], in0=Gb[:], in1=hg[:, hf * 256:(hf + 1) * 256], op=ALU.mult), r=[Gb, hg], w=[Wb])
                    for k in range(2):
                        P.op("pe", lambda e, k=k, Wb=Wb: e.transpose(psT_v[:, k, :], Wb[:, k * 128:(k + 1) * 128],
                                                                    ident[:]), r=[Wb, ident], w=[psT])
                    P.op("act", lambda e, WTb=WTb: e.activation(out=WTb[:], in_=psT_v, func=AF.Copy), r=[psT], w=[WTb])
                for dq in range(4):
                    pair = psO[dq % 2]
                    for dc in range(2):
                        d0 = (dq * 2 + dc) * 512
                        for et in range(4):
                            P.op("pe", lambda e, et=et, d0=d0, pb=pair[dc]: e.matmul(
                                pb[:], WT[et // 2][:, et % 2, :], wB[:, et, d0:d0 + 512],
                                start=(et == 0), stop=(et == 3)), r=[WT[et // 2], wB], w=[pair[dc]])
                    for dc in range(2):
                        d0 = (dq * 2 + dc) * 512
                        P.op("dve", lambda e, d0=d0, pb=pair[dc], tt=tt: e.tensor_tensor(
                            out=acc[tt][:, d0:d0 + 512], in0=acc[tt][:, d0:d0 + 512], in1=pb[:], op=ALU.add),
                            r=[acc[tt], pair[dc]], w=[acc[tt]])
        for tt in range(NT):
            P.dma("sp", Y[g0 + tt * 128:g0 + (tt + 1) * 128, :], acc[tt][:], r=[acc[tt]], w=[Y])
        P.barrier()
    P.release(m0)


def inproj_stage(P, C, Y, T, gam_d, W, fm_specs, tm_specs):
    TB = min(T, 1024)
    m0 = P.mark()
    hnT = P.sb([32, TB], BF16, "ip_hnT")
    gam = P.sb([32], F32, "ip_gam")
    xt = P.sb([4096], F32, "ip_xt")
    hn = P.sb([4096], BF16, "ip_hn")
    ss = P.sb([1], F32, "ip_ss")
    rs = P.sb([1], F32, "ip_rs")
    slab = [P.sb([32, 512], BF16, "ip_slab%d" % i) for i in range(2)]
    stg = [P.sb([512], F32, "ip_stg%d" % i) for i in range(3)]
    P.dma("sp", gam[:], gam_d, w=[gam])
    banks = [P.ps[3], P.ps[4], P.ps[5], P.ps[6]]
    cnt = [0, 0, 0]
    for t0 in range(0, T, TB):
        for tt in range(TB // 128):
            norm_to_T(P, C, Y, t0 + tt * 128, gam, xt, hn, hnT, tt * 128, stat=(ss, rs))
        for (col0, ncols, dst, row0, func) in fm_specs:
            for s0 in range(0, ncols, 512):
                sl = slab[cnt[0] % 2]
                cnt[0] += 1
                load_wslab(P, sl, W, col0 + s0, col0 + s0 + 512, 32)
                for nt in range(4):
                    for tc in range(0, TB, 512):
                        tw = min(512, TB - tc)
                        pb = banks[cnt[1] % 4]
                        cnt[1] += 1
                        for kt in range(32):
                            P.op("pe", lambda e, kt=kt, nt=nt, pb=pb, sl=sl, tc=tc, tw=tw: e.matmul(
                                pb[:, 0:tw], sl[:, kt, nt * 128:(nt + 1) * 128], hnT[:, kt, tc:tc + tw],
                                start=(kt == 0), stop=(kt == 31)), r=[sl, hnT], w=[pb])
                        sg = stg[cnt[2] % 3]
                        cnt[2] += 1
                        P.op("act", lambda e, pb=pb, sg=sg, tw=tw, func=func: e.activation(
                            out=sg[:, 0:tw], in_=pb[:, 0:tw], func=func), r=[pb], w=[sg])
                        rr = row0 + s0 + nt * 128
                        P.dma("sp", dst[rr:rr + 128, t0 + tc:t0 + tc + tw], sg[:, 0:tw], r=[sg], w=[dst])
        for (col0, ncols, dst, dcol0, func) in tm_specs:
            for s0 in range(0, ncols, 512):
                sl = slab[cnt[0] % 2]
                cnt[0] += 1
                load_wslab(P, sl, W, col0 + s0, col0 + s0 + 512, 32)
                for tt in range(TB // 128):
                    pb = banks[cnt[1] % 4]
                    cnt[1] += 1
                    for kt in range(32):
                        P.op("pe", lambda e, kt=kt, tt=tt, pb=pb, sl=sl: e.matmul(
                            pb[:], hnT[:, kt, tt * 128:(tt + 1) * 128], sl[:, kt, :],
                            start=(kt == 0), stop=(kt == 31)), r=[sl, hnT], w=[pb])
                    sg = stg[cnt[2] % 3]
                    cnt[2] += 1
                    P.op("act", lambda e, pb=pb, sg=sg, func=func: e.activation(
                        out=sg[:], in_=pb[:], func=func), r=[pb], w=[sg])
                    r0 = t0 + tt * 128
                    P.dma("sp", dst[r0:r0 + 128, dcol0 + s0:dcol0 + s0 + 512], sg[:], r=[sg], w=[dst])
    P.release(m0)


def outproj_stage(P, C, Y, T, OT, W, K):
    KT = K // 128
    TB = min(T, 512 if KT <= 32 else 256)
    NS = 512 if KT <= 32 else 256
    m0 = P.mark()
    oT = P.sb([KT, TB], BF16, "op_oT")
    yt = [P.sb([4096], F32, "op_y%d" % i) for i in range(TB // 128)]
    slab = [P.sb([KT, NS], BF16, "op_slab%d" % i) for i in range(2)]
    banks = [P.ps[3], P.ps[4], P.ps[5], P.ps[6]]
    OTv = OT.t.rearrange("(kt p) t -> p kt t", p=128)
    cnt = [0, 0]
    for t0 in range(0, T, TB):
        for k0 in range(0, KT, 8):
            P.dma("sp", oT[:, k0:k0 + 8, :], OTv[:, k0:k0 + 8, t0:t0 + TB], r=[OT], w=[oT])
        for tt in range(TB // 128):
            P.dma("sp", yt[tt][:], Y[t0 + tt * 128:t0 + (tt + 1) * 128, :], r=[Y], w=[yt[tt]])
        for s0 in range(0, 4096, NS):
            sl = slab[cnt[0] % 2]
            cnt[0] += 1
            load_wslab(P, sl, W, s0, s0 + NS, KT)
            for tt in range(TB // 128):
                pb = banks[cnt[1] % 4]
                cnt[1] += 1
                for kt in range(KT):
                    P.op("pe", lambda e, kt=kt, tt=tt, pb=pb, sl=sl: e.matmul(
                        pb[:, 0:NS], oT[:, kt, tt * 128:(tt + 1) * 128], sl[:, kt, :],
                        start=(kt == 0), stop=(kt == KT - 1)), r=[sl, oT], w=[pb])
                P.op("dve", lambda e, pb=pb, tt=tt, s0=s0: e.tensor_tensor(
                    out=yt[tt][:, s0:s0 + NS], in0=yt[tt][:, s0:s0 + NS], in1=pb[:, 0:NS], op=ALU.add),
                    r=[yt[tt], pb], w=[yt[tt]])
        for tt in range(TB // 128):
            P.dma("sp", Y[t0 + tt * 128:t0 + (tt + 1) * 128, :], yt[tt][:], r=[yt[tt]], w=[Y])
    P.release(m0)


def make_consts(T):
    cols = {}
    parts = []
    off = 0

    def add(name, arr):
        nonlocal off
        arr = np.asarray(arr, np.float32)
        cols[name] = (off, arr.shape[1])
        parts.append(arr)
        off += arr.shape[1]

    t = np.arange(T)
    add("reset", np.broadcast_to((t % 64 != 0).astype(np.float32)[None, :], (128, T)))
    s = np.arange(128)
    same = (s[:, None] // 64) == (s[None, :] // 64)
    add("mc2", (same & (s[:, None] <= s[None, :])).astype(np.float32))
    add("usuf", (same & (s[:, None] > s[None, :])).astype(np.float32))
    add("onesdiv", np.full((128, 128), 1.0 / 128, np.float32))
    add("same", same.astype(np.float32))
    add("iota", np.broadcast_to(t.astype(np.float32)[None, :], (128, T)))
    add("mask8", ((s[:, None] // 16) == np.arange(8)[None, :]).astype(np.float32))
    return np.ascontiguousarray(np.concatenate(parts, axis=1)), cols


def load_consts(P, C, consts_d, cols):
    n = sum(v[1] for v in cols.values())
    cb = P.sb([n], F32, "consts")
    P.dma("sp", cb[:], consts_d, w=[cb])
    C["cb"] = cb
    C["cols"] = cols


def cview(C, name):
    o, n = C["cols"][name]
    return C["cb"][:, o:o + n]


def hgrn2_pass1(P, C, T, ZF, ZT, lbp_d, lbrow_d, KH, VB, QT, KT, DECd, SLd):
    NT = T // 128
    NCH = T // 64
    cb = C["cb"]
    m0 = P.mark()
    lbrow = P.sb([2048], F32, "lbrow")
    omlrow = P.sb([2048], F32, "omlrow")
    lbp = P.sb([16], F32, "lbp")
    omlp = P.sb([16], F32, "omlp")
    tmp2 = P.sb([2, 2048], F32, "lbtmp")
    tmpp = P.sb([2, 16], F32, "lbtmpp")
    P.dma("sp", tmp2[:], lbrow_d, w=[tmp2])
    P.dma("sp", tmpp[:], lbp_d, w=[tmpp])
    P.op("dve", lambda e: e.tensor_tensor(out=lbrow[:], in0=tmp2[:, 0, :], in1=tmp2[:, 1, :], op=ALU.subtract),
         r=[tmp2], w=[lbrow])
    P.op("act", lambda e: e.activation(out=lbrow[:], in_=lbrow[:], func=AF.Sigmoid), r=[lbrow], w=[lbrow])
    P.op("dve", lambda e: e.tensor_scalar(out=omlrow[:], in0=lbrow[:], scalar1=-1.0, scalar2=1.0,
                                          op0=ALU.mult, op1=ALU.add), r=[lbrow], w=[omlrow])
    P.op("dve", lambda e: e.tensor_tensor(out=lbp[:], in0=tmpp[:, 0, :], in1=tmpp[:, 1, :], op=ALU.subtract),
         r=[tmpp], w=[lbp])
    P.op("act", lambda e: e.activation(out=lbp[:], in_=lbp[:], func=AF.Sigmoid), r=[lbp], w=[lbp])
    P.op("dve", lambda e: e.tensor_scalar(out=omlp[:], in0=lbp[:], scalar1=-1.0, scalar2=1.0,
                                          op0=ALU.mult, op1=ALU.add), r=[lbp], w=[omlp])
    ft = P.sb([512], F32, "h_ft")
    vt = P.sb([512], F32, "h_vt")
    lg = P.sb([512], F32, "h_lg")
    eR = P.sb([512], F32, "h_eR")
    k1 = P.sb([512], F32, "h_k1")
    khat = P.sb([NT, 512], BF16, "h_khat")
    vb = P.sb([NT, 512], BF16, "h_vb")
    A = P.sb([T], F32, "h_A")
    B = P.sb([T], F32, "h_B")
    Cb = P.sb([T], F32, "h_C")
    Dd = P.sb([T], F32, "h_D")
    E = P.sb([T], F32, "h_E")
    qt = P.sb([T], BF16, "h_qt")
    kt = P.sb([T], BF16, "h_kt")
    dec = P.sb([16, NCH], F32, "h_dec")
    dsum = P.sb([1], F32, "h_dsum")
    SL = P.sb([16 * 128 + 16], F32, "h_SL")
    S = P.sb([128], F32, "h_S")
    usuf = cview(C, "usuf")
    reset = cview(C, "reset")
    psR = [P.ps[0], P.ps[1]]
    psU = [P.ps[2], P.ps[3]]
    KHv = KH.t.rearrange("(nt p) c -> p nt c", p=128)
    VBv = VB.t.rearrange("(nt p) c -> p nt c", p=128)
    for hg in range(4):
        h0 = hg * 4
        for tt in range(NT):
            pr = psR[tt % 2]
            P.dma("sp", ft[:], ZT[tt * 128:(tt + 1) * 128, h0 * 128:h0 * 128 + 512], r=[ZT], w=[ft])
            P.dma("sp", vt[:], ZT[tt * 128:(tt + 1) * 128, 2048 + h0 * 128:2048 + h0 * 128 + 512], r=[ZT], w=[vt])
            P.op("dve", lambda e, h0=h0: e.tensor_tensor(out=ft[:], in0=ft[:], in1=omlrow[:, h0 * 128:h0 * 128 + 512],
                                                         op=ALU.mult), r=[ft, omlrow], w=[ft])
            P.op("dve", lambda e, h0=h0: e.tensor_tensor(out=ft[:], in0=ft[:], in1=lbrow[:, h0 * 128:h0 * 128 + 512],
                                                         op=ALU.add), r=[ft, lbrow], w=[ft])
            P.op("act", lambda e: e.activation(out=lg[:], in_=ft[:], func=AF.Ln), r=[ft], w=[lg])
            P.op("pe", lambda e, pr=pr: e.matmul(pr[:], usuf, lg[:], start=True, stop=True), r=[cb, lg], w=[pr])
            P.op("act", lambda e, pr=pr: e.activation(out=eR[:], in_=pr[:], func=AF.Exp), r=[pr], w=[eR])
            P.op("dve", lambda e: e.tensor_scalar(out=k1[:], in0=ft[:], scalar1=-1.0, scalar2=1.0,
                                                  op0=ALU.mult, op1=ALU.add), r=[ft], w=[k1])
            P.op("dve", lambda e, tt=tt: e.tensor_tensor(out=khat[:, tt, :], in0=k1[:], in1=eR[:], op=ALU.mult),
                 r=[k1, eR], w=[khat])
            P.op("pool", lambda e, tt=tt: e.tensor_copy(out=vb[:, tt, :], in_=vt[:]), r=[vt], w=[vb])
        P.dma("sp", KHv[:, :, h0 * 128:h0 * 128 + 512], khat[:], r=[khat], w=[KH])
        P.dma("sp", VBv[:, :, h0 * 128:h0 * 128 + 512], vb[:], r=[vb], w=[VB])
        for hl in range(4):
            h = h0 + hl
            P.dma("sp", A[:], ZF[h * 128:(h + 1) * 128, :], r=[ZF], w=[A])
            P.dma("sp", B[:], ZF[2048 + h * 128:2048 + (h + 1) * 128, :], r=[ZF], w=[B])
            P.op("dve", lambda e, h=h: e.tensor_scalar(out=B[:], in0=B[:], scalar1=omlp[:, h:h + 1],
                                                       scalar2=lbp[:, h:h + 1], op0=ALU.mult, op1=ALU.add),
                 r=[B, omlp, lbp], w=[B])
            P.op("act", lambda e: e.activation(out=Cb[:], in_=B[:], func=AF.Ln), r=[B], w=[Cb])
            P.op("dve", lambda e: e.tensor_tensor_scan(out=Dd[:], data0=reset[:, 0:T], data1=Cb[:], initial=0.0,
                                                       op0=ALU.mult, op1=ALU.add), r=[cb, Cb], w=[Dd])
            P.op("act", lambda e: e.activation(out=E[:], in_=Dd[:], func=AF.Exp), r=[Dd], w=[E])
            P.op("dve", lambda e: e.tensor_tensor(out=qt[:], in0=A[:], in1=E[:], op=ALU.mult), r=[A, E], w=[qt])
            P.op("act", lambda e: e.activation(out=A[:], in_=Dd[:], func=AF.Exp, scale=-1.0), r=[Dd, qt], w=[A])
            P.op("dve", lambda e: e.tensor_scalar(out=B[:], in0=B[:], scalar1=-1.0, scalar2=1.0,
                                                  op0=ALU.mult, op1=ALU.add), r=[B, Cb], w=[B])
            P.op("dve", lambda e: e.tensor_tensor(out=kt[:], in0=B[:], in1=A[:], op=ALU.mult), r=[A, B], w=[kt])
            Ev = E[:].rearrange("p (c s) -> p c s", s=64)
            Dv = Dd[:].rearrange("p (c s) -> p c s", s=64)
            P.op("dve", lambda e, h=h, Ev=Ev: e.tensor_copy(out=dec[:, h, :], in_=Ev[:, :, 63]), r=[E], w=[dec])
            P.op("dve", lambda e, Dv=Dv: e.tensor_reduce(out=dsum[:], in_=Dv[:, :, 63], axis=AX.X, op=ALU.add),
                 r=[Dd], w=[dsum])
            P.op("act", lambda e, h=h: e.activation(out=SL[:, 2048 + h:2048 + h + 1], in_=dsum[:], func=AF.Exp),
                 r=[dsum], w=[SL])
            P.dma("sp", QT[h * 128:(h + 1) * 128, :], qt[:], r=[qt], w=[QT])
            P.dma("sp", KT[h * 128:(h + 1) * 128, :], kt[:], r=[kt], w=[KT])
            P.op("dve", lambda e: e.memset(S[:], 0.0), w=[S])
            for c in range(NCH):
                tt, b0 = c // 2, (c % 2) * 64
                pu = psU[c % 2]
                P.op("pe", lambda e, tt=tt, b0=b0, hl=hl, pu=pu: e.matmul(
                    pu[:, 0:128], khat[b0:b0 + 64, tt, hl * 128:(hl + 1) * 128],
                    vb[b0:b0 + 64, tt, hl * 128:(hl + 1) * 128], start=True, stop=True), r=[khat, vb], w=[pu])
                P.op("dve", lambda e, h=h, c=c, pu=pu: e.scalar_tensor_tensor(
                    out=S[:], in0=S[:], scalar=dec[:, h, c:c + 1], in1=pu[:, 0:128], op0=ALU.mult, op1=ALU.add),
                    r=[S, dec, pu], w=[S])
            P.op("dve", lambda e, h=h: e.tensor_copy(out=SL[:, h * 128:(h + 1) * 128], in_=S[:]), r=[S], w=[SL])
    P.dma("sp", DECd.t, dec[:].rearrange("p a b -> p (a b)"), r=[dec], w=[DECd])
    P.dma("sp", SLd.t, SL[:], r=[SL], w=[SLd])
    P.release(m0)


def hgrn2_pass2(P, C, T, ZF, KH, VB, QT, KT, DECd, SId, OT):
    NT = T // 128
    NCH = T // 64
    cb = C["cb"]
    m0 = P.mark()
    khat = P.sb([NT, 512], BF16, "g_khat")
    vb = P.sb([NT, 512], BF16, "g_vb")
    qt = P.sb([T], BF16, "g_qt")
    kt = P.sb([T], BF16, "g_kt")
    sg = P.sb([T], F32, "g_sg")
    og = P.sb([T], BF16, "g_og")
    dec = P.sb([16, NCH], F32, "g_dec")
    S = P.sb([128], F32, "g_S")
    Sb = P.sb([128], BF16, "g_Sb")
    am = P.sb([128], BF16, "g_am")
    osb = P.sb([128], F32, "g_osb")
    sq = P.sb([128], F32, "g_sq")
    rsd = P.sb([128], F32, "g_rsd")
    mc2 = cview(C, "mc2")
    onesdiv = cview(C, "onesdiv")
    KHv = KH.t.rearrange("(nt p) c -> p nt c", p=128)
    VBv = VB.t.rearrange("(nt p) c -> p nt c", p=128)
    P.dma("sp", dec[:].rearrange("p a b -> p (a b)"), DECd.t, r=[DECd], w=[dec])
    psA, psO, psM = P.ps[0], P.ps[1], P.ps[4]
    psU = [P.ps[2], P.ps[3]]
    for hg in range(4):
        h0 = hg * 4
        P.dma("sp", khat[:], KHv[:, :, h0 * 128:h0 * 128 + 512], r=[KH], w=[khat])
        P.dma("sp", vb[:], VBv[:, :, h0 * 128:h0 * 128 + 512], r=[VB], w=[vb])
        for hl in range(4):
            h = h0 + hl
            P.dma("sp", qt[:], QT[h * 128:(h + 1) * 128, :], r=[QT], w=[qt])
            P.dma("sp", kt[:], KT[h * 128:(h + 1) * 128, :], r=[KT], w=[kt])
            P.dma("sp", sg[:], ZF[4096 + h * 128:4096 + (h + 1) * 128, :], r=[ZF], w=[sg])
            P.dma("sp", S[:], SId[:, h * 128:(h + 1) * 128], r=[SId], w=[S])
            P.op("act", lambda e: e.activation(out=Sb[:], in_=S[:], func=AF.Copy), r=[S], w=[Sb])
            for tt in range(NT):
                ts = slice(tt * 128, (tt + 1) * 128)
                P.op("pe", lambda e, ts=ts: e.matmul(psA[:, 0:128], kt[:, ts], qt[:, ts], start=True, stop=True),
                     r=[kt, qt], w=[psA])
                P.op("dve", lambda e: e.tensor_tensor(out=am[:], in0=psA[:, 0:128], in1=mc2, op=ALU.mult),
                     r=[psA, cb], w=[am])
                P.op("pe", lambda e, tt=tt, hl=hl: e.matmul(psO[:, 0:128], vb[:, tt, hl * 128:(hl + 1) * 128], am[:],
                                                            start=True, stop=False), r=[vb, am], w=[psO])
                for ci in range(2):
                    c = 2 * tt + ci
                    b0 = ci * 64
                    pu = psU[c % 2]
                    P.op("pe", lambda e, b0=b0, c=c: e.matmul(psO[:, b0:b0 + 64], Sb[:], qt[:, c * 64:(c + 1) * 64],
                                                              start=False, stop=True), r=[Sb, qt], w=[psO])
                    P.op("pe", lambda e, tt=tt, b0=b0, hl=hl, pu=pu: e.matmul(
                        pu[:, 0:128], khat[b0:b0 + 64, tt, hl * 128:(hl + 1) * 128],
                        vb[b0:b0 + 64, tt, hl * 128:(hl + 1) * 128], start=True, stop=True), r=[khat, vb], w=[pu])
                    P.op("dve", lambda e, h=h, c=c, pu=pu: e.scalar_tensor_tensor(
                        out=S[:], in0=S[:], scalar=dec[:, h, c:c + 1], in1=pu[:, 0:128], op0=ALU.mult, op1=ALU.add),
                        r=[S, dec, pu], w=[S])
                    P.op("act", lambda e: e.activation(out=Sb[:], in_=S[:], func=AF.Copy), r=[S], w=[Sb])
                P.op("act", lambda e: e.activation(out=osb[:], in_=psO[:, 0:128], func=AF.Copy), r=[psO], w=[osb])
                P.op("act", lambda e: e.activation(out=sq[:], in_=psO[:, 0:128], func=AF.Square), r=[psO], w=[sq])
                P.op("pe", lambda e: e.matmul(psM[:, 0:128], onesdiv, sq[:], start=True, stop=True), r=[cb, sq], w=[psM])
                P.op("act", lambda e: e.activation(out=rsd[:], in_=psM[:, 0:128], func=AF.Sqrt, bias=C["eps"][:], scale=1.0),
                     r=[psM, C["eps"]], w=[rsd])
                P.op("dve", lambda e: e.reciprocal(out=rsd[:], in_=rsd[:]), r=[rsd], w=[rsd])
                P.op("dve", lambda e: e.tensor_tensor(out=osb[:], in0=osb[:], in1=rsd[:], op=ALU.mult), r=[osb, rsd], w=[osb])
                P.op("dve", lambda e, ts=ts: e.tensor_tensor(out=og[:, ts], in0=osb[:], in1=sg[:, ts], op=ALU.mult),
                     r=[osb, sg], w=[og])
            P.dma("sp", OT[h * 128:(h + 1) * 128, :], og[:], r=[og], w=[OT])
    P.release(m0)


MAGIC = 12582912.0
TWO_PI = float(2.0 * np.pi)


def sincos(P, C, arg, s_out, c_out, t0, t1):
    P.op("dve", lambda e: e.tensor_scalar(out=t0[:], in0=arg[:], scalar1=1.0 / TWO_PI, scalar2=MAGIC,
                                          op0=ALU.mult, op1=ALU.add), r=[arg], w=[t0])
    P.op("dve", lambda e: e.tensor_scalar(out=t0[:], in0=t0[:], scalar1=-MAGIC, scalar2=-TWO_PI,
                                          op0=ALU.add, op1=ALU.mult), r=[t0], w=[t0])
    P.op("dve", lambda e: e.tensor_tensor(out=t1[:], in0=arg[:], in1=t0[:], op=ALU.add), r=[arg, t0], w=[t1])
    P.op("act", lambda e: e.activation(out=s_out[:], in_=t1[:], func=AF.Sin), r=[t1], w=[s_out])
    P.op("act", lambda e: e.activation(out=t0[:], in_=t1[:], func=AF.Abs), r=[t1], w=[t0])
    P.op("act", lambda e: e.activation(out=c_out[:], in_=t0[:], func=AF.Sin, scale=-1.0, bias=C["halfpi"][:]),
         r=[t0, C["halfpi"]], w=[c_out])


def s5_scan(P, C, T, ZF, prm, XI_d, SXL_d, Y2):
    cb = C["cb"]
    NTC = (T + 511) // 512
    m0 = P.mark()
    iota = cview(C, "iota")[:, 0:T]
    mask8 = cview(C, "mask8")
    lbu = P.sb([64, 2, 128], BF16, "s5_lbu")
    lc = P.sb([64, 2, 128], BF16, "s5_lc") if Y2 is not None else None
    mp = P.mark()
    def pb(name, free=(16, 64)):
        return P.sb(list(free), F32, "s5_" + name)
    are, aim, brT, biT = pb("are"), pb("aim"), pb("brT"), pb("biT")
    ldt = P.sb([16], F32, "s5_ldt")
    for b_, nm in ((are, "are_b"), (aim, "aim_b"), (brT, "brT"), (biT, "biT"), (ldt, "ldt_b")):
        P.dma("sp", b_[:], prm[nm], w=[b_])
    dt = P.sb([16], F32, "s5_dt")
    P.op("act", lambda e: e.activation(out=dt[:], in_=ldt[:], func=AF.Exp), r=[ldt], w=[dt])
    dtb = dt[:].unsqueeze(2).to_broadcast([128, 16, 64])
    P.op("dve", lambda e: e.tensor_scalar(out=are[:], in0=are[:], scalar1=-1e-4, scalar2=None, op0=ALU.min),
         r=[are], w=[are])
    x1, th, sn, cs, t0, t1 = pb("x1"), pb("th"), pb("sn"), pb("cs"), pb("t0"), pb("t1")
    P.op("dve", lambda e: e.tensor_tensor(out=x1[:], in0=are[:], in1=dtb, op=ALU.mult), r=[are, dt], w=[x1])
    P.op("act", lambda e: e.activation(out=x1[:], in_=x1[:], func=AF.Exp), r=[x1], w=[x1])
    P.op("dve", lambda e: e.tensor_tensor(out=th[:], in0=aim[:], in1=dtb, op=ALU.mult), r=[aim, dt], w=[th])
    sincos(P, C, th, sn, cs, t0, t1)
    abr, abi = cs, sn
    P.op("dve", lambda e: e.tensor_tensor(out=abr[:], in0=cs[:], in1=x1[:], op=ALU.mult), r=[cs, x1], w=[abr])
    P.op("dve", lambda e: e.tensor_tensor(out=abi[:], in0=sn[:], in1=x1[:], op=ALU.mult), r=[sn, x1], w=[abi])
    P.op("dve", lambda e: e.tensor_scalar(out=abr[:], in0=abr[:], scalar1=-1.0, scalar2=None, op0=ALU.add),
         r=[abr], w=[abr])
    den = x1
    P.op("dve", lambda e: e.tensor_tensor(out=t0[:], in0=are[:], in1=are[:], op=ALU.mult), r=[are], w=[t0])
    P.op("dve", lambda e: e.tensor_tensor(out=t1[:], in0=aim[:], in1=aim[:], op=ALU.mult), r=[aim], w=[t1])
    P.op("dve", lambda e: e.tensor_tensor(out=den[:], in0=t0[:], in1=t1[:], op=ALU.add), r=[t0, t1], w=[den])
    P.op("dve", lambda e: e.reciprocal(out=den[:], in_=den[:]), r=[den], w=[den])
    zr, zi = th, pb("zi")
    P.op("dve", lambda e: e.tensor_tensor(out=t0[:], in0=abr[:], in1=are[:], op=ALU.mult), r=[abr, are], w=[t0])
    P.op("dve", lambda e: e.tensor_tensor(out=t1[:], in0=abi[:], in1=aim[:], op=ALU.mult), r=[abi, aim], w=[t1])
    P.op("dve", lambda e: e.tensor_tensor(out=t0[:], in0=t0[:], in1=t1[:], op=ALU.add), r=[t0, t1], w=[t0])
    P.op("dve", lambda e: e.tensor_tensor(out=zr[:], in0=t0[:], in1=den[:], op=ALU.mult), r=[t0, den], w=[zr])
    P.op("dve", lambda e: e.tensor_tensor(out=t0[:], in0=abi[:], in1=are[:], op=ALU.mult), r=[abi, are], w=[t0])
    P.op("dve", lambda e: e.tensor_tensor(out=t1[:], in0=abr[:], in1=aim[:], op=ALU.mult), r=[abr, aim], w=[t1])
    P.op("dve", lambda e: e.tensor_tensor(out=t0[:], in0=t0[:], in1=t1[:], op=ALU.subtract), r=[t0, t1], w=[t0])
    P.op("dve", lambda e: e.tensor_tensor(out=zi[:], in0=t0[:], in1=den[:], op=ALU.mult), r=[t0, den], w=[zi])
    bbr, bbi = are, aim
    P.op("dve", lambda e: e.tensor_tensor(out=t0[:], in0=zr[:], in1=brT[:], op=ALU.mult), r=[zr, brT], w=[t0])
    P.op("dve", lambda e: e.tensor_tensor(out=t1[:], in0=zi[:], in1=biT[:], op=ALU.mult), r=[zi, biT], w=[t1])
    P.op("dve", lambda e: e.tensor_tensor(out=bbr[:], in0=t0[:], in1=t1[:], op=ALU.subtract), r=[t0, t1], w=[bbr])
    P.op("dve", lambda e: e.tensor_tensor(out=t0[:], in0=zr[:], in1=biT[:], op=ALU.mult), r=[zr, biT], w=[t0])
    P.op("dve", lambda e: e.tensor_tensor(out=t1[:], in0=zi[:], in1=brT[:], op=ALU.mult), r=[zi, brT], w=[t1])
    P.op("dve", lambda e: e.tensor_tensor(out=bbi[:], in0=t0[:], in1=t1[:], op=ALU.add), r=[t0, t1], w=[bbi])
    for m in range(64):
        kt, ga = m // 4, (2 * m) % 8
        for ri, src in ((0, bbr), (1, bbi)):
            for g2 in range(2):
                P.op("dve", lambda e, m=m, ri=ri, src=src, g2=g2, kt=kt, ga=ga: e.tensor_scalar(
                    out=lbu[:, m, ri, g2 * 64:(g2 + 1) * 64], in0=src[:, kt, :],
                    scalar1=mask8[:, ga + g2:ga + g2 + 1], scalar2=None, op0=ALU.mult), r=[src, cb], w=[lbu])
    if Y2 is not None:
        crT = P.sb([64, 16], F32, "s5_crT")
        ciT = P.sb([64, 16], F32, "s5_ciT")
        P.dma("sp", crT[:], prm["crT"], w=[crT])
        P.dma("sp", ciT[:], prm["ciT"], w=[ciT])
        P.op("pool", lambda e: e.memset(lc[:], 0.0), w=[lc])
        for m in range(64):
            for g2 in range(2):
                lg_ = (2 * m + g2) % 8
                ps_ = slice(g2 * 64, (g2 + 1) * 64)
                P.op("act", lambda e, m=m, ps_=ps_, lg_=lg_: e.activation(
                    out=lc[ps_, m, 0, lg_ * 16:(lg_ + 1) * 16], in_=crT[ps_, m, :], func=AF.Copy),
                    r=[crT], w=[lc])
                P.op("act", lambda e, m=m, ps_=ps_, lg_=lg_: e.activation(
                    out=lc[ps_, m, 1, lg_ * 16:(lg_ + 1) * 16], in_=ciT[ps_, m, :], func=AF.Copy, scale=-1.0),
                    r=[ciT], w=[lc])
    P.barrier()
    P.off = mp
    apr = P.sb([64], F32, "s5_apr")
    api = P.sb([64], F32, "s5_api")
    ldp = P.sb([64], F32, "s5_ldp")
    for b_, nm in ((apr, "are_p"), (api, "aim_p"), (ldp, "ldt_p")):
        P.dma("sp", b_[:], prm[nm], w=[b_])
    rp = P.sb([64], F32, "s5_rp")
    thp = P.sb([64], F32, "s5_thp")
    thn = P.sb([64], F32, "s5_thn")
    P.op("act", lambda e: e.activation(out=ldp[:], in_=ldp[:], func=AF.Exp), r=[ldp], w=[ldp])
    P.op("dve", lambda e: e.tensor_scalar(out=apr[:], in0=apr[:], scalar1=-1e-4, scalar2=None, op0=ALU.min),
         r=[apr], w=[apr])
    P.op("dve", lambda e: e.tensor_tensor(out=rp[:], in0=apr[:], in1=ldp[:], op=ALU.mult), r=[apr, ldp], w=[rp])
    P.op("act", lambda e: e.activation(out=rp[:], in_=rp[:], func=AF.Exp), r=[rp], w=[rp])
    P.op("dve", lambda e: e.tensor_tensor(out=thp[:], in0=api[:], in1=ldp[:], op=ALU.mult), r=[api, ldp], w=[thp])
    P.op("dve", lambda e: e.tensor_scalar(out=thn[:], in0=thp[:], scalar1=1.0 / TWO_PI, scalar2=None, op0=ALU.mult),
         r=[thp], w=[thn])
    xh0 = P.sb([64, 2], F32, "s5_xh0")
    if XI_d is not None:
        xi = P.sb([64, 2], F32, "s5_xi")
        P.dma("sp", xi[:], XI_d.t, r=[XI_d], w=[xi])
        s1_, c1_, q0, q1 = (P.sb([64], F32, "s5_i%d" % i) for i in range(4))
        sincos(P, C, thp, s1_, c1_, q0, q1)
        P.op("dve", lambda e: e.tensor_tensor(out=q0[:], in0=xi[:, :, 0], in1=c1_[:], op=ALU.mult), r=[xi, c1_], w=[q0])
        P.op("dve", lambda e: e.tensor_tensor(out=q1[:], in0=xi[:, :, 1], in1=s1_[:], op=ALU.mult), r=[xi, s1_], w=[q1])
        P.op("dve", lambda e: e.tensor_tensor(out=xh0[:, :, 0], in0=q0[:], in1=q1[:], op=ALU.subtract), r=[q0, q1], w=[xh0])
        P.op("dve", lambda e: e.tensor_tensor(out=q0[:], in0=xi[:, :, 1], in1=c1_[:], op=ALU.mult), r=[xi, c1_], w=[q0])
        P.op("dve", lambda e: e.tensor_tensor(out=q1[:], in0=xi[:, :, 0], in1=s1_[:], op=ALU.mult), r=[xi, s1_], w=[q1])
        P.op("dve", lambda e: e.tensor_tensor(out=xh0[:, :, 1], in0=q0[:], in1=q1[:], op=ALU.add), r=[q0, q1], w=[xh0])
    else:
        P.op("dve", lambda e: e.memset(xh0[:], 0.0), w=[xh0])
    sxl = P.sb([64, 2], F32, "s5_sxl")
    fq = [P.sb([1], F32, "s5_fq%d" % i) for i in range(2)]
    B1, B2, B3, B4, B5, B6 = (P.sb([T], F32, "s5_B%d" % i) for i in range(6))
    uf = P.sb([T], F32, "s5_uf")
    ub = P.sb([T], BF16, "s5_ub")
    xrb = P.sb([T], BF16, "s5_xrb")
    xib = P.sb([T], BF16, "s5_xib")
    dsk = P.sb([16], F32, "s5_d")
    ystg = P.sb([512], F32, "s5_ystg")
    if Y2 is not None:
        P.dma("sp", dsk[:], prm["d"], w=[dsk])
    psB = [P.ps[0], P.ps[1]]
    psY = [P.ps[2], P.ps[3], P.ps[4], P.ps[5]]
    for m in range(64):
        kt = m // 4
        if m % 4 == 0:
            P.dma("sp", uf[:], ZF[6144 + kt * 128:6144 + (kt + 1) * 128, :], r=[ZF], w=[uf])
            P.op("pool", lambda e: e.tensor_copy(out=ub[:], in_=uf[:]), r=[uf], w=[ub])
        for ri, dst in ((0, B1), (1, B2)):
            for tc in range(NTC):
                c0, c1 = tc * 512, min(T, tc * 512 + 512)
                pbk = psB[(ri * NTC + tc) % 2]
                P.op("pe", lambda e, m=m, ri=ri, c0=c0, c1=c1, pbk=pbk: e.matmul(
                    pbk[:, 0:c1 - c0], lbu[:, m, ri, :], ub[:, c0:c1], start=True, stop=True), r=[lbu, ub], w=[pbk])
                P.op("act", lambda e, dst=dst, c0=c0, c1=c1, pbk=pbk: e.activation(
                    out=dst[:, c0:c1], in_=pbk[:, 0:c1 - c0], func=AF.Copy), r=[pbk], w=[dst])
        P.op("dve", lambda e, m=m: e.tensor_scalar(out=B5[:], in0=iota, scalar1=thn[:, m:m + 1], scalar2=MAGIC,
                                                   op0=ALU.mult, op1=ALU.add), r=[cb, thn], w=[B5])
        P.op("dve", lambda e: e.tensor_scalar(out=B5[:], in0=B5[:], scalar1=-MAGIC, scalar2=-TWO_PI,
                                              op0=ALU.add, op1=ALU.mult), r=[B5], w=[B5])
        P.op("dve", lambda e, m=m: e.scalar_tensor_tensor(out=B5[:], in0=iota, scalar=thp[:, m:m + 1], in1=B5[:],
                                                          op0=ALU.mult, op1=ALU.add), r=[cb, thp, B5], w=[B5])
        P.op("act", lambda e: e.activation(out=B4[:], in_=B5[:], func=AF.Sin), r=[B5], w=[B4])
        P.op("act", lambda e: e.activation(out=B6[:], in_=B5[:], func=AF.Abs), r=[B5], w=[B6])
        P.op("act", lambda e: e.activation(out=B3[:], in_=B6[:], func=AF.Sin, scale=-1.0, bias=C["halfpi"][:]),
             r=[B6, C["halfpi"]], w=[B3])
        P.op("dve", lambda e: e.tensor_tensor(out=B5[:], in0=B1[:], in1=B3[:], op=ALU.mult), r=[B1, B3], w=[B5])
        P.op("pool", lambda e: e.tensor_tensor(out=B6[:], in0=B2[:], in1=B4[:], op=ALU.mult), r=[B2, B4], w=[B6])
        P.op("dve", lambda e: e.tensor_tensor(out=B5[:], in0=B5[:], in1=B6[:], op=ALU.add), r=[B5, B6], w=[B5])
        P.op("pool", lambda e: e.tensor_tensor(out=B6[:], in0=B2[:], in1=B3[:], op=ALU.mult), r=[B2, B3], w=[B6])
        P.op("dve", lambda e: e.tensor_tensor(out=B2[:], in0=B1[:], in1=B4[:], op=ALU.mult), r=[B1, B4, B6], w=[B2])
        P.op("dve", lambda e: e.tensor_tensor(out=B6[:], in0=B6[:], in1=B2[:], op=ALU.subtract), r=[B6, B2], w=[B6])
        rb = rp[:, m:m + 1].to_broadcast([128, T])
        P.op("dve", lambda e, m=m, rb=rb: e.tensor_tensor_scan(out=B1[:], data0=rb, data1=B5[:], initial=xh0[:, m, 0:1],
                                                               op0=ALU.mult, op1=ALU.add), r=[rp, B5, xh0], w=[B1])
        P.op("dve", lambda e, m=m, rb=rb: e.tensor_tensor_scan(out=B2[:], data0=rb, data1=B6[:], initial=xh0[:, m, 1:2],
                                                               op0=ALU.mult, op1=ALU.add), r=[rp, B6, xh0], w=[B2])
        if SXL_d is not None:
            L = T - 1
            P.op("dve", lambda e, L=L: e.tensor_tensor(out=fq[0][:], in0=B1[:, L:L + 1], in1=B3[:, L:L + 1], op=ALU.mult),
                 r=[B1, B3], w=[fq[0]])
            P.op("dve", lambda e, L=L: e.tensor_tensor(out=fq[1][:], in0=B2[:, L:L + 1], in1=B4[:, L:L + 1], op=ALU.mult),
                 r=[B2, B4], w=[fq[1]])
            P.op("dve", lambda e, m=m: e.tensor_tensor(out=sxl[:, m, 0:1], in0=fq[0][:], in1=fq[1][:], op=ALU.subtract),
                 r=fq, w=[sxl])
            P.op("dve", lambda e, L=L: e.tensor_tensor(out=fq[0][:], in0=B2[:, L:L + 1], in1=B3[:, L:L + 1], op=ALU.mult),
                 r=[B2, B3], w=[fq[0]])
            P.op("dve", lambda e, L=L: e.tensor_tensor(out=fq[1][:], in0=B1[:, L:L + 1], in1=B4[:, L:L + 1], op=ALU.mult),
                 r=[B1, B4], w=[fq[1]])
            P.op("dve", lambda e, m=m: e.tensor_tensor(out=sxl[:, m, 1:2], in0=fq[0][:], in1=fq[1][:], op=ALU.add),
                 r=fq, w=[sxl])
        if Y2 is not None:
            P.op("dve", lambda e: e.tensor_tensor(out=B5[:], in0=B1[:], in1=B3[:], op=ALU.mult), r=[B1, B3], w=[B5])
            P.op("pool", lambda e: e.tensor_tensor(out=B6[:], in0=B2[:], in1=B4[:], op=ALU.mult), r=[B2, B4], w=[B6])
            P.op("dve", lambda e: e.tensor_tensor(out=xrb[:], in0=B5[:], in1=B6[:], op=ALU.subtract), r=[B5, B6], w=[xrb])
            P.op("pool", lambda e: e.tensor_tensor(out=B5[:], in0=B2[:], in1=B3[:], op=ALU.mult), r=[B2, B3], w=[B5])
            P.op("dve", lambda e: e.tensor_tensor(out=B6[:], in0=B1[:], in1=B4[:], op=ALU.mult), r=[B1, B4], w=[B6])
            P.op("dve", lambda e: e.tensor_tensor(out=xib[:], in0=B5[:], in1=B6[:], op=ALU.add), r=[B5, B6], w=[xib])
            for tc in range(NTC):
                c0, c1 = tc * 512, min(T, tc * 512 + 512)
                for ri, src in ((0, xrb), (1, xib)):
                    P.op("pe", lambda e, m=m, ri=ri, src=src, c0=c0, c1=c1, tc=tc: e.matmul(
                        psY[tc][:, 0:c1 - c0], lc[:, m, ri, :], src[:, c0:c1],
                        start=(m % 4 == 0 and ri == 0), stop=(m % 4 == 3 and ri == 1)), r=[lc, src], w=[psY[tc]])
            if m % 4 == 3:
                ft = m // 4
                for tc in range(NTC):
                    c0, c1 = tc * 512, min(T, tc * 512 + 512)
                    P.op("dve", lambda e, ft=ft, c0=c0, c1=c1, tc=tc: e.scalar_tensor_tensor(
                        out=ystg[:, 0:c1 - c0], in0=uf[:, c0:c1], scalar=dsk[:, ft:ft + 1], in1=psY[tc][:, 0:c1 - c0],
                        op0=ALU.mult, op1=ALU.add), r=[uf, dsk, psY[tc]], w=[ystg])
                    P.op("act", lambda e, c0=c0, c1=c1: e.activation(out=ystg[:, 0:c1 - c0], in_=ystg[:, 0:c1 - c0],
                                                                     func=AF.Gelu), r=[ystg], w=[ystg])
                    P.dma("sp", Y2[ft * 128:(ft + 1) * 128, c0:c1], ystg[:, 0:c1 - c0], r=[ystg], w=[Y2])
    if SXL_d is not None:
        P.dma("sp", SXL_d.t, sxl[:], r=[sxl], w=[SXL_d])
    P.release(m0)


def glu_stage(P, C, T, Y2, Wg, OT, row0):
    m0 = P.mark()
    yb = P.sb([16, T], BF16, "gl_yb")
    yf = P.sb([T], F32, "gl_yf")
    og = P.sb([T], BF16, "gl_og")
    sl = P.sb([16, 512], BF16, "gl_slab")
    sgm = P.sb([512], F32, "gl_sg")
    for kt in range(16):
        P.dma("sp", yf[:], Y2[kt * 128:(kt + 1) * 128, :], r=[Y2], w=[yf])
        P.op("dve", lambda e, kt=kt: e.tensor_copy(out=yb[:, kt, :], in_=yf[:]), r=[yf], w=[yb])
    banks = [P.ps[0], P.ps[1]]
    k = 0
    for s in range(4):
        load_wslab(P, sl, Wg, s * 512, (s + 1) * 512, 16)
        for nt in range(4):
            n = s * 4 + nt
            P.dma("sp", yf[:], Y2[n * 128:(n + 1) * 128, :], r=[Y2], w=[yf])
            for c0 in range(0, T, 512):
                c1 = min(T, c0 + 512)
                pb = banks[k % 2]
                k += 1
                for kt in range(16):
                    P.op("pe", lambda e, kt=kt, nt=nt, c0=c0, c1=c1, pb=pb: e.matmul(
                        pb[:, 0:c1 - c0], sl[:, kt, nt * 128:(nt + 1) * 128], yb[:, kt, c0:c1],
                        start=(kt == 0), stop=(kt == 15)), r=[sl, yb], w=[pb])
                P.op("act", lambda e, c0=c0, c1=c1, pb=pb: e.activation(out=sgm[:, 0:c1 - c0], in_=pb[:, 0:c1 - c0],
                                                                        func=AF.Sigmoid), r=[pb], w=[sgm])
                P.op("dve", lambda e, c0=c0, c1=c1: e.tensor_tensor(out=og[:, c0:c1], in0=yf[:, c0:c1],
                                                                    in1=sgm[:, 0:c1 - c0], op=ALU.mult),
                     r=[yf, sgm], w=[og])
            P.dma("sp", OT[row0 + n * 128:row0 + (n + 1) * 128, :], og[:], r=[og], w=[OT])
    P.release(m0)


def s5_host_params(a_re, a_im, log_dt, b_re, b_im, c_re, c_im, b_d):
    def bl(a):
        a = a.reshape(16, 8, 1, 64)
        a = np.broadcast_to(a, (16, 8, 16, 64)).transpose(1, 2, 0, 3)
        return np.ascontiguousarray(a.reshape(128, 16, 64))
    def pl(a):
        return np.ascontiguousarray(a.reshape(64, 2, 64).transpose(1, 2, 0).reshape(128, 64))
    out = {}
    out["are_b"] = bl(a_re)
    out["aim_b"] = bl(a_im)
    l = np.broadcast_to(log_dt.reshape(16, 8, 1), (16, 8, 16)).transpose(1, 2, 0)
    out["ldt_b"] = np.ascontiguousarray(l.reshape(128, 16))
    out["brT"] = np.ascontiguousarray(b_re.reshape(16, 8, 64, 16).transpose(1, 3, 0, 2).reshape(128, 16, 64))
    out["biT"] = np.ascontiguousarray(b_im.reshape(16, 8, 64, 16).transpose(1, 3, 0, 2).reshape(128, 16, 64))
    out["are_p"] = pl(a_re)
    out["aim_p"] = pl(a_im)
    out["ldt_p"] = pl(np.broadcast_to(log_dt[:, None], (128, 64)))
    out["crT"] = np.ascontiguousarray(c_re.reshape(64, 2, 16, 64).transpose(1, 3, 0, 2).reshape(128, 64, 16))
    out["ciT"] = np.ascontiguousarray(c_im.reshape(64, 2, 16, 64).transpose(1, 3, 0, 2).reshape(128, 64, 16))
    out["d"] = np.ascontiguousarray(b_d.reshape(16, 128).T)
    return {k: v.astype(np.float32) for k, v in out.items()}


S5_SHAPES = {"are_b": [128, 16, 64], "aim_b": [128, 16, 64], "ldt_b": [128, 16], "brT": [128, 16, 64],
             "biT": [128, 16, 64], "are_p": [128, 64], "aim_p": [128, 64], "ldt_p": [128, 64],
             "crT": [128, 64, 16], "ciT": [128, 64, 16], "d": [128, 16]}


def ret_gammas():
    return [1.0 - 2.0 ** (-5.0 - h) for h in range(16)]


def make_ret_consts():
    g = np.array(ret_gammas(), np.float64)
    s = np.arange(128)
    same = (s[:, None] // 64) == (s[None, :] // 64)
    dist = np.abs(s[:, None] - s[None, :]).astype(np.float64)
    decT = np.stack([np.where(same, gh ** dist, 0.0) / 16.0 for gh in g], axis=1)
    tl = np.arange(64, dtype=np.float64)
    qdec = np.broadcast_to((g[:, None] ** (tl[None, :] + 1.0))[None], (128, 16, 64))
    kdec = (g[None, :] ** (63.0 - (s % 64))[:, None]) / 16.0
    return (np.ascontiguousarray(decT.reshape(128, 16 * 128), np.float32),
            np.ascontiguousarray(qdec.reshape(128, 16 * 64), np.float32),
            np.ascontiguousarray(kdec, np.float32))


def make_rope(pos0, T):
    pos = np.arange(pos0, pos0 + T, dtype=np.float32)
    inv = (np.float32(10000.0) ** (-np.arange(0, 256, 2, dtype=np.float32) / np.float32(256))).astype(np.float32)
    ang = (pos[:, None] * inv[None, :]).astype(np.float32)
    c, s_ = np.cos(ang).astype(np.float32), np.sin(ang).astype(np.float32)
    NT = T // 128
    cT = c.reshape(NT, 128, 128).transpose(1, 0, 2).reshape(128, NT * 128)
    sT = s_.reshape(NT, 128, 128).transpose(1, 0, 2).reshape(128, NT * 128)
    return np.ascontiguousarray(np.concatenate([c.T, s_.T, cT, sT], axis=1), np.float32)


def ret_pass1(P, C, T, ZF, ZT, rope_d, rc, KD, VB, QR, KR, QD, RLd):
    NT = T // 128
    NCH = T // 64
    gam = ret_gammas()
    m0 = P.mark()
    rope = P.sb([4 * T], F32, "r_rope")
    P.dma("sp", rope[:], rope_d, w=[rope])
    cosF, sinF = rope[:, 0:T], rope[:, T:2 * T]
    cosT = rope[:, 2 * T:3 * T].rearrange("p (n d) -> p n d", n=NT)
    sinT = rope[:, 3 * T:4 * T].rearrange("p (n d) -> p n d", n=NT)
    qdec = P.sb([16, 64], F32, "r_qdec")
    kdec = P.sb([16], F32, "r_kdec")
    P.dma("sp", qdec[:].rearrange("p a b -> p (a b)"), rc["qdec"], w=[qdec])
    P.dma("sp", kdec[:], rc["kdec"], w=[kdec])
    kt_ = P.sb([256], F32, "r_kt")
    vt_ = P.sb([512], F32, "r_vt")
    u0 = P.sb([128], F32, "r_u0")
    u1 = P.sb([128], F32, "r_u1")
    kr_ = P.sb([256], F32, "r_kr")
    kd = P.sb([NT, 256], BF16, "r_kd")
    vb = P.sb([NT, 512], BF16, "r_vb")
    XA, XB, W0, W1 = (P.sb([T], F32, "r_X%d" % i) for i in range(4))
    oA, oB, dA, dB = (P.sb([T], BF16, "r_o%d" % i) for i in range(4))
    R = [P.sb([512], F32, "r_R%d" % i) for i in range(2)]
    KDv = KD.t.rearrange("(nt p) c -> p nt c", p=128)
    VBv = VB.t.rearrange("(nt p) c -> p nt c", p=128)
    psU = [[P.ps[0], P.ps[1]], [P.ps[2], P.ps[3]]]
    for h in range(16):
        for tt in range(NT):
            rows = slice(tt * 128, (tt + 1) * 128)
            P.dma("sp", kt_[:], ZT[rows, h * 256:(h + 1) * 256], r=[ZT], w=[kt_])
            P.dma("sp", vt_[:], ZT[rows, 4096 + h * 512:4096 + (h + 1) * 512], r=[ZT], w=[vt_])
            c_, s_ = cosT[:, tt, :], sinT[:, tt, :]
            P.op("dve", lambda e, c_=c_: e.tensor_tensor(out=u0[:], in0=kt_[:, 0:128], in1=c_, op=ALU.mult), r=[kt_, rope], w=[u0])
            P.op("pool", lambda e, s_=s_: e.tensor_tensor(out=u1[:], in0=kt_[:, 128:256], in1=s_, op=ALU.mult), r=[kt_, rope], w=[u1])
            P.op("dve", lambda e: e.tensor_tensor(out=kr_[:, 0:128], in0=u0[:], in1=u1[:], op=ALU.subtract), r=[u0, u1], w=[kr_])
            P.op("dve", lambda e, c_=c_: e.tensor_tensor(out=u0[:], in0=kt_[:, 128:256], in1=c_, op=ALU.mult), r=[kt_, rope], w=[u0])
            P.op("pool", lambda e, s_=s_: e.tensor_tensor(out=u1[:], in0=kt_[:, 0:128], in1=s_, op=ALU.mult), r=[kt_, rope], w=[u1])
            P.op("dve", lambda e: e.tensor_tensor(out=kr_[:, 128:256], in0=u0[:], in1=u1[:], op=ALU.add), r=[u0, u1], w=[kr_])
            P.op("dve", lambda e, tt=tt, h=h: e.tensor_scalar(out=kd[:, tt, :], in0=kr_[:], scalar1=kdec[:, h:h + 1], scalar2=None,
                                                              op0=ALU.mult), r=[kr_, kdec], w=[kd])
            P.op("pool", lambda e, tt=tt: e.tensor_copy(out=vb[:, tt, :], in_=vt_[:]), r=[vt_], w=[vb])
        P.dma("sp", KDv[:, :, h * 256:(h + 1) * 256], kd[:], r=[kd], w=[KD])
        P.dma("sp", VBv[:, :, h * 512:(h + 1) * 512], vb[:], r=[vb], w=[VB])
        for which, base, outs in (("q", 0, (oA, oB)), ("k", 4096, (oA, oB))):
            P.dma("sp", XA[:], ZF[base + h * 256:base + h * 256 + 128, :], r=[ZF], w=[XA])
            P.dma("sp", XB[:], ZF[base + h * 256 + 128:base + h * 256 + 256, :], r=[ZF], w=[XB])
            P.op("dve", lambda e: e.tensor_tensor(out=W0[:], in0=XA[:], in1=cosF, op=ALU.mult), r=[XA, rope], w=[W0])
            P.op("pool", lambda e: e.tensor_tensor(out=W1[:], in0=XB[:], in1=sinF, op=ALU.mult), r=[XB, rope], w=[W1])
            P.op("dve", lambda e: e.tensor_tensor(out=W0[:], in0=W0[:], in1=W1[:], op=ALU.subtract), r=[W0, W1], w=[W0])
            P.op("act", lambda e: e.activation(out=oA[:], in_=W0[:], func=AF.Copy), r=[W0], w=[oA])
            if which == "q":
                P.op("dve", lambda e, h=h: e.tensor_tensor(
                    out=dA[:].rearrange("p (c s) -> p c s", s=64), in0=W0[:].rearrange("p (c s) -> p c s", s=64),
                    in1=qdec[:, h, :].unsqueeze(1).to_broadcast([128, NCH, 64]), op=ALU.mult), r=[W0, qdec], w=[dA])
            P.op("pool", lambda e: e.tensor_tensor(out=W1[:], in0=XA[:], in1=sinF, op=ALU.mult), r=[XA, rope], w=[W1])
            P.op("dve", lambda e: e.tensor_tensor(out=W0[:], in0=XB[:], in1=cosF, op=ALU.mult), r=[XB, rope], w=[W0])
            P.op("dve", lambda e: e.tensor_tensor(out=W0[:], in0=W0[:], in1=W1[:], op=ALU.add), r=[W0, W1], w=[W0])
            P.op("act", lambda e: e.activation(out=oB[:], in_=W0[:], func=AF.Copy), r=[W0], w=[oB])
            if which == "q":
                P.op("dve", lambda e, h=h: e.tensor_tensor(
                    out=dB[:].rearrange("p (c s) -> p c s", s=64), in0=W0[:].rearrange("p (c s) -> p c s", s=64),
                    in1=qdec[:, h, :].unsqueeze(1).to_broadcast([128, NCH, 64]), op=ALU.mult), r=[W0, qdec], w=[dB])
            dst = QR if which == "q" else KR
            P.dma("sp", dst[h * 256:h * 256 + 128, :], oA[:], r=[oA], w=[dst])
            P.dma("sp", dst[h * 256 + 128:h * 256 + 256, :], oB[:], r=[oB], w=[dst])
            if which == "q":
                P.dma("sp", QD[h * 256:h * 256 + 128, :], dA[:], r=[dA], w=[QD])
                P.dma("sp", QD[h * 256 + 128:h * 256 + 256, :], dB[:], r=[dB], w=[QD])
        for dtl in range(2):
            P.op("dve", lambda e, dtl=dtl: e.memset(R[dtl][:], 0.0), w=[R[dtl]])
        g64 = float(gam[h] ** 64)
        for c in range(NCH):
            tt, b0 = c // 2, (c % 2) * 64
            for dtl in range(2):
                pu = psU[c % 2][dtl]
                P.op("pe", lambda e, tt=tt, b0=b0, dtl=dtl, pu=pu: e.matmul(
                    pu[:], kd[b0:b0 + 64, tt, dtl * 128:(dtl + 1) * 128], vb[b0:b0 + 64, tt, :], start=True, stop=True),
                    r=[kd, vb], w=[pu])
                P.op("dve", lambda e, dtl=dtl, pu=pu, g64=g64: e.scalar_tensor_tensor(
                    out=R[dtl][:], in0=R[dtl][:], scalar=g64, in1=pu[:], op0=ALU.mult, op1=ALU.add),
                    r=[R[dtl], pu], w=[R[dtl]])
        for dtl in range(2):
            P.dma("sp", RLd[:, h, dtl, :], R[dtl][:], r=[R[dtl]], w=[RLd])
    P.release(m0)


def ret_pass2(P, C, T, ZF, rc, KD, VB, QR, KR, QD, RId, OT):
    NT = T // 128
    gam = ret_gammas()
    cb = C["cb"]
    m0 = P.mark()
    decT = P.sb([16, 128], F32, "q_decT")
    P.dma("sp", decT[:].rearrange("p a b -> p (a b)"), rc["decT"], w=[decT])
    od512 = P.sb([128], F32, "q_od")
    P.op("dve", lambda e: e.memset(od512[:], 1.0 / 512), w=[od512])
    kd = P.sb([NT, 256], BF16, "q_kd")
    vb = P.sb([NT, 512], BF16, "q_vb")
    qr = P.sb([2, T], BF16, "q_qr")
    kr = P.sb([2, T], BF16, "q_kr")
    qd = P.sb([2, T], BF16, "q_qd")
    sg = P.sb([4, T], F32, "q_sg")
    og = P.sb([4, T], BF16, "q_og")
    R = [P.sb([512], F32, "q_R%d" % i) for i in range(2)]
    Rb = [P.sb([512], BF16, "q_Rb%d" % i) for i in range(2)]
    sm = P.sb([128], BF16, "q_sm")
    osb = P.sb([4, 128], F32, "q_osb")
    sq = P.sb([4, 128], F32, "q_sq")
    rsd = P.sb([128], F32, "q_rsd")
    KDv = KD.t.rearrange("(nt p) c -> p nt c", p=128)
    VBv = VB.t.rearrange("(nt p) c -> p nt c", p=128)
    psS, psO, psM = P.ps[4], P.ps[5], P.ps[6]
    psU = [[P.ps[0], P.ps[1]], [P.ps[2], P.ps[3]]]
    psOv = psO.t.rearrange("p (j t) -> p j t", j=4)
    for h in range(16):
        g64 = float(gam[h] ** 64)
        P.dma("sp", kd[:], KDv[:, :, h * 256:(h + 1) * 256], r=[KD], w=[kd])
        P.dma("sp", vb[:], VBv[:, :, h * 512:(h + 1) * 512], r=[VB], w=[vb])
        for dtl in range(2):
            rs_ = slice(h * 256 + dtl * 128, h * 256 + (dtl + 1) * 128)
            P.dma("sp", qr[:, dtl, :], QR[rs_, :], r=[QR], w=[qr])
            P.dma("sp", kr[:, dtl, :], KR[rs_, :], r=[KR], w=[kr])
            P.dma("sp", qd[:, dtl, :], QD[rs_, :], r=[QD], w=[qd])
            P.dma("sp", R[dtl][:], RId[:, h, dtl, :], r=[RId], w=[R[dtl]])
            P.op("act", lambda e, dtl=dtl: e.activation(out=Rb[dtl][:], in_=R[dtl][:], func=AF.Copy), r=[R[dtl]], w=[Rb[dtl]])
        for j in range(4):
            P.dma("sp", sg[:, j, :], ZF[8192 + h * 512 + j * 128:8192 + h * 512 + (j + 1) * 128, :], r=[ZF], w=[sg])
        for tt in range(NT):
            ts = slice(tt * 128, (tt + 1) * 128)
            for dtl in range(2):
                P.op("pe", lambda e, dtl=dtl, ts=ts: e.matmul(psS[:, 0:128], kr[:, dtl, ts], qr[:, dtl, ts],
                                                              start=(dtl == 0), stop=(dtl == 1)), r=[kr, qr], w=[psS])
            P.op("dve", lambda e, h=h: e.tensor_tensor(out=sm[:], in0=psS[:, 0:128], in1=decT[:, h, :], op=ALU.mult),
                 r=[psS, decT], w=[sm])
            for ci in range(2):
                c = 2 * tt + ci
                b0 = ci * 64
                for j in range(4):
                    P.op("pe", lambda e, j=j, tt=tt, b0=b0: e.matmul(
                        psOv[:, j, b0:b0 + 64], vb[b0:b0 + 64, tt, j * 128:(j + 1) * 128], sm[b0:b0 + 64, b0:b0 + 64],
                        start=True, stop=False), r=[vb, sm], w=[psO])
                    for dtl in range(2):
                        P.op("pe", lambda e, j=j, dtl=dtl, b0=b0, c=c: e.matmul(
                            psOv[:, j, b0:b0 + 64], Rb[dtl][:, j * 128:(j + 1) * 128], qd[:, dtl, c * 64:(c + 1) * 64],
                            start=False, stop=(dtl == 1)), r=[Rb[dtl], qd], w=[psO])
                for dtl in range(2):
                    pu = psU[c % 2][dtl]
                    P.op("pe", lambda e, tt=tt, b0=b0, dtl=dtl, pu=pu: e.matmul(
                        pu[:], kd[b0:b0 + 64, tt, dtl * 128:(dtl + 1) * 128], vb[b0:b0 + 64, tt, :], start=True, stop=True),
                        r=[kd, vb], w=[pu])
                    P.op("dve", lambda e, dtl=dtl, pu=pu, g64=g64: e.scalar_tensor_tensor(
                        out=R[dtl][:], in0=R[dtl][:], scalar=g64, in1=pu[:], op0=ALU.mult, op1=ALU.add),
                        r=[R[dtl], pu], w=[R[dtl]])
                    P.op("act", lambda e, dtl=dtl: e.activation(out=Rb[dtl][:], in_=R[dtl][:], func=AF.Copy),
                         r=[R[dtl]], w=[Rb[dtl]])
            P.op("act", lambda e: e.activation(out=osb[:], in_=psOv, func=AF.Copy), r=[psO], w=[osb])
            P.op("act", lambda e: e.activation(out=sq[:], in_=psOv, func=AF.Square), r=[psO], w=[sq])
            for j in range(4):
                P.op("pe", lambda e, j=j: e.matmul(psM[:, 0:128], od512[:], sq[:, j, :], start=(j == 0), stop=(j == 3)),
                     r=[od512, sq], w=[psM])
            P.op("act", lambda e: e.activation(out=rsd[:], in_=psM[:, 0:128], func=AF.Sqrt, bias=C["eps"][:], scale=1.0),
                 r=[psM, C["eps"]], w=[rsd])
            P.op("dve", lambda e: e.reciprocal(out=rsd[:], in_=rsd[:]), r=[rsd], w=[rsd])
            P.op("dve", lambda e: e.tensor_tensor(out=osb[:], in0=osb[:], in1=rsd[:].unsqueeze(1).to_broadcast([128, 4, 128]),
                                                  op=ALU.mult), r=[osb, rsd], w=[osb])
            P.op("dve", lambda e, ts=ts: e.tensor_tensor(out=og[:, :, ts], in0=osb[:], in1=sg[:, :, ts], op=ALU.mult),
                 r=[osb, sg], w=[og])
        for j in range(4):
            P.dma("sp", OT[h * 512 + j * 128:h * 512 + (j + 1) * 128, :], og[:, j, :], r=[og], w=[OT])
    P.release(m0)


def gather_states(P, src, dst, ncores):
    if ncores == 1:
        P.dma("sp", dst.t, src.t, r=[src], w=[dst])
    else:
        P.collective("AllGather", src.handle.ap().opt(), dst.handle.ap().opt(), [list(range(ncores))],
                     r=[src], w=[dst])


def hgrn2_combine(P, C, SLall, sel, SId, ncores, seg_per_batch):
    m0 = P.mark()
    F = P.sb([16, 128], F32, "hc_F")
    SI = P.sb([16, 128], F32, "hc_SI")
    L = P.sb([2064], F32, "hc_L")
    P.op("dve", lambda e: e.memset(F[:], 0.0), w=[F])
    P.op("dve", lambda e: e.memset(SI[:], 0.0), w=[SI])
    for r in range(ncores):
        P.op("dve", lambda e, r=r: e.scalar_tensor_tensor(out=SI[:], in0=F[:], scalar=sel[:, r:r + 1], in1=SI[:],
                                                          op0=ALU.mult, op1=ALU.add), r=[F, sel, SI], w=[SI])
        if r == ncores - 1:
            break
        if (r + 1) % seg_per_batch == 0:
            P.op("dve", lambda e: e.memset(F[:], 0.0), w=[F])
            continue
        P.dma("sp", L[:], SLall[r * 128:(r + 1) * 128, :], r=[SLall], w=[L])
        P.op("dve", lambda e: e.tensor_tensor(out=F[:], in0=F[:], in1=L[:, 2048:2064].unsqueeze(2).to_broadcast([128, 16, 128]),
                                              op=ALU.mult), r=[F, L], w=[F])
        P.op("dve", lambda e: e.tensor_tensor(out=F[:], in0=F[:], in1=L[:, 0:2048].rearrange("p (h d) -> p h d", h=16),
                                              op=ALU.add), r=[F, L], w=[F])
    P.dma("sp", SId.t, SI[:].rearrange("p h d -> p (h d)"), r=[SI], w=[SId])
    P.release(m0)


def ret_combine(P, C, T, RLall, sel, RId, ncores, seg_per_batch):
    gam = ret_gammas()
    m0 = P.mark()
    F = P.sb([512], F32, "rc_F")
    SI = P.sb([512], F32, "rc_SI")
    L = [P.sb([512], F32, "rc_L%d" % i) for i in range(2)]
    RLv = RLall.t.rearrange("r (h d e) -> r h d e", h=16, d=2)
    k = 0
    for h in range(16):
        gT = float(gam[h] ** T)
        for dtl in range(2):
            P.op("dve", lambda e: e.memset(F[:], 0.0), w=[F])
            P.op("dve", lambda e: e.memset(SI[:], 0.0), w=[SI])
            for r in range(ncores):
                P.op("dve", lambda e, r=r: e.scalar_tensor_tensor(out=SI[:], in0=F[:], scalar=sel[:, r:r + 1], in1=SI[:],
                                                                  op0=ALU.mult, op1=ALU.add), r=[F, sel, SI], w=[SI])
                if r == ncores - 1:
                    break
                if (r + 1) % seg_per_batch == 0:
                    P.op("dve", lambda e: e.memset(F[:], 0.0), w=[F])
                    continue
                Lb = L[k % 2]
                k += 1
                P.dma("sp", Lb[:], RLv[r * 128:(r + 1) * 128, h, dtl, :], r=[RLall], w=[Lb])
                P.op("dve", lambda e, gT=gT, Lb=Lb: e.scalar_tensor_tensor(out=F[:], in0=F[:], scalar=gT, in1=Lb[:],
                                                                          op0=ALU.mult, op1=ALU.add), r=[F, Lb], w=[F])
            P.dma("sp", RId[:, h, dtl, :], SI[:], r=[SI], w=[RId])
    P.release(m0)


def s5_combine(P, C, T, prm, SXall, sel, XId, ncores, seg_per_batch):
    m0 = P.mark()
    apr = P.sb([64], F32, "sc_apr")
    api = P.sb([64], F32, "sc_api")
    ldp = P.sb([64], F32, "sc_ldp")
    for b_, nm in ((apr, "are_p"), (api, "aim_p"), (ldp, "ldt_p")):
        P.dma("sp", b_[:], prm[nm], w=[b_])
    mg = P.sb([64], F32, "sc_mg")
    th = P.sb([64], F32, "sc_th")
    sn, cs, t0, t1 = (P.sb([64], F32, "sc_t%d" % i) for i in range(4))
    P.op("act", lambda e: e.activation(out=ldp[:], in_=ldp[:], func=AF.Exp), r=[ldp], w=[ldp])
    P.op("dve", lambda e: e.tensor_scalar(out=apr[:], in0=apr[:], scalar1=-1e-4, scalar2=None, op0=ALU.min), r=[apr], w=[apr])
    P.op("dve", lambda e: e.tensor_tensor(out=mg[:], in0=apr[:], in1=ldp[:], op=ALU.mult), r=[apr, ldp], w=[mg])
    P.op("act", lambda e: e.activation(out=mg[:], in_=mg[:], func=AF.Exp, scale=float(T)), r=[mg], w=[mg])
    P.op("dve", lambda e: e.tensor_tensor(out=th[:], in0=api[:], in1=ldp[:], op=ALU.mult), r=[api, ldp], w=[th])
    P.op("dve", lambda e: e.tensor_scalar(out=t0[:], in0=th[:], scalar1=1.0 / TWO_PI, scalar2=MAGIC, op0=ALU.mult, op1=ALU.add),
         r=[th], w=[t0])
    P.op("dve", lambda e: e.tensor_scalar(out=t0[:], in0=t0[:], scalar1=-MAGIC, scalar2=-TWO_PI, op0=ALU.add, op1=ALU.mult),
         r=[t0], w=[t0])
    P.op("dve", lambda e: e.tensor_tensor(out=th[:], in0=th[:], in1=t0[:], op=ALU.add), r=[th, t0], w=[th])
    P.op("dve", lambda e: e.tensor_scalar(out=th[:], in0=th[:], scalar1=float(T), scalar2=None, op0=ALU.mult), r=[th], w=[th])
    sincos(P, C, th, sn, cs, t0, t1)
    ar, ai = cs, sn
    P.op("dve", lambda e: e.tensor_tensor(out=ar[:], in0=cs[:], in1=mg[:], op=ALU.mult), r=[cs, mg], w=[ar])
    P.op("dve", lambda e: e.tensor_tensor(out=ai[:], in0=sn[:], in1=mg[:], op=ALU.mult), r=[sn, mg], w=[ai])
    F = P.sb([64, 2], F32, "sc_F")
    SI = P.sb([64, 2], F32, "sc_SI")
    L = P.sb([64, 2], F32, "sc_L")
    nr = P.sb([64], F32, "sc_nr")
    ni = P.sb([64], F32, "sc_ni")
    P.op("dve", lambda e: e.memset(F[:], 0.0), w=[F])
    P.op("dve", lambda e: e.memset(SI[:], 0.0), w=[SI])
    SXv = SXall.t.rearrange("r (m two) -> r m two", two=2)
    for r in range(ncores):
        P.op("dve", lambda e, r=r: e.scalar_tensor_tensor(out=SI[:], in0=F[:], scalar=sel[:, r:r + 1], in1=SI[:],
                                                          op0=ALU.mult, op1=ALU.add), r=[F, sel, SI], w=[SI])
        if r == ncores - 1:
            break
        if (r + 1) % seg_per_batch == 0:
            P.op("dve", lambda e: e.memset(F[:], 0.0), w=[F])
            continue
        P.dma("sp", L[:], SXv[r * 128:(r + 1) * 128, :, :], r=[SXall], w=[L])
        P.op("dve", lambda e: e.tensor_tensor(out=t0[:], in0=F[:, :, 0], in1=ar[:], op=ALU.mult), r=[F, ar], w=[t0])
        P.op("dve", lambda e: e.tensor_tensor(out=t1[:], in0=F[:, :, 1], in1=ai[:], op=ALU.mult), r=[F, ai], w=[t1])
        P.op("dve", lambda e: e.tensor_tensor(out=nr[:], in0=t0[:], in1=t1[:], op=ALU.subtract), r=[t0, t1], w=[nr])
        P.op("dve", lambda e: e.tensor_tensor(out=t0[:], in0=F[:, :, 1], in1=ar[:], op=ALU.mult), r=[F, ar], w=[t0])
        P.op("dve", lambda e: e.tensor_tensor(out=t1[:], in0=F[:, :, 0], in1=ai[:], op=ALU.mult), r=[F, ai], w=[t1])
        P.op("dve", lambda e: e.tensor_tensor(out=ni[:], in0=t0[:], in1=t1[:], op=ALU.add), r=[t0, t1], w=[ni])
        P.op("dve", lambda e: e.tensor_tensor(out=F[:, :, 0], in0=nr[:], in1=L[:, :, 0], op=ALU.add), r=[nr, L], w=[F])
        P.op("dve", lambda e: e.tensor_tensor(out=F[:, :, 1], in0=ni[:], in1=L[:, :, 1], op=ALU.add), r=[ni, L], w=[F])
    P.dma("sp", XId.t, SI[:], r=[SI], w=[XId])
    P.release(m0)


def final_norm_stage(P, C, Y, T, gfin_d):
    m0 = P.mark()
    grep_ = P.sb([4096], F32, "fn_g")
    P.dma("sp", grep_[:], gfin_d, w=[grep_])
    xt = [P.sb([4096], F32, "fn_x%d" % i) for i in range(2)]
    jk = P.sb([4096], BF16, "fn_j")
    ss = P.sb([1], F32, "fn_ss")
    rs = P.sb([1], F32, "fn_rs")
    for tt in range(T // 128):
        x_ = xt[tt % 2]
        rows = slice(tt * 128, (tt + 1) * 128)
        P.dma("sp", x_[:], Y[rows, :], r=[Y], w=[x_])
        P.op("dve", lambda e: e.memset(ss[:], 0.0), w=[ss])
        P.op("act", lambda e, x_=x_: e.activation(out=jk[:], in_=x_[:], func=AF.Square, accum_out=ss[:]), r=[x_, ss], w=[jk, ss])
        P.op("act", lambda e: e.activation(out=rs[:], in_=ss[:], func=AF.Sqrt, scale=1.0 / D, bias=C["eps"][:]), r=[ss, C["eps"]], w=[rs])
        P.op("dve", lambda e: e.reciprocal(out=rs[:], in_=rs[:]), r=[rs], w=[rs])
        P.op("dve", lambda e, x_=x_: e.scalar_tensor_tensor(out=x_[:], in0=x_[:], scalar=rs[:], in1=grep_[:], op0=ALU.mult, op1=ALU.mult),
             r=[x_, rs, grep_], w=[x_])
        P.dma("sp", Y[rows, :], x_[:], r=[x_], w=[Y])
    P.release(m0)


WEIGHTS = [("ab_w_in", 4096, 10240), ("b_w_glu", 2048, 2048), ("ab_w_out", 4096, 4096),
           ("wq0", 4096, 2048), ("ut0", 4096, 16384), ("v0", 16384, 4096),
           ("c_w_in", 4096, 24576), ("c_w_out", 8192, 4096),
           ("wq1", 4096, 2048), ("ut1", 4096, 16384), ("v1", 16384, 4096)]


PART_W = {0: ("ab_w_in", "b_w_glu", "ab_w_out", "wq0", "ut0", "v0"),
          1: ("c_w_in", "c_w_out", "wq1", "ut1", "v1"),
          "1a": ("c_w_in", "c_w_out"), "1b": ("wq1", "ut1", "v1")}


def part_weights(parts):
    names = [n for p in parts for n in PART_W[p]]
    return [w for w in WEIGHTS if w[0] in names]


def build_full(T, ncores, seg_per_batch, parts=(0, 1)):
    nc = bass.Bass("TRN2", target_bir_lowering=False)
    dt_in = lambda name, shape: nc.dram_tensor(name, list(shape), F32, kind="ExternalInput").ap()
    x_in = dt_in("x", [T, 4096])
    y = nc.dram_tensor("y", [T, 4096], F32, kind="ExternalOutput").ap()
    cnp, cols = make_consts(T)
    consts_d = dt_in("consts", cnp.shape)
    rope_d = dt_in("rope", [128, 4 * T])
    rc = {"decT": dt_in("decT", [128, 2048]), "qdec": dt_in("qdec", [128, 1024]), "kdec": dt_in("kdec", [128, 16])}
    sel_d = dt_in("sel", [128, 8])
    gams = [dt_in("gam%d" % i, [128, 32]) for i in range(4)]
    gfin_d = dt_in("gfin", [128, 4096])
    lbp_d = dt_in("lbp", [128, 2, 16])
    lbrow_d = dt_in("lbrow", [128, 2, 2048])
    prm = {k: dt_in("s5_" + k, v) for k, v in S5_SHAPES.items()}
    keys = [dt_in("keys%d" % i, [128, 16, 128]) for i in range(2)]
    with ExitStack() as st:
        P = Prog(nc, st)
        C = setup_consts(P)
        load_consts(P, C, consts_d, cols)
        sel = P.sb([8], F32, "sel")
        P.dma("sp", sel[:], sel_d, w=[sel])
        Y = Buf(y, "Y")
        for r0 in range(0, T, 256):
            P.dma("sp", y[r0:r0 + 256, :], x_in[r0:r0 + 256, :], w=[Y])
        W = {}
        for name, K, N in part_weights(parts):
            W[name] = Weight(P, name, K, N, ncores)
            W[name].distribute(P)
        NCH = T // 64
        if 0 in parts:
            build_layer0(P, C, Y, T, ncores, seg_per_batch, W, gams, lbp_d, lbrow_d, prm, keys, sel)
        if 1 in parts or "1a" in parts or "1b" in parts:
            build_layer1(P, C, Y, T, ncores, seg_per_batch, W, gams, rope_d, rc, keys, sel, gfin_d,
                         mixer=(1 in parts or "1a" in parts), ffn=(1 in parts or "1b" in parts))
        stats = P.emit()
    return nc, cnp, stats


def build_layer0(P, C, Y, T, ncores, seg_per_batch, W, gams, lbp_d, lbrow_d, prm, keys, sel):
    if True:
        NCH = T // 64
        ZF = P.dram("ZF0", [8192, T], F32)
        ZT = P.dram("ZT0", [T, 4096], F32)
        KH = P.dram("KH", [T, 2048], BF16)
        VB = P.dram("VB", [T, 2048], BF16)
        QT = P.dram("QT", [2048, T], BF16)
        KT = P.dram("KT", [2048, T], BF16)
        DECd = P.dram("DEC", [128, 16 * NCH], F32)
        SLd = P.dram("SL", [128, 2064], F32)
        SLall = P.dram("SLall", [ncores * 128, 2064], F32)
        SId = P.dram("SI", [128, 2048], F32)
        SXd = P.dram("SX", [128, 128], F32)
        SXall = P.dram("SXall", [ncores * 128, 128], F32)
        XId = P.dram("XI", [128, 128], F32)
        Y2 = P.dram("Y2", [2048, T], F32)
        OT0 = P.dram("OT0", [4096, T], BF16)
        inproj_stage(P, C, Y, T, gams[0], W["ab_w_in"].full,
                     [(0, 2048, ZF, 0, AF.Silu), (2048, 2048, ZF, 2048, AF.Sigmoid),
                      (6144, 2048, ZF, 4096, AF.Silu), (8192, 2048, ZF, 6144, AF.Copy)],
                     [(2048, 2048, ZT, 0, AF.Sigmoid), (4096, 2048, ZT, 2048, AF.Copy)])
        hgrn2_pass1(P, C, T, ZF, ZT, lbp_d, lbrow_d, KH, VB, QT, KT, DECd, SLd)
        SXd3 = Buf(SXd.t.rearrange("p (m two) -> p m two", two=2), "SX3")
        XId3 = Buf(XId.t.rearrange("p (m two) -> p m two", two=2), "XI3")
        s5_scan(P, C, T, ZF, prm, None, SXd3, None)
        P.barrier()
        SXd.w = SXd3.w
        gather_states(P, SLd, SLall, ncores)
        gather_states(P, SXd, SXall, ncores)
        hgrn2_combine(P, C, SLall, sel, SId, ncores, seg_per_batch)
        s5_combine(P, C, T, prm, SXall, sel, XId, ncores, seg_per_batch)
        XId3.w = XId.w
        hgrn2_pass2(P, C, T, ZF, KH, VB, QT, KT, DECd, SId, OT0)
        s5_scan(P, C, T, ZF, prm, XId3, None, Y2)
        glu_stage(P, C, T, Y2, W["b_w_glu"].full, OT0, 2048)
        outproj_stage(P, C, Y, T, OT0, W["ab_w_out"].full, 4096)
        peer_stage(P, C, Y, T, gams[1], W["wq0"].full, keys[0], W["ut0"].full, W["v0"].full)


def build_layer1(P, C, Y, T, ncores, seg_per_batch, W, gams, rope_d, rc, keys, sel, gfin_d, mixer=True, ffn=True):
    if mixer:
        ZF1 = P.dram("ZF1", [16384, T], F32)
        ZT1 = P.dram("ZT1", [T, 12288], F32)
        KD = P.dram("KD", [T, 4096], BF16)
        VB1 = P.dram("VB1", [T, 8192], BF16)
        QR = P.dram("QR", [4096, T], BF16)
        KR = P.dram("KR", [4096, T], BF16)
        QD = P.dram("QD", [4096, T], BF16)
        RL2 = P.dram("RL", [128, 16384], F32)
        RLall = P.dram("RLall", [ncores * 128, 16384], F32)
        RI2 = P.dram("RI", [128, 16384], F32)
        RLd = Buf(RL2.t.rearrange("p (h d e) -> p h d e", h=16, d=2), "RL4")
        RId = Buf(RI2.t.rearrange("p (h d e) -> p h d e", h=16, d=2), "RI4")
        OT1 = P.dram("OT1", [8192, T], BF16)
        inproj_stage(P, C, Y, T, gams[2], W["c_w_in"].full,
                     [(0, 4096, ZF1, 0, AF.Copy), (4096, 4096, ZF1, 4096, AF.Copy), (16384, 8192, ZF1, 8192, AF.Silu)],
                     [(4096, 4096, ZT1, 0, AF.Copy), (8192, 8192, ZT1, 4096, AF.Copy)])
        ret_pass1(P, C, T, ZF1, ZT1, rope_d, rc, KD, VB1, QR, KR, QD, RLd)
        P.barrier()
        RL2.w = RLd.w
        gather_states(P, RL2, RLall, ncores)
        ret_combine(P, C, T, RLall, sel, RId, ncores, seg_per_batch)
        ret_pass2(P, C, T, ZF1, rc, KD, VB1, QR, KR, QD, RId, OT1)
        outproj_stage(P, C, Y, T, OT1, W["c_w_out"].full, 8192)
    if ffn:
        peer_stage(P, C, Y, T, gams[3], W["wq1"].full, keys[1], W["ut1"].full, W["v1"].full)
        final_norm_stage(P, C, Y, T, gfin_d)


def host_inputs(inp, T, ncores, seg_per_batch, cnp, parts=(0, 1), x_override=None):
    f = lambda a: np.ascontiguousarray(np.asarray(a, dtype=np.float32))
    x = f(inp["x"] if x_override is None else x_override).reshape(-1, 4096)
    gl = lambda g: f(np.asarray(g).reshape(32, 128).T)
    decT, qdec, kdec = make_ret_consts()
    common = {"consts": cnp, "decT": decT, "qdec": qdec, "kdec": kdec,
              "gam0": gl(inp["mix_norm_g"][0]), "gam1": gl(inp["ffn_norm_g"][0]),
              "gam2": gl(inp["mix_norm_g"][1]), "gam3": gl(inp["ffn_norm_g"][1]),
              "gfin": f(np.broadcast_to(np.asarray(inp["final_norm_g"])[None, :], (128, 4096)))}
    lbparam = np.asarray(inp["a_lb_param"], np.float32)
    common["lbp"] = f(lbparam.reshape(2, 16, 128).transpose(2, 0, 1))
    common["lbrow"] = f(np.broadcast_to(lbparam[None], (128, 2, 2048)))
    hp = s5_host_params(*(np.asarray(inp[k][0], np.float32) for k in
                          ("b_a_re", "b_a_im", "b_log_dt", "b_b_re", "b_b_im", "b_c_re", "b_c_im", "b_d")))
    common.update({"s5_" + k: v for k, v in hp.items()})
    for l in range(2):
        keys = np.asarray(inp["peer_sub_keys"][l], np.float32)
        common["keys%d" % l] = f(keys.reshape(16, 128, 128).transpose(2, 0, 1))
    wsrc = {k: inp[k][0] for k in ("ab_w_in", "b_w_glu", "ab_w_out", "c_w_in", "c_w_out") if k in inp}
    shards = {}
    for name, K, N in part_weights(parts):
        if name.startswith("wq"):
            Wm = np.asarray(inp["peer_w_q"][int(name[2])], np.float32)
        elif name.startswith("ut"):
            Wm = np.ascontiguousarray(np.asarray(inp["peer_u"][int(name[2])], np.float32).T)
        elif name.startswith("v") and name[1:].isdigit():
            Wm = np.asarray(inp["peer_v"][int(name[1])], np.float32)
        else:
            Wm = np.asarray(wsrc[name], np.float32)
        assert Wm.shape == (K, N), (name, Wm.shape)
        shards[name] = shard_rows(Wm, ncores)
        del Wm
    maps = []
    for r in range(ncores):
        m = dict(common)
        m["x"] = f(x[r * T:(r + 1) * T])
        seg = r % seg_per_batch
        m["rope"] = make_rope(seg * T, T)
        s_ = np.zeros((128, 8), np.float32)
        s_[:, r] = 1.0
        m["sel"] = s_
        for name, _, _ in part_weights(parts):
            m[name] = shards[name][r]
        maps.append(m)
    return maps


LAUNCH_W = {"A0": ("ab_w_in",), "B0": ("ab_w_in", "b_w_glu", "ab_w_out", "wq0", "ut0", "v0"),
            "A1": ("c_w_in",), "B1a": ("c_w_in", "c_w_out"), "B1b": ("wq1", "ut1", "v1")}


def build_launch(T, mode, ncores=8, spb=4):
    nc = bass.Bass("TRN2", target_bir_lowering=False)
    dt_in = lambda name, shape: nc.dram_tensor(name, list(shape), F32, kind="ExternalInput").ap()
    dt_out = lambda name, shape: nc.dram_tensor(name, list(shape), F32, kind="ExternalOutput").ap()
    x_in = dt_in("x", [T, 4096])
    cnp, cols = make_consts(T)
    consts_d = dt_in("consts", cnp.shape)
    NCH = T // 64
    with ExitStack() as st:
        P = Prog(nc, st)
        C = setup_consts(P)
        load_consts(P, C, consts_d, cols)
        W = {}
        for name, K, N in WEIGHTS:
            if name in LAUNCH_W[mode]:
                W[name] = Weight(P, name, K, N, 1)
                W[name].distribute(P)
        if mode in ("B0", "B1a", "B1b"):
            y = dt_out("y", [T, 4096])
            Y = Buf(y, "Y")
            for r0 in range(0, T, 256):
                P.dma("sp", y[r0:r0 + 256, :], x_in[r0:r0 + 256, :], w=[Y])
        else:
            Y = Buf(x_in, "Xin")
        if mode in ("B0", "B1a"):
            sel = P.sb([8], F32, "sel")
            P.dma("sp", sel[:], dt_in("sel", [128, 8]), w=[sel])
        if mode in ("A0", "B0"):
            gam0 = dt_in("gam0", [128, 32])
            lbp_d = dt_in("lbp", [128, 2, 16])
            lbrow_d = dt_in("lbrow", [128, 2, 2048])
            prm = {k: dt_in("s5_" + k, v) for k, v in S5_SHAPES.items()}
            ZF = P.dram("ZF0", [8192, T], F32)
            ZT = P.dram("ZT0", [T, 4096], F32)
            KH = P.dram("KH", [T, 2048], BF16)
            VB = P.dram("VB", [T, 2048], BF16)
            QT = P.dram("QT", [2048, T], BF16)
            KT = P.dram("KT", [2048, T], BF16)
            DECd = P.dram("DEC", [128, 16 * NCH], F32)
            if mode == "A0":
                SLd = Buf(dt_out("sl", [128, 2064]), "SL")
                SXd3 = Buf(dt_out("sx", [128, 64, 2]), "SX")
            else:
                SLd = P.dram("SL", [128, 2064], F32)
            inproj_stage(P, C, Y, T, gam0, W["ab_w_in"].full,
                         [(0, 2048, ZF, 0, AF.Silu), (2048, 2048, ZF, 2048, AF.Sigmoid),
                          (6144, 2048, ZF, 4096, AF.Silu), (8192, 2048, ZF, 6144, AF.Copy)],
                         [(2048, 2048, ZT, 0, AF.Sigmoid), (4096, 2048, ZT, 2048, AF.Copy)])
            hgrn2_pass1(P, C, T, ZF, ZT, lbp_d, lbrow_d, KH, VB, QT, KT, DECd, SLd)
            if mode == "A0":
                s5_scan(P, C, T, ZF, prm, None, SXd3, None)
            else:
                gam1 = dt_in("gam1", [128, 32])
                keys0 = dt_in("keys0", [128, 16, 128])
                SLall = Buf(dt_in("slall", [ncores * 128, 2064]), "SLall")
                SXall = Buf(dt_in("sxall", [ncores * 128, 128]), "SXall")
                SId = P.dram("SI", [128, 2048], F32)
                XId = P.dram("XI", [128, 128], F32)
                XId3 = Buf(XId.t.rearrange("p (m two) -> p m two", two=2), "XI3")
                Y2 = P.dram("Y2", [2048, T], F32)
                OT0 = P.dram("OT0", [4096, T], BF16)
                hgrn2_combine(P, C, SLall, sel, SId, ncores, spb)
                s5_combine(P, C, T, prm, SXall, sel, XId, ncores, spb)
                hgrn2_pass2(P, C, T, ZF, KH, VB, QT, KT, DECd, SId, OT0)
                s5_scan(P, C, T, ZF, prm, XId3, None, Y2)
                glu_stage(P, C, T, Y2, W["b_w_glu"].full, OT0, 2048)
                outproj_stage(P, C, Y, T, OT0, W["ab_w_out"].full, 4096)
                peer_stage(P, C, Y, T, gam1, W["wq0"].full, keys0, W["ut0"].full, W["v0"].full)
        if mode in ("A1", "B1a"):
            gam2 = dt_in("gam2", [128, 32])
            rope_d = dt_in("rope", [128, 4 * T])
            rc = {"decT": dt_in("decT", [128, 2048]), "qdec": dt_in("qdec", [128, 1024]), "kdec": dt_in("kdec", [128, 16])}
            ZF1 = P.dram("ZF1", [16384, T], F32)
            ZT1 = P.dram("ZT1", [T, 12288], F32)
            KD = P.dram("KD", [T, 4096], BF16)
            VB1 = P.dram("VB1", [T, 8192], BF16)
            QR = P.dram("QR", [4096, T], BF16)
            KR = P.dram("KR", [4096, T], BF16)
            QD = P.dram("QD", [4096, T], BF16)
            if mode == "A1":
                rl_ap = dt_out("rl", [128, 16384])
            else:
                rl_ap = P.dram("RL", [128, 16384], F32).t
            RLd = Buf(rl_ap.rearrange("p (h d e) -> p h d e", h=16, d=2), "RL4")
            inproj_stage(P, C, Y, T, gam2, W["c_w_in"].full,
                         [(0, 4096, ZF1, 0, AF.Copy), (4096, 4096, ZF1, 4096, AF.Copy), (16384, 8192, ZF1, 8192, AF.Silu)],
                         [(4096, 4096, ZT1, 0, AF.Copy), (8192, 8192, ZT1, 4096, AF.Copy)])
            ret_pass1(P, C, T, ZF1, ZT1, rope_d, rc, KD, VB1, QR, KR, QD, RLd)
            if mode == "B1a":
                RLall = Buf(dt_in("rlall", [ncores * 128, 16384]), "RLall")
                RI2 = P.dram("RI", [128, 16384], F32)
                RId = Buf(RI2.t.rearrange("p (h d e) -> p h d e", h=16, d=2), "RI4")
                OT1 = P.dram("OT1", [8192, T], BF16)
                ret_combine(P, C, T, RLall, sel, RId, ncores, spb)
                ret_pass2(P, C, T, ZF1, rc, KD, VB1, QR, KR, QD, RId, OT1)
                outproj_stage(P, C, Y, T, OT1, W["c_w_out"].full, 8192)
        if mode == "B1b":
            gam3 = dt_in("gam3", [128, 32])
            keys1 = dt_in("keys1", [128, 16, 128])
            gfin_d = dt_in("gfin", [128, 4096])
            peer_stage(P, C, Y, T, gam3, W["wq1"].full, keys1, W["ut1"].full, W["v1"].full)
            final_norm_stage(P, C, Y, T, gfin_d)
        P.barrier()
        stats = P.emit()
    return nc, cnp, stats


def launch_inputs(inp, T, mode, cnp, xcur, extra, ncores=8, spb=4):
    f = lambda a: np.ascontiguousarray(np.asarray(a, dtype=np.float32))
    gl = lambda g: f(np.asarray(g).reshape(32, 128).T)
    x = xcur.reshape(-1, 4096)
    common = {"consts": cnp}
    if mode in ("A0", "B0"):
        common["gam0"] = gl(inp["mix_norm_g"][0])
        lbparam = np.asarray(inp["a_lb_param"], np.float32)
        common["lbp"] = f(lbparam.reshape(2, 16, 128).transpose(2, 0, 1))
        common["lbrow"] = f(np.broadcast_to(lbparam[None], (128, 2, 2048)))
        hp = s5_host_params(*(np.asarray(inp[k][0], np.float32) for k in
                              ("b_a_re", "b_a_im", "b_log_dt", "b_b_re", "b_b_im", "b_c_re", "b_c_im", "b_d")))
        common.update({"s5_" + k: v for k, v in hp.items()})
        common["ab_w_in"] = f(inp["ab_w_in"][0])
    if mode == "B0":
        common["gam1"] = gl(inp["ffn_norm_g"][0])
        common["keys0"] = f(np.asarray(inp["peer_sub_keys"][0], np.float32).reshape(16, 128, 128).transpose(2, 0, 1))
        common["b_w_glu"] = f(inp["b_w_glu"][0])
        common["ab_w_out"] = f(inp["ab_w_out"][0])
        common["wq0"] = f(inp["peer_w_q"][0])
        common["ut0"] = f(np.asarray(inp["peer_u"][0], np.float32).T)
        common["v0"] = f(inp["peer_v"][0])
        common["slall"] = extra["slall"]
        common["sxall"] = extra["sxall"]
    if mode in ("A1", "B1a"):
        decT, qdec, kdec = make_ret_consts()
        common.update({"gam2": gl(inp["mix_norm_g"][1]), "decT": decT, "qdec": qdec, "kdec": kdec,
                       "c_w_in": f(inp["c_w_in"][0])})
    if mode == "B1a":
        common["c_w_out"] = f(inp["c_w_out"][0])
        common["rlall"] = extra["rlall"]
    if mode == "B1b":
        common["gam3"] = gl(inp["ffn_norm_g"][1])
        common["keys1"] = f(np.asarray(inp["peer_sub_keys"][1], np.float32).reshape(16, 128, 128).transpose(2, 0, 1))
        common["gfin"] = f(np.broadcast_to(np.asarray(inp["final_norm_g"])[None, :], (128, 4096)))
        common["wq1"] = f(inp["peer_w_q"][1])
        common["ut1"] = f(np.asarray(inp["peer_u"][1], np.float32).T)
        common["v1"] = f(inp["peer_v"][1])
    maps = []
    for r in range(ncores):
        m = dict(common)
        m["x"] = f(x[r * T:(r + 1) * T])
        if mode in ("A1", "B1a"):
            m["rope"] = make_rope((r % spb) * T, T)
        if mode in ("B0", "B1a"):
            s_ = np.zeros((128, 8), np.float32)
            s_[:, r] = 1.0
            m["sel"] = s_
        maps.append(m)
    return maps


_CACHE = {}


def _run(mode, inputs, xcur, extra, T=2048):
    if mode not in _CACHE:
        _CACHE[mode] = build_launch(T, mode)
    nc, cnp, _ = _CACHE[mode]
    maps = launch_inputs(inputs, T, mode, cnp, xcur, extra)
    res = run_bass_kernel_spmd(nc, maps, core_ids=list(range(8)))
    del maps
    return res.results


def kernel(**inputs):
    x = np.ascontiguousarray(np.asarray(inputs["x"], np.float32)).reshape(-1, 4096)
    cat = lambda rs, k: np.ascontiguousarray(np.concatenate([np.asarray(r[k], np.float32).reshape(128, -1) for r in rs], axis=0))
    rs = _run("A0", inputs, x, None)
    extra = {"slall": cat(rs, "sl"), "sxall": cat(rs, "sx")}
    rs = _run("B0", inputs, x, extra)
    x = np.concatenate([np.asarray(r["y"], np.float32) for r in rs], axis=0)
    rs = _run("A1", inputs, x, None)
    extra = {"rlall": cat(rs, "rl")}
    rs = _run("B1a", inputs, x, extra)
    x = np.concatenate([np.asarray(r["y"], np.float32) for r in rs], axis=0)
    rs = _run("B1b", inputs, x, None)
    x = np.concatenate([np.asarray(r["y"], np.float32) for r in rs], axis=0)
    return x.reshape(2, 8192, 4096)
```

```python
from contextlib import ExitStack
import math
import numpy as np
import concourse.bass as bass
import concourse.mybir as mybir
from concourse.bass_utils import run_bass_kernel_spmd

F32 = mybir.dt.float32
BF16 = mybir.dt.bfloat16
ALU = mybir.AluOpType
AF = mybir.ActivationFunctionType
AX = mybir.AxisListType

D = 4096
EPS = 1e-6
N_DMA_SEMS = 20
SB_WORDS = 53000
NEG = -1.0e30


class Buf:
    def __init__(self, ap, name):
        self.t = ap
        self.name = name
        self.w = None
        self.r = []

    def __getitem__(self, idx):
        return self.t[idx]


class Op:
    __slots__ = ("eng", "fn", "deps", "kind", "sem", "val", "need_inc", "prev_same_sem")

    def __init__(self, eng, fn, kind):
        self.eng = eng
        self.fn = fn
        self.kind = kind
        self.deps = []
        self.sem = None
        self.val = None
        self.need_inc = False
        self.prev_same_sem = None


class Prog:
    ENGS = ("pe", "act", "dve", "pool", "sp")

    def __init__(self, nc, stack):
        self.nc = nc
        self.stack = stack
        self.ops = []
        self.eobj = {"pe": nc.tensor, "act": nc.scalar, "dve": nc.vector,
                     "pool": nc.gpsimd, "sp": nc.sync}
        self.esets = [{e: stack.enter_context(nc.semaphore("s%d_%s" % (i, e))) for e in self.ENGS[:4]}
                      for i in range(4)]
        self.nds = {"sp": 76, "pool": 8}
        self.dsem = {q: [stack.enter_context(nc.semaphore("d_%s%d" % (q, i)))
                         for i in range(n)] for q, n in self.nds.items()}
        self.ccsem = stack.enter_context(nc.semaphore("s_cc"))
        self.cccount = 0
        self.dcount = {q: 0 for q in self.dsem}
        self.dlast = {}
        self.last = {}
        self.pending_dma = []
        self.big = stack.enter_context(nc.sbuf_tensor("arena", [128, SB_WORDS], F32))
        self.off = 0
        self.nb = 0
        self.ps = [Buf(stack.enter_context(nc.psum_tensor("psb%d" % i, [128, 512], F32)), "ps%d" % i)
                   for i in range(8)]

    def sb(self, free, dtype, name=None):
        n = int(np.prod(free))
        words = (n * (2 if dtype == BF16 else 4) + 3) // 4
        words = (words + 7) // 8 * 8
        assert self.off + words <= SB_WORDS, ("SBUF arena overflow", self.off, words)
        ap = self.big[:, self.off:self.off + words]
        self.off += words
        if dtype != F32:
            ap = ap.bitcast(dtype)
        ap = ap[:, 0:n]
        if len(free) == 2:
            ap = ap.rearrange("p (a b) -> p a b", a=free[0])
        elif len(free) == 3:
            ap = ap.rearrange("p (a b c) -> p a b c", a=free[0], b=free[1])
        self.nb += 1
        return Buf(ap, name or "b%d" % self.nb)

    def mark(self):
        return self.off

    def release(self, m):
        self.barrier()
        self.off = m

    def dram(self, name, shape, dtype):
        t = self.nc.dram_tensor(name, list(shape), dtype)
        b = Buf(t.ap(), name)
        b.handle = t
        return b

    def _track(self, op, r, w):
        deps = []
        for b in r:
            if b.w is not None:
                deps.append(b.w)
        for b in w:
            if b.w is not None:
                deps.append(b.w)
            for x in b.r:
                if x.eng == op.eng and x.kind == "c" and op.kind == "c":
                    continue
                deps.append(x)
        op.deps = [d for d in deps if not (d.eng == "pe" and op.eng == "pe"
                                           and d.kind == "c" and op.kind == "c")]
        for d in op.deps:
            d.need_inc = True
        for b in r:
            if op.kind == "c":
                b.r = [x for x in b.r if not (x.kind == "c" and x.eng == op.eng)]
            b.r.append(op)
        for b in w:
            b.w = op
            b.r = []

    def op(self, eng, fn, r=(), w=()):
        o = Op(eng, fn, "c")
        self._track(o, r, w)
        self.ops.append(o)
        self.last[eng] = o
        return o

    def dma(self, q, out, in_, r=(), w=()):
        o = Op(q, (out, in_), "d")
        k = self.dcount[q]
        self.dcount[q] += 1
        slot = k % self.nds[q]
        o.sem = self.dsem[q][slot]
        o.val = 16 * (k // self.nds[q] + 1)
        o.prev_same_sem = self.dlast.get((q, slot))
        self.dlast[(q, slot)] = o
        o.need_inc = True
        self._track(o, r, w)
        self.ops.append(o)
        self.pending_dma.append(o)
        return o

    def collective(self, kind, in_ap, out_ap, groups, r=(), w=()):
        o = Op("pool", (kind, in_ap, out_ap, groups), "cc")
        self.cccount += 1
        o.sem = self.ccsem
        o.val = self.cccount
        o.need_inc = True
        self._track(o, r, w)
        self.ops.append(o)
        self.pending_dma.append(o)
        return o

    def barrier(self):
        srcs = list(self.last.values()) + self.pending_dma
        for s in srcs:
            s.need_inc = True
        for e in self.ENGS:
            o = Op(e, None, "b")
            o.deps = list(srcs)
            self.ops.append(o)
        self.pending_dma = []

    def emit(self, final_wait_ops=()):
        cnt = {e: 0 for e in self.ENGS}
        epoch = 0
        tot = {e: 0 for e in self.ENGS}
        prev_b = False
        for o in self.ops:
            if o.kind == "b":
                prev_b = True
                continue
            if prev_b:
                prev_b = False
                if max(cnt.values()) > 14000 and epoch + 1 < len(self.esets):
                    epoch += 1
                    cnt = {e: 0 for e in self.ENGS}
            if o.kind == "c" and o.need_inc:
                cnt[o.eng] += 1
                tot[o.eng] += 1
                o.sem = self.esets[epoch][o.eng]
                o.val = cnt[o.eng]
        cnt = tot
        seen = {e: {} for e in self.ENGS}
        nwait = 0
        for o in self.ops:
            e = self.eobj[o.eng]
            sn = seen[o.eng]
            deps = list(o.deps)
            if o.kind == "d" and o.prev_same_sem is not None:
                deps.append(o.prev_same_sem)
            need = {}
            for d in deps:
                key = id(d.sem)
                if sn.get(key, 0) >= d.val:
                    continue
                if key not in need or need[key][1] < d.val:
                    need[key] = (d.sem, d.val)
            for key, (sem, val) in need.items():
                e.wait_ge(sem, val)
                sn[key] = val
                nwait += 1
            if o.kind == "d":
                out, in_ = o.fn
                e.dma_start(out=out, in_=in_).then_inc(o.sem, 16)
            elif o.kind == "cc":
                kind, in_ap, out_ap, groups = o.fn
                e.collective_compute(kind, ALU.bypass, replica_groups=groups,
                                     ins=[in_ap], outs=[out_ap]).then_inc(o.sem)
            elif o.kind == "c":
                ins = o.fn(e)
                if o.need_inc:
                    ins.then_inc(o.sem, 1)
        for o in final_wait_ops:
            e = self.eobj[o.eng]
            if seen[o.eng].get(id(o.sem), 0) < o.val:
                e.wait_ge(o.sem, o.val)
                seen[o.eng][id(o.sem)] = o.val
        return dict(n_ops=len(self.ops), n_wait=nwait, cnt=cnt, dma=dict(self.dcount), cc=self.cccount)


def piece_rows(K, N, ncores):
    kp = K
    while kp * N * 2 > (16 << 20) and kp % 2 == 0 and (kp // 2) % ncores == 0 and (kp // 2) >= ncores:
        kp //= 2
    return kp


def shard_rows(W, ncores):
    K, N = W.shape
    if ncores == 1:
        return [np.ascontiguousarray(W)]
    kp = piece_rows(K, N, ncores)
    npc = K // kp
    sub = kp // ncores
    Wr = W.reshape(npc, ncores, sub, N)
    return [np.ascontiguousarray(Wr[:, r].reshape(npc * sub, N)) for r in range(ncores)]


def tile_w(W, NS):
    K, N = W.shape
    KT = K // 128
    return np.ascontiguousarray(W.reshape(KT, 128, N // NS, NS).transpose(2, 1, 0, 3).reshape((N // NS) * 128, KT * NS))


class Weight:
    def __init__(self, P, name, K, N, ncores=1, NS=None):
        self.K, self.N, self.name, self.NS = K, N, name, NS
        nc = P.nc
        if NS is None:
            self.shape = [K, N]
        else:
            self.shape = [(N // NS) * 128, (K // 128) * NS]
        self.src = nc.dram_tensor(name, self.shape, F32, kind="ExternalInput").ap()
        self.full = P.dram(name + "_bf", self.shape, BF16)
        self.full.NS = NS

    def distribute(self, P):
        rows, cols = self.shape
        step = max(1, min(rows, (4 << 20) // (4 * cols)))
        for r0 in range(0, rows, step):
            r1 = min(rows, r0 + step)
            P.dma("pool", self.full[r0:r1, :], self.src[r0:r1, :], w=[self.full])


def load_wslab(P, dst, W, n0, n1, KT, r_extra=()):
    NS = W.NS
    assert n0 % NS == 0 and n1 - n0 == NS, (n0, n1, NS)
    s_ = n0 // NS
    src = W.t[s_ * 128:(s_ + 1) * 128, :].rearrange("p (kt n) -> p kt n", kt=KT)
    h = KT // 2
    P.dma("sp", dst[:, 0:h, 0:NS], src[:, 0:h, :], r=[W], w=[dst])
    P.dma("sp", dst[:, h:KT, 0:NS], src[:, h:KT, :], r=[W], w=[dst])


def norm_to_T(P, C, Y, r0, gam_sb, xt, hn, hnT, col0, acc=None, stat=None):
    P.dma("sp", xt[:], Y[r0:r0 + 128, :], r=[Y], w=[xt])
    ss, rs = stat
    P.op("dve", lambda e: e.memset(ss[:], 0.0), w=[ss])
    P.op("act", lambda e: e.activation(out=hn[:], in_=xt[:], func=AF.Square, accum_out=ss[:]),
         r=[xt, ss], w=[hn, ss])
    P.op("act", lambda e: e.activation(out=rs[:], in_=ss[:], func=AF.Sqrt, scale=1.0 / D, bias=C["eps"][:]),
         r=[ss, C["eps"]], w=[rs])
    P.op("dve", lambda e: e.reciprocal(out=rs[:], in_=rs[:]), r=[rs], w=[rs])
    if acc is not None:
        P.op("pool", lambda e: e.tensor_copy(out=acc[:], in_=xt[:]), r=[xt], w=[acc])
    P.op("dve", lambda e: e.tensor_scalar(out=hn[:], in0=xt[:], scalar1=rs[:], scalar2=None, op0=ALU.mult),
         r=[xt, rs], w=[hn])
    ident = C["ident_bf"]
    for kb in range(4):
        pb = P.ps[kb % 2]
        pv = pb.t.bitcast(BF16).rearrange("p (a b) -> p a b", a=8)
        for k in range(8):
            kt = kb * 8 + k
            P.op("pe", lambda e, kt=kt, k=k, pv=pv: e.transpose(pv[:, k, :], hn[:, kt * 128:(kt + 1) * 128], ident[:]),
                 r=[hn, ident], w=[pb])
        for k in range(8):
            kt = kb * 8 + k
            P.op("act", lambda e, kt=kt, k=k, pv=pv: e.activation(
                out=hnT[:, kt, col0:col0 + 128], in_=pv[:, k, :], func=AF.Copy, scale=gam_sb[:, kt:kt + 1]),
                r=[pb, gam_sb], w=[hnT])


def setup_consts(P):
    C = {}
    idf = P.sb([128], F32, "idf")
    P.op("pool", lambda e: e.memset(idf[:], 0.0), w=[idf])
    P.op("pool", lambda e: e.affine_select(out=idf[:], in_=idf[:], pattern=[[-1, 128]],
                                           compare_op=ALU.not_equal, fill=1.0, base=0, channel_multiplier=1),
         r=[idf], w=[idf])
    ib = P.sb([128], BF16, "ident_bf")
    P.op("dve", lambda e: e.tensor_copy(out=ib[:], in_=idf[:]), r=[idf], w=[ib])
    eps = P.sb([1], F32, "eps")
    P.op("dve", lambda e: e.memset(eps[:], EPS), w=[eps])
    hp = P.sb([1], F32, "halfpi")
    P.op("dve", lambda e: e.memset(hp[:], float(np.pi / 2)), w=[hp])
    C["halfpi"] = hp
    C["ident_f"] = idf
    C["ident_bf"] = ib
    C["eps"] = eps
    return C


def peer_stage(P, C, Y, T, gam_d, Wq, keys_d, UT, V):
    TG = min(T, 256)
    NT = TG // 128
    m0 = P.mark()
    hnT = P.sb([32, TG], BF16, "hnT")
    acc = [P.sb([4096], F32, "acc%d" % i) for i in range(NT)]
    wA = P.sb([32, 512], BF16, "wA")
    wB = P.sb([4, 4096], BF16, "wB")
    qT = P.sb([16, TG], BF16, "qT")
    keysT = P.sb([16, 128], BF16, "keysT")
    gam = P.sb([32], F32, "gam")
    sc = [P.sb([16, 128], F32, "sc%d" % i) for i in range(NT)]
    tau = [P.sb([8], F32, "tau%d" % i) for i in range(NT)]
    nbias = [P.sb([8], F32, "nb%d" % i) for i in range(NT)]
    a16 = P.sb([16], F32, "a16")
    b16 = P.sb([16], F32, "b16")
    c16 = P.sb([16], F32, "c16")
    j16 = P.sb([16], F32, "j16")
    tmp128 = P.sb([128], F32, "tmp128")
    cand = P.sb([16, 16], F32, "cand")
    cand2 = P.sb([256], F32, "cand2")
    negm = P.sb([1], F32, "negm")
    zz = P.sb([1], F32, "zz")
    ss = P.sb([1], F32, "ss")
    rs = P.sb([1], F32, "rs")
    m1 = P.mark()
    xt = P.sb([4096], F32, "xt")
    hn = P.sb([4096], BF16, "hn")
    P.off = m1
    S = [P.sb([8, 2, 128], F32, "S%d" % i) for i in range(2)]
    E = [P.sb([8, 256], F32, "E%d" % i) for i in range(2)]
    hidg = [P.sb([512], F32, "hidg%d" % i) for i in range(2)]
    G = [P.sb([256], F32, "G%d" % i) for i in range(2)]
    Wt = [P.sb([256], BF16, "W%d" % i) for i in range(2)]
    WT = [P.sb([2, 128], BF16, "WT%d" % i) for i in range(2 * NT)]

    P.dma("pool", keysT[:], keys_d, w=[keysT])
    P.dma("sp", gam[:], gam_d, w=[gam])
    ident = C["ident_bf"]
    psH = [P.ps[0], P.ps[1]]
    psT = P.ps[2]
    psT_v = psT.t.bitcast(BF16)[:, 0:256].rearrange("p (a b) -> p a b", a=2)
    psO = [[P.ps[3], P.ps[4]], [P.ps[5], P.ps[6]]]

    for g0 in range(0, T, TG):
        for tt in range(NT):
            norm_to_T(P, C, Y, g0 + tt * 128, gam, xt, hn, hnT, tt * 128, acc=acc[tt], stat=(ss, rs))
        for s in range(4):
            load_wslab(P, wA, Wq, s * 512, (s + 1) * 512, 32)
            for ft in range(4):
                pb = P.ps[3 + (s * 4 + ft) % 2]
                for kt in range(32):
                    P.op("pe", lambda e, kt=kt, ft=ft, pb=pb: e.matmul(
                        pb[:, 0:TG], wA[:, kt, ft * 128:(ft + 1) * 128], hnT[:, kt, :],
                        start=(kt == 0), stop=(kt == 31)), r=[wA, hnT], w=[pb])
                P.op("act", lambda e, s=s, ft=ft, pb=pb: e.activation(
                    out=qT[:, s * 4 + ft, :], in_=pb[:, 0:TG], func=AF.Copy), r=[pb], w=[qT])
        for tt in range(NT):
            for q4 in range(4):
                pb = P.ps[5 + q4 % 2]
                for k in range(4):
                    hp = q4 * 4 + k
                    P.op("pe", lambda e, hp=hp, k=k, pb=pb, tt=tt: e.matmul(
                        pb[:, k * 128:(k + 1) * 128], qT[:, hp, tt * 128:(tt + 1) * 128], keysT[:, hp, :],
                        start=True, stop=True), r=[qT, keysT], w=[pb])
                P.op("dve", lambda e, q4=q4, pb=pb, tt=tt: e.tensor_copy(
                    out=sc[tt][:, q4 * 4:(q4 + 1) * 4, :], in_=pb[:].rearrange("p (a b) -> p a b", a=4)),
                    r=[pb], w=[sc[tt]])
            for h in range(8):
                s1 = sc[tt][:, 2 * h, :]
                s2 = sc[tt][:, 2 * h + 1, :]
                for src, dst in ((s1, a16), (s2, b16)):
                    P.op("dve", lambda e, src=src, dst=dst: e.max(out=dst[:, 0:8], in_=src), r=[sc[tt]], w=[dst])
                    P.op("dve", lambda e, src=src, dst=dst: e.match_replace(
                        out=tmp128[:], in_to_replace=dst[:, 0:8], in_values=src, imm_value=NEG),
                        r=[sc[tt], dst], w=[tmp128])
                    P.op("dve", lambda e, dst=dst: e.max(out=dst[:, 8:16], in_=tmp128[:]), r=[tmp128], w=[dst])
                P.op("dve", lambda e: e.tensor_tensor(
                    out=cand[:], in0=a16[:].unsqueeze(2).to_broadcast([128, 16, 16]),
                    in1=b16[:].unsqueeze(1).to_broadcast([128, 16, 16]), op=ALU.add), r=[a16, b16], w=[cand])
                cf = cand[:].rearrange("p a b -> p (a b)")
                P.op("dve", lambda e, cf=cf: e.max(out=c16[:, 0:8], in_=cf), r=[cand], w=[c16])
                P.op("dve", lambda e, cf=cf: e.match_replace(out=cand2[:], in_to_replace=c16[:, 0:8],
                                                             in_values=cf, imm_value=NEG), r=[cand, c16], w=[cand2])
                P.op("dve", lambda e: e.max(out=c16[:, 8:16], in_=cand2[:]), r=[cand2], w=[c16])
                P.op("dve", lambda e, h=h, tt=tt: e.tensor_copy(out=tau[tt][:, h:h + 1], in_=c16[:, 15:16]),
                     r=[c16], w=[tau[tt]])
                P.op("dve", lambda e: e.tensor_scalar(out=negm[:], in0=c16[:, 0:1], scalar1=-1.0, scalar2=None,
                                                      op0=ALU.mult), r=[c16], w=[negm])
                P.op("dve", lambda e: e.memset(zz[:], 0.0), w=[zz])
                P.op("act", lambda e: e.activation(out=j16[:], in_=c16[:], func=AF.Exp, bias=negm[:], scale=1.0,
                                                   accum_out=zz[:]), r=[c16, negm, zz], w=[j16, zz])
                P.op("act", lambda e: e.activation(out=zz[:], in_=zz[:], func=AF.Ln), r=[zz], w=[zz])
                P.op("dve", lambda e, h=h, tt=tt: e.tensor_tensor(out=nbias[tt][:, h:h + 1], in0=negm[:], in1=zz[:],
                                                                  op=ALU.subtract), r=[negm, zz], w=[nbias[tt]])
        P.barrier()
        for c in range(32):
            e0 = c * 512
            load_wslab(P, wA, UT, e0, e0 + 512, 32)
            Vv = V.t.rearrange("(et p) d -> p et d", p=128)
            for et in range(4):
                P.dma("sp", wB[:, et, :], Vv[:, c * 4 + et, :], r=[V], w=[wB])
            for tt in range(NT):
                ph = psH[(c * NT + tt) % 2]
                hg = hidg[(c * NT + tt) % 2]
                for kt in range(32):
                    P.op("pe", lambda e, kt=kt, tt=tt, ph=ph: e.matmul(
                        ph[:], hnT[:, kt, tt * 128:(tt + 1) * 128], wA[:, kt, :],
                        start=(kt == 0), stop=(kt == 31)), r=[hnT, wA], w=[ph])
                P.op("act", lambda e, ph=ph, hg=hg: e.activation(out=hg[:], in_=ph[:], func=AF.Gelu), r=[ph], w=[hg])
            for tt in range(NT):
                hg = hidg[(c * NT + tt) % 2]
                scv = sc[tt][:].rearrange("p (h two) n -> p h two n", two=2)
                for hf in range(2):
                    i0 = c * 4 + hf * 2
                    Sb, Eb, Gb, Wb, WTb = S[hf], E[hf], G[hf], Wt[hf], WT[tt * 2 + hf]
                    P.op("dve", lambda e, Sb=Sb, i0=i0, scv=scv: e.tensor_tensor(
                        out=Sb[:],
                        in0=scv[:, :, 0, i0:i0 + 2].unsqueeze(3).to_broadcast([128, 8, 2, 128]),
                        in1=scv[:, :, 1, :].unsqueeze(2).to_broadcast([128, 8, 2, 128]),
                        op=ALU.add), r=[sc[tt]], w=[Sb])
                    for h in range(8):
                        sv = Sb[:, h, :, :].rearrange("p a b -> p (a b)")
                        P.op("act", lambda e, sv=sv, Eb=Eb, h=h, tt=tt: e.activation(
                            out=Eb[:, h, :], in_=sv, func=AF.Exp, bias=nbias[tt][:, h:h + 1], scale=1.0),
                            r=[Sb, nbias[tt]], w=[Eb])
                        P.op("dve", lambda e, sv=sv, Eb=Eb, h=h, tt=tt: e.scalar_tensor_tensor(
                            out=Eb[:, h, :], in0=sv, scalar=tau[tt][:, h:h + 1], in1=Eb[:, h, :],
                            op0=ALU.is_ge, op1=ALU.mult), r=[Sb, Eb, tau[tt]], w=[Eb])
                    P.op("pool", lambda e, Eb=Eb: e.tensor_tensor(
                        out=Eb[:, 0:4, :], in0=Eb[:, 0:4, :], in1=Eb[:, 4:8, :], op=ALU.add), r=[Eb], w=[Eb])
                    P.op("pool", lambda e, Eb=Eb: e.tensor_tensor(
                        out=Eb[:, 0:2, :], in0=Eb[:, 0:2, :], in1=Eb[:, 2:4, :], op=ALU.add), r=[Eb], w=[Eb])
                    P.op("pool", lambda e, Eb=Eb, Gb=Gb: e.tensor_tensor(
                        out=Gb[:], in0=Eb[:, 0, :], in1=Eb[:, 1, :], op=ALU.add), r=[Eb], w=[Gb])
                    P.op("dve", lambda e, Gb=Gb, Wb=Wb, hg=hg, hf=hf: e.tensor_tensor(
                        out=Wb[:], in0=Gb[:], in1=hg[:, hf * 256:(hf + 1) * 256], op=ALU.mult), r=[Gb, hg], w=[Wb])
                    for k in range(2):
                        P.op("pe", lambda e, k=k, Wb=Wb: e.transpose(psT_v[:, k, :], Wb[:, k * 128:(k + 1) * 128],
                                                                    ident[:]), r=[Wb, ident], w=[psT])
                    P.op("act", lambda e, WTb=WTb: e.activation(out=WTb[:], in_=psT_v, func=AF.Copy), r=[psT], w=[WTb])
            for tt in range(NT):
                for dq in range(4):
                    pair = psO[dq % 2]
                    for dc in range(2):
                        d0 = (dq * 2 + dc) * 512
                        for et in range(4):
                            wt_ = WT[tt * 2 + et // 2]
                            P.op("pe", lambda e, et=et, d0=d0, pb=pair[dc], wt_=wt_: e.matmul(
                                pb[:], wt_[:, et % 2, :], wB[:, et, d0:d0 + 512],
                                start=(et == 0), stop=(et == 3)), r=[wt_, wB], w=[pair[dc]])
                    for dc in range(2):
                        d0 = (dq * 2 + dc) * 512
                        P.op("dve", lambda e, d0=d0, pb=pair[dc], tt=tt: e.tensor_tensor(
                            out=acc[tt][:, d0:d0 + 512], in0=acc[tt][:, d0:d0 + 512], in1=pb[:], op=ALU.add),
                            r=[acc[tt], pair[dc]], w=[acc[tt]])
        for tt in range(NT):
            P.dma("sp", Y[g0 + tt * 128:g0 + (tt + 1) * 128, :], acc[tt][:], r=[acc[tt]], w=[Y])
        P.barrier()
    P.release(m0)


def inproj_stage(P, C, Y, T, gam_d, W, fm_specs, tm_specs):
    TB = min(T, 1024)
    m0 = P.mark()
    hnT = P.sb([32, TB], BF16, "ip_hnT")
    gam = P.sb([32], F32, "ip_gam")
    xt = P.sb([4096], F32, "ip_xt")
    hn = P.sb([4096], BF16, "ip_hn")
    ss = P.sb([1], F32, "ip_ss")
    rs = P.sb([1], F32, "ip_rs")
    slab = [P.sb([32, 512], BF16, "ip_slab%d" % i) for i in range(2)]
    stg = [P.sb([512], F32, "ip_stg%d" % i) for i in range(3)]
    P.dma("sp", gam[:], gam_d, w=[gam])
    banks = [P.ps[3], P.ps[4], P.ps[5], P.ps[6]]
    cnt = [0, 0, 0]
    for t0 in range(0, T, TB):
        for tt in range(TB // 128):
            norm_to_T(P, C, Y, t0 + tt * 128, gam, xt, hn, hnT, tt * 128, stat=(ss, rs))
        for (col0, ncols, dst, row0, func) in fm_specs:
            for s0 in range(0, ncols, 512):
                sl = slab[cnt[0] % 2]
                cnt[0] += 1
                load_wslab(P, sl, W, col0 + s0, col0 + s0 + 512, 32)
                for nt in range(4):
                    for tc in range(0, TB, 512):
                        tw = min(512, TB - tc)
                        pb = banks[cnt[1] % 4]
                        cnt[1] += 1
                        for kt in range(32):
                            P.op("pe", lambda e, kt=kt, nt=nt, pb=pb, sl=sl, tc=tc, tw=tw: e.matmul(
                                pb[:, 0:tw], sl[:, kt, nt * 128:(nt + 1) * 128], hnT[:, kt, tc:tc + tw],
                                start=(kt == 0), stop=(kt == 31)), r=[sl, hnT], w=[pb])
                        sg = stg[cnt[2] % 3]
                        cnt[2] += 1
                        P.op("act", lambda e, pb=pb, sg=sg, tw=tw, func=func: e.activation(
                            out=sg[:, 0:tw], in_=pb[:, 0:tw], func=func), r=[pb], w=[sg])
                        rr = row0 + s0 + nt * 128
                        P.dma("sp", dst[rr:rr + 128, t0 + tc:t0 + tc + tw], sg[:, 0:tw], r=[sg], w=[dst])
        for (col0, ncols, dst, dcol0, func) in tm_specs:
            for s0 in range(0, ncols, 512):
                sl = slab[cnt[0] % 2]
                cnt[0] += 1
                load_wslab(P, sl, W, col0 + s0, col0 + s0 + 512, 32)
                for tt in range(TB // 128):
                    pb = banks[cnt[1] % 4]
                    cnt[1] += 1
                    for kt in range(32):
                        P.op("pe", lambda e, kt=kt, tt=tt, pb=pb, sl=sl: e.matmul(
                            pb[:], hnT[:, kt, tt * 128:(tt + 1) * 128], sl[:, kt, :],
                            start=(kt == 0), stop=(kt == 31)), r=[sl, hnT], w=[pb])
                    sg = stg[cnt[2] % 3]
                    cnt[2] += 1
                    P.op("act", lambda e, pb=pb, sg=sg, func=func: e.activation(
                        out=sg[:], in_=pb[:], func=func), r=[pb], w=[sg])
                    r0 = t0 + tt * 128
                    P.dma("sp", dst[r0:r0 + 128, dcol0 + s0:dcol0 + s0 + 512], sg[:], r=[sg], w=[dst])
    P.release(m0)


def outproj_stage(P, C, Y, T, OT, W, K):
    KT = K // 128
    TB = min(T, 512 if KT <= 32 else 256)
    NS = 512 if KT <= 32 else 256
    m0 = P.mark()
    oT = P.sb([KT, TB], BF16, "op_oT")
    yt = [P.sb([4096], F32, "op_y%d" % i) for i in range(TB // 128)]
    slab = [P.sb([KT, NS], BF16, "op_slab%d" % i) for i in range(2)]
    banks = [P.ps[3], P.ps[4], P.ps[5], P.ps[6]]
    OTv = OT.t.rearrange("(kt p) t -> p kt t", p=128)
    cnt = [0, 0]
    for t0 in range(0, T, TB):
        for k0 in range(0, KT, 8):
            P.dma("sp", oT[:, k0:k0 + 8, :], OTv[:, k0:k0 + 8, t0:t0 + TB], r=[OT], w=[oT])
        for tt in range(TB // 128):
            P.dma("sp", yt[tt][:], Y[t0 + tt * 128:t0 + (tt + 1) * 128, :], r=[Y], w=[yt[tt]])
        for s0 in range(0, 4096, NS):
            sl = slab[cnt[0] % 2]
            cnt[0] += 1
            load_wslab(P, sl, W, s0, s0 + NS, KT)
            for tt in range(TB // 128):
                pb = banks[cnt[1] % 4]
                cnt[1] += 1
                for kt in range(KT):
                    P.op("pe", lambda e, kt=kt, tt=tt, pb=pb, sl=sl: e.matmul(
                        pb[:, 0:NS], oT[:, kt, tt * 128:(tt + 1) * 128], sl[:, kt, :],
                        start=(kt == 0), stop=(kt == KT - 1)), r=[sl, oT], w=[pb])
                P.op("dve", lambda e, pb=pb, tt=tt, s0=s0: e.tensor_tensor(
                    out=yt[tt][:, s0:s0 + NS], in0=yt[tt][:, s0:s0 + NS], in1=pb[:, 0:NS], op=ALU.add),
                    r=[yt[tt], pb], w=[yt[tt]])
        for tt in range(TB // 128):
            P.dma("sp", Y[t0 + tt * 128:t0 + (tt + 1) * 128, :], yt[tt][:], r=[yt[tt]], w=[Y])
    P.release(m0)


def make_consts(T):
    cols = {}
    parts = []
    off = 0

    def add(name, arr):
        nonlocal off
        arr = np.asarray(arr, np.float32)
        cols[name] = (off, arr.shape[1])
        parts.append(arr)
        off += arr.shape[1]

    t = np.arange(T)
    add("reset", np.broadcast_to((t % 64 != 0).astype(np.float32)[None, :], (128, T)))
    s = np.arange(128)
    same = (s[:, None] // 64) == (s[None, :] // 64)
    add("mc2", (same & (s[:, None] <= s[None, :])).astype(np.float32))
    add("usuf", (same & (s[:, None] > s[None, :])).astype(np.float32))
    add("onesdiv", np.full((128, 128), 1.0 / 128, np.float32))
    add("same", same.astype(np.float32))
    add("iota", np.broadcast_to(t.astype(np.float32)[None, :], (128, T)))
    add("mask8", ((s[:, None] // 16) == np.arange(8)[None, :]).astype(np.float32))
    return np.ascontiguousarray(np.concatenate(parts, axis=1)), cols


def load_consts(P, C, consts_d, cols):
    n = sum(v[1] for v in cols.values())
    cb = P.sb([n], F32, "consts")
    P.dma("sp", cb[:], consts_d, w=[cb])
    C["cb"] = cb
    C["cols"] = cols


def cview(C, name):
    o, n = C["cols"][name]
    return C["cb"][:, o:o + n]


def hgrn2_pass1(P, C, T, ZF, ZT, lbp_d, lbrow_d, KH, VB, QT, KT, DECd, SLd):
    NT = T // 128
    NCH = T // 64
    cb = C["cb"]
    m0 = P.mark()
    lbrow = P.sb([2048], F32, "lbrow")
    omlrow = P.sb([2048], F32, "omlrow")
    lbp = P.sb([16], F32, "lbp")
    omlp = P.sb([16], F32, "omlp")
    tmp2 = P.sb([2, 2048], F32, "lbtmp")
    tmpp = P.sb([2, 16], F32, "lbtmpp")
    P.dma("sp", tmp2[:], lbrow_d, w=[tmp2])
    P.dma("sp", tmpp[:], lbp_d, w=[tmpp])
    P.op("dve", lambda e: e.tensor_tensor(out=lbrow[:], in0=tmp2[:, 0, :], in1=tmp2[:, 1, :], op=ALU.subtract),
         r=[tmp2], w=[lbrow])
    P.op("act", lambda e: e.activation(out=lbrow[:], in_=lbrow[:], func=AF.Sigmoid), r=[lbrow], w=[lbrow])
    P.op("dve", lambda e: e.tensor_scalar(out=omlrow[:], in0=lbrow[:], scalar1=-1.0, scalar2=1.0,
                                          op0=ALU.mult, op1=ALU.add), r=[lbrow], w=[omlrow])
    P.op("dve", lambda e: e.tensor_tensor(out=lbp[:], in0=tmpp[:, 0, :], in1=tmpp[:, 1, :], op=ALU.subtract),
         r=[tmpp], w=[lbp])
    P.op("act", lambda e: e.activation(out=lbp[:], in_=lbp[:], func=AF.Sigmoid), r=[lbp], w=[lbp])
    P.op("dve", lambda e: e.tensor_scalar(out=omlp[:], in0=lbp[:], scalar1=-1.0, scalar2=1.0,
                                          op0=ALU.mult, op1=ALU.add), r=[lbp], w=[omlp])
    ft = P.sb([512], F32, "h_ft")
    vt = P.sb([512], F32, "h_vt")
    lg = P.sb([512], F32, "h_lg")
    eR = P.sb([512], F32, "h_eR")
    k1 = P.sb([512], F32, "h_k1")
    khat = P.sb([NT, 512], BF16, "h_khat")
    vb = P.sb([NT, 512], BF16, "h_vb")
    A = P.sb([T], F32, "h_A")
    B = P.sb([T], F32, "h_B")
    Cb = P.sb([T], F32, "h_C")
    Dd = P.sb([T], F32, "h_D")
    E = P.sb([T], F32, "h_E")
    qt = P.sb([T], BF16, "h_qt")
    kt = P.sb([T], BF16, "h_kt")
    dec = P.sb([16, NCH], F32, "h_dec")
    dsum = P.sb([1], F32, "h_dsum")
    SL = P.sb([16 * 128 + 16], F32, "h_SL")
    S = P.sb([128], F32, "h_S")
    usuf = cview(C, "usuf")
    reset = cview(C, "reset")
    psR = [P.ps[0], P.ps[1]]
    psU = [P.ps[2], P.ps[3]]
    KHv = KH.t.rearrange("(nt p) c -> p nt c", p=128)
    VBv = VB.t.rearrange("(nt p) c -> p nt c", p=128)
    for hg in range(4):
        h0 = hg * 4
        for tt in range(NT):
            pr = psR[tt % 2]
            P.dma("sp", ft[:], ZT[tt * 128:(tt + 1) * 128, h0 * 128:h0 * 128 + 512], r=[ZT], w=[ft])
            P.dma("sp", vt[:], ZT[tt * 128:(tt + 1) * 128, 2048 + h0 * 128:2048 + h0 * 128 + 512], r=[ZT], w=[vt])
            P.op("dve", lambda e, h0=h0: e.tensor_tensor(out=ft[:], in0=ft[:], in1=omlrow[:, h0 * 128:h0 * 128 + 512],
                                                         op=ALU.mult), r=[ft, omlrow], w=[ft])
            P.op("dve", lambda e, h0=h0: e.tensor_tensor(out=ft[:], in0=ft[:], in1=lbrow[:, h0 * 128:h0 * 128 + 512],
                                                         op=ALU.add), r=[ft, lbrow], w=[ft])
            P.op("act", lambda e: e.activation(out=lg[:], in_=ft[:], func=AF.Ln), r=[ft], w=[lg])
            P.op("pe", lambda e, pr=pr: e.matmul(pr[:], usuf, lg[:], start=True, stop=True), r=[cb, lg], w=[pr])
            P.op("act", lambda e, pr=pr: e.activation(out=eR[:], in_=pr[:], func=AF.Exp), r=[pr], w=[eR])
            P.op("dve", lambda e: e.tensor_scalar(out=k1[:], in0=ft[:], scalar1=-1.0, scalar2=1.0,
                                                  op0=ALU.mult, op1=ALU.add), r=[ft], w=[k1])
            P.op("dve", lambda e, tt=tt: e.tensor_tensor(out=khat[:, tt, :], in0=k1[:], in1=eR[:], op=ALU.mult),
                 r=[k1, eR], w=[khat])
            P.op("pool", lambda e, tt=tt: e.tensor_copy(out=vb[:, tt, :], in_=vt[:]), r=[vt], w=[vb])
        P.dma("sp", KHv[:, :, h0 * 128:h0 * 128 + 512], khat[:], r=[khat], w=[KH])
        P.dma("sp", VBv[:, :, h0 * 128:h0 * 128 + 512], vb[:], r=[vb], w=[VB])
        for hl in range(4):
            h = h0 + hl
            P.dma("sp", A[:], ZF[h * 128:(h + 1) * 128, :], r=[ZF], w=[A])
            P.dma("sp", B[:], ZF[2048 + h * 128:2048 + (h + 1) * 128, :], r=[ZF], w=[B])
            P.op("dve", lambda e, h=h: e.tensor_scalar(out=B[:], in0=B[:], scalar1=omlp[:, h:h + 1],
                                                       scalar2=lbp[:, h:h + 1], op0=ALU.mult, op1=ALU.add),
                 r=[B, omlp, lbp], w=[B])
            P.op("act", lambda e: e.activation(out=Cb[:], in_=B[:], func=AF.Ln), r=[B], w=[Cb])
            P.op("dve", lambda e: e.tensor_tensor_scan(out=Dd[:], data0=reset[:, 0:T], data1=Cb[:], initial=0.0,
                                                       op0=ALU.mult, op1=ALU.add), r=[cb, Cb], w=[Dd])
            P.op("act", lambda e: e.activation(out=E[:], in_=Dd[:], func=AF.Exp), r=[Dd], w=[E])
            P.op("dve", lambda e: e.tensor_tensor(out=qt[:], in0=A[:], in1=E[:], op=ALU.mult), r=[A, E], w=[qt])
            P.op("act", lambda e: e.activation(out=A[:], in_=Dd[:], func=AF.Exp, scale=-1.0), r=[Dd, qt], w=[A])
            P.op("dve", lambda e: e.tensor_scalar(out=B[:], in0=B[:], scalar1=-1.0, scalar2=1.0,
                                                  op0=ALU.mult, op1=ALU.add), r=[B, Cb], w=[B])
            P.op("dve", lambda e: e.tensor_tensor(out=kt[:], in0=B[:], in1=A[:], op=ALU.mult), r=[A, B], w=[kt])
            Ev = E[:].rearrange("p (c s) -> p c s", s=64)
            Dv = Dd[:].rearrange("p (c s) -> p c s", s=64)
            P.op("dve", lambda e, h=h, Ev=Ev: e.tensor_copy(out=dec[:, h, :], in_=Ev[:, :, 63]), r=[E], w=[dec])
            P.op("dve", lambda e, Dv=Dv: e.tensor_reduce(out=dsum[:], in_=Dv[:, :, 63], axis=AX.X, op=ALU.add),
                 r=[Dd], w=[dsum])
            P.op("act", lambda e, h=h: e.activation(out=SL[:, 2048 + h:2048 + h + 1], in_=dsum[:], func=AF.Exp),
                 r=[dsum], w=[SL])
            P.dma("sp", QT[h * 128:(h + 1) * 128, :], qt[:], r=[qt], w=[QT])
            P.dma("sp", KT[h * 128:(h + 1) * 128, :], kt[:], r=[kt], w=[KT])
            P.op("dve", lambda e: e.memset(S[:], 0.0), w=[S])
            for c in range(NCH):
                tt, b0 = c // 2, (c % 2) * 64
                pu = psU[c % 2]
                P.op("pe", lambda e, tt=tt, b0=b0, hl=hl, pu=pu: e.matmul(
                    pu[:, 0:128], khat[b0:b0 + 64, tt, hl * 128:(hl + 1) * 128],
                    vb[b0:b0 + 64, tt, hl * 128:(hl + 1) * 128], start=True, stop=True), r=[khat, vb], w=[pu])
                P.op("dve", lambda e, h=h, c=c, pu=pu: e.scalar_tensor_tensor(
                    out=S[:], in0=S[:], scalar=dec[:, h, c:c + 1], in1=pu[:, 0:128], op0=ALU.mult, op1=ALU.add),
                    r=[S, dec, pu], w=[S])
            P.op("dve", lambda e, h=h: e.tensor_copy(out=SL[:, h * 128:(h + 1) * 128], in_=S[:]), r=[S], w=[SL])
    P.dma("sp", DECd.t, dec[:].rearrange("p a b -> p (a b)"), r=[dec], w=[DECd])
    P.dma("sp", SLd.t, SL[:], r=[SL], w=[SLd])
    P.release(m0)


def hgrn2_pass2(P, C, T, ZF, KH, VB, QT, KT, DECd, SId, OT):
    NT = T // 128
    NCH = T // 64
    cb = C["cb"]
    m0 = P.mark()
    khat = P.sb([NT, 512], BF16, "g_khat")
    vb = P.sb([NT, 512], BF16, "g_vb")
    qt = P.sb([T], BF16, "g_qt")
    kt = P.sb([T], BF16, "g_kt")
    sg = P.sb([T], F32, "g_sg")
    og = P.sb([T], BF16, "g_og")
    dec = P.sb([16, NCH], F32, "g_dec")
    S = P.sb([128], F32, "g_S")
    Sb = P.sb([128], BF16, "g_Sb")
    am = P.sb([128], BF16, "g_am")
    osb = P.sb([128], F32, "g_osb")
    sq = P.sb([128], F32, "g_sq")
    rsd = P.sb([128], F32, "g_rsd")
    mc2 = cview(C, "mc2")
    onesdiv = cview(C, "onesdiv")
    KHv = KH.t.rearrange("(nt p) c -> p nt c", p=128)
    VBv = VB.t.rearrange("(nt p) c -> p nt c", p=128)
    P.dma("sp", dec[:].rearrange("p a b -> p (a b)"), DECd.t, r=[DECd], w=[dec])
    psA, psO, psM = P.ps[0], P.ps[1], P.ps[4]
    psU = [P.ps[2], P.ps[3]]
    for hg in range(4):
        h0 = hg * 4
        P.dma("sp", khat[:], KHv[:, :, h0 * 128:h0 * 128 + 512], r=[KH], w=[khat])
        P.dma("sp", vb[:], VBv[:, :, h0 * 128:h0 * 128 + 512], r=[VB], w=[vb])
        for hl in range(4):
            h = h0 + hl
            P.dma("sp", qt[:], QT[h * 128:(h + 1) * 128, :], r=[QT], w=[qt])
            P.dma("sp", kt[:], KT[h * 128:(h + 1) * 128, :], r=[KT], w=[kt])
            P.dma("sp", sg[:], ZF[4096 + h * 128:4096 + (h + 1) * 128, :], r=[ZF], w=[sg])
            P.dma("sp", S[:], SId[:, h * 128:(h + 1) * 128], r=[SId], w=[S])
            P.op("act", lambda e: e.activation(out=Sb[:], in_=S[:], func=AF.Copy), r=[S], w=[Sb])
            for tt in range(NT):
                ts = slice(tt * 128, (tt + 1) * 128)
                P.op("pe", lambda e, ts=ts: e.matmul(psA[:, 0:128], kt[:, ts], qt[:, ts], start=True, stop=True),
                     r=[kt, qt], w=[psA])
                P.op("dve", lambda e: e.tensor_tensor(out=am[:], in0=psA[:, 0:128], in1=mc2, op=ALU.mult),
                     r=[psA, cb], w=[am])
                P.op("pe", lambda e, tt=tt, hl=hl: e.matmul(psO[:, 0:128], vb[:, tt, hl * 128:(hl + 1) * 128], am[:],
                                                            start=True, stop=False), r=[vb, am], w=[psO])
                for ci in range(2):
                    c = 2 * tt + ci
                    b0 = ci * 64
                    pu = psU[c % 2]
                    P.op("pe", lambda e, b0=b0, c=c: e.matmul(psO[:, b0:b0 + 64], Sb[:], qt[:, c * 64:(c + 1) * 64],
                                                              start=False, stop=True), r=[Sb, qt], w=[psO])
                    P.op("pe", lambda e, tt=tt, b0=b0, hl=hl, pu=pu: e.matmul(
                        pu[:, 0:128], khat[b0:b0 + 64, tt, hl * 128:(hl + 1) * 128],
                        vb[b0:b0 + 64, tt, hl * 128:(hl + 1) * 128], start=True, stop=True), r=[khat, vb], w=[pu])
                    P.op("dve", lambda e, h=h, c=c, pu=pu: e.scalar_tensor_tensor(
                        out=S[:], in0=S[:], scalar=dec[:, h, c:c + 1], in1=pu[:, 0:128], op0=ALU.mult, op1=ALU.add),
                        r=[S, dec, pu], w=[S])
                    P.op("act", lambda e: e.activation(out=Sb[:], in_=S[:], func=AF.Copy), r=[S], w=[Sb])
                P.op("act", lambda e: e.activation(out=osb[:], in_=psO[:, 0:128], func=AF.Copy), r=[psO], w=[osb])
                P.op("act", lambda e: e.activation(out=sq[:], in_=psO[:, 0:128], func=AF.Square), r=[psO], w=[sq])
                P.op("pe", lambda e: e.matmul(psM[:, 0:128], onesdiv, sq[:], start=True, stop=True), r=[cb, sq], w=[psM])
                P.op("act", lambda e: e.activation(out=rsd[:], in_=psM[:, 0:128], func=AF.Sqrt, bias=C["eps"][:], scale=1.0),
                     r=[psM, C["eps"]], w=[rsd])
                P.op("dve", lambda e: e.reciprocal(out=rsd[:], in_=rsd[:]), r=[rsd], w=[rsd])
                P.op("dve", lambda e: e.tensor_tensor(out=osb[:], in0=osb[:], in1=rsd[:], op=ALU.mult), r=[osb, rsd], w=[osb])
                P.op("dve", lambda e, ts=ts: e.tensor_tensor(out=og[:, ts], in0=osb[:], in1=sg[:, ts], op=ALU.mult),
                     r=[osb, sg], w=[og])
            P.dma("sp", OT[h * 128:(h + 1) * 128, :], og[:], r=[og], w=[OT])
    P.release(m0)


MAGIC = 12582912.0
TWO_PI = float(2.0 * np.pi)


def sincos(P, C, arg, s_out, c_out, t0, t1):
    P.op("dve", lambda e: e.tensor_scalar(out=t0[:], in0=arg[:], scalar1=1.0 / TWO_PI, scalar2=MAGIC,
                                          op0=ALU.mult, op1=ALU.add), r=[arg], w=[t0])
    P.op("dve", lambda e: e.tensor_scalar(out=t0[:], in0=t0[:], scalar1=-MAGIC, scalar2=-TWO_PI,
                                          op0=ALU.add, op1=ALU.mult), r=[t0], w=[t0])
    P.op("dve", lambda e: e.tensor_tensor(out=t1[:], in0=arg[:], in1=t0[:], op=ALU.add), r=[arg, t0], w=[t1])
    P.op("act", lambda e: e.activation(out=s_out[:], in_=t1[:], func=AF.Sin), r=[t1], w=[s_out])
    P.op("act", lambda e: e.activation(out=t0[:], in_=t1[:], func=AF.Abs), r=[t1], w=[t0])
    P.op("act", lambda e: e.activation(out=c_out[:], in_=t0[:], func=AF.Sin, scale=-1.0, bias=C["halfpi"][:]),
         r=[t0, C["halfpi"]], w=[c_out])


def s5_scan(P, C, T, ZF, prm, XI_d, SXL_d, Y2):
    cb = C["cb"]
    NTC = (T + 511) // 512
    m0 = P.mark()
    iota = cview(C, "iota")[:, 0:T]
    mask8 = cview(C, "mask8")
    lbu = P.sb([64, 2, 128], BF16, "s5_lbu")
    lc = P.sb([64, 2, 128], BF16, "s5_lc") if Y2 is not None else None
    mp = P.mark()
    def pb(name, free=(16, 64)):
        return P.sb(list(free), F32, "s5_" + name)
    are, aim, brT, biT = pb("are"), pb("aim"), pb("brT"), pb("biT")
    ldt = P.sb([16], F32, "s5_ldt")
    for b_, nm in ((are, "are_b"), (aim, "aim_b"), (brT, "brT"), (biT, "biT"), (ldt, "ldt_b")):
        P.dma("sp", b_[:], prm[nm], w=[b_])
    dt = P.sb([16], F32, "s5_dt")
    P.op("act", lambda e: e.activation(out=dt[:], in_=ldt[:], func=AF.Exp), r=[ldt], w=[dt])
    dtb = dt[:].unsqueeze(2).to_broadcast([128, 16, 64])
    P.op("dve", lambda e: e.tensor_scalar(out=are[:], in0=are[:], scalar1=-1e-4, scalar2=None, op0=ALU.min),
         r=[are], w=[are])
    x1, th, sn, cs, t0, t1 = pb("x1"), pb("th"), pb("sn"), pb("cs"), pb("t0"), pb("t1")
    P.op("dve", lambda e: e.tensor_tensor(out=x1[:], in0=are[:], in1=dtb, op=ALU.mult), r=[are, dt], w=[x1])
    P.op("act", lambda e: e.activation(out=x1[:], in_=x1[:], func=AF.Exp), r=[x1], w=[x1])
    P.op("dve", lambda e: e.tensor_tensor(out=th[:], in0=aim[:], in1=dtb, op=ALU.mult), r=[aim, dt], w=[th])
    sincos(P, C, th, sn, cs, t0, t1)
    abr, abi = cs, sn
    P.op("dve", lambda e: e.tensor_tensor(out=abr[:], in0=cs[:], in1=x1[:], op=ALU.mult), r=[cs, x1], w=[abr])
    P.op("dve", lambda e: e.tensor_tensor(out=abi[:], in0=sn[:], in1=x1[:], op=ALU.mult), r=[sn, x1], w=[abi])
    P.op("dve", lambda e: e.tensor_scalar(out=abr[:], in0=abr[:], scalar1=-1.0, scalar2=None, op0=ALU.add),
         r=[abr], w=[abr])
    den = x1
    P.op("dve", lambda e: e.tensor_tensor(out=t0[:], in0=are[:], in1=are[:], op=ALU.mult), r=[are], w=[t0])
    P.op("dve", lambda e: e.tensor_tensor(out=t1[:], in0=aim[:], in1=aim[:], op=ALU.mult), r=[aim], w=[t1])
    P.op("dve", lambda e: e.tensor_tensor(out=den[:], in0=t0[:], in1=t1[:], op=ALU.add), r=[t0, t1], w=[den])
    P.op("dve", lambda e: e.reciprocal(out=den[:], in_=den[:]), r=[den], w=[den])
    zr, zi = th, pb("zi")
    P.op("dve", lambda e: e.tensor_tensor(out=t0[:], in0=abr[:], in1=are[:], op=ALU.mult), r=[abr, are], w=[t0])
    P.op("dve", lambda e: e.tensor_tensor(out=t1[:], in0=abi[:], in1=aim[:], op=ALU.mult), r=[abi, aim], w=[t1])
    P.op("dve", lambda e: e.tensor_tensor(out=t0[:], in0=t0[:], in1=t1[:], op=ALU.add), r=[t0, t1], w=[t0])
    P.op("dve", lambda e: e.tensor_tensor(out=zr[:], in0=t0[:], in1=den[:], op=ALU.mult), r=[t0, den], w=[zr])
    P.op("dve", lambda e: e.tensor_tensor(out=t0[:], in0=abi[:], in1=are[:], op=ALU.mult), r=[abi, are], w=[t0])
    P.op("dve", lambda e: e.tensor_tensor(out=t1[:], in0=abr[:], in1=aim[:], op=ALU.mult), r=[abr, aim], w=[t1])
    P.op("dve", lambda e: e.tensor_tensor(out=t0[:], in0=t0[:], in1=t1[:], op=ALU.subtract), r=[t0, t1], w=[t0])
    P.op("dve", lambda e: e.tensor_tensor(out=zi[:], in0=t0[:], in1=den[:], op=ALU.mult), r=[t0, den], w=[zi])
    bbr, bbi = are, aim
    P.op("dve", lambda e: e.tensor_tensor(out=t0[:], in0=zr[:], in1=brT[:], op=ALU.mult), r=[zr, brT], w=[t0])
    P.op("dve", lambda e: e.tensor_tensor(out=t1[:], in0=zi[:], in1=biT[:], op=ALU.mult), r=[zi, biT], w=[t1])
    P.op("dve", lambda e: e.tensor_tensor(out=bbr[:], in0=t0[:], in1=t1[:], op=ALU.subtract), r=[t0, t1], w=[bbr])
    P.op("dve", lambda e: e.tensor_tensor(out=t0[:], in0=zr[:], in1=biT[:], op=ALU.mult), r=[zr, biT], w=[t0])
    P.op("dve", lambda e: e.tensor_tensor(out=t1[:], in0=zi[:], in1=brT[:], op=ALU.mult), r=[zi, brT], w=[t1])
    P.op("dve", lambda e: e.tensor_tensor(out=bbi[:], in0=t0[:], in1=t1[:], op=ALU.add), r=[t0, t1], w=[bbi])
    for m in range(64):
        kt, ga = m // 4, (2 * m) % 8
        for ri, src in ((0, bbr), (1, bbi)):
            for g2 in range(2):
                P.op("dve", lambda e, m=m, ri=ri, src=src, g2=g2, kt=kt, ga=ga: e.tensor_scalar(
                    out=lbu[:, m, ri, g2 * 64:(g2 + 1) * 64], in0=src[:, kt, :],
                    scalar1=mask8[:, ga + g2:ga + g2 + 1], scalar2=None, op0=ALU.mult), r=[src, cb], w=[lbu])
    if Y2 is not None:
        crT = P.sb([64, 16], F32, "s5_crT")
        ciT = P.sb([64, 16], F32, "s5_ciT")
        P.dma("sp", crT[:], prm["crT"], w=[crT])
        P.dma("sp", ciT[:], prm["ciT"], w=[ciT])
        P.op("pool", lambda e: e.memset(lc[:], 0.0), w=[lc])
        for m in range(64):
            for g2 in range(2):
                lg_ = (2 * m + g2) % 8
                ps_ = slice(g2 * 64, (g2 + 1) * 64)
                P.op("act", lambda e, m=m, ps_=ps_, lg_=lg_: e.activation(
                    out=lc[ps_, m, 0, lg_ * 16:(lg_ + 1) * 16], in_=crT[ps_, m, :], func=AF.Copy),
                    r=[crT], w=[lc])
                P.op("act", lambda e, m=m, ps_=ps_, lg_=lg_: e.activation(
                    out=lc[ps_, m, 1, lg_ * 16:(lg_ + 1) * 16], in_=ciT[ps_, m, :], func=AF.Copy, scale=-1.0),
                    r=[ciT], w=[lc])
    P.barrier()
    P.off = mp
    apr = P.sb([64], F32, "s5_apr")
    api = P.sb([64], F32, "s5_api")
    ldp = P.sb([64], F32, "s5_ldp")
    for b_, nm in ((apr, "are_p"), (api, "aim_p"), (ldp, "ldt_p")):
        P.dma("sp", b_[:], prm[nm], w=[b_])
    rp = P.sb([64], F32, "s5_rp")
    thp = P.sb([64], F32, "s5_thp")
    thn = P.sb([64], F32, "s5_thn")
    P.op("act", lambda e: e.activation(out=ldp[:], in_=ldp[:], func=AF.Exp), r=[ldp], w=[ldp])
    P.op("dve", lambda e: e.tensor_scalar(out=apr[:], in0=apr[:], scalar1=-1e-4, scalar2=None, op0=ALU.min),
         r=[apr], w=[apr])
    P.op("dve", lambda e: e.tensor_tensor(out=rp[:], in0=apr[:], in1=ldp[:], op=ALU.mult), r=[apr, ldp], w=[rp])
    P.op("act", lambda e: e.activation(out=rp[:], in_=rp[:], func=AF.Exp), r=[rp], w=[rp])
    P.op("dve", lambda e: e.tensor_tensor(out=thp[:], in0=api[:], in1=ldp[:], op=ALU.mult), r=[api, ldp], w=[thp])
    P.op("dve", lambda e: e.tensor_scalar(out=thn[:], in0=thp[:], scalar1=1.0 / TWO_PI, scalar2=None, op0=ALU.mult),
         r=[thp], w=[thn])
    xh0 = P.sb([64, 2], F32, "s5_xh0")
    if XI_d is not None:
        xi = P.sb([64, 2], F32, "s5_xi")
        P.dma("sp", xi[:], XI_d.t, r=[XI_d], w=[xi])
        s1_, c1_, q0, q1 = (P.sb([64], F32, "s5_i%d" % i) for i in range(4))
        sincos(P, C, thp, s1_, c1_, q0, q1)
        P.op("dve", lambda e: e.tensor_tensor(out=q0[:], in0=xi[:, :, 0], in1=c1_[:], op=ALU.mult), r=[xi, c1_], w=[q0])
        P.op("dve", lambda e: e.tensor_tensor(out=q1[:], in0=xi[:, :, 1], in1=s1_[:], op=ALU.mult), r=[xi, s1_], w=[q1])
        P.op("dve", lambda e: e.tensor_tensor(out=xh0[:, :, 0], in0=q0[:], in1=q1[:], op=ALU.subtract), r=[q0, q1], w=[xh0])
        P.op("dve", lambda e: e.tensor_tensor(out=q0[:], in0=xi[:, :, 1], in1=c1_[:], op=ALU.mult), r=[xi, c1_], w=[q0])
        P.op("dve", lambda e: e.tensor_tensor(out=q1[:], in0=xi[:, :, 0], in1=s1_[:], op=ALU.mult), r=[xi, s1_], w=[q1])
        P.op("dve", lambda e: e.tensor_tensor(out=xh0[:, :, 1], in0=q0[:], in1=q1[:], op=ALU.add), r=[q0, q1], w=[xh0])
    else:
        P.op("dve", lambda e: e.memset(xh0[:], 0.0), w=[xh0])
    sxl = P.sb([64, 2], F32, "s5_sxl")
    fq = [P.sb([1], F32, "s5_fq%d" % i) for i in range(2)]
    B1, B2, B3, B4, B5, B6 = (P.sb([T], F32, "s5_B%d" % i) for i in range(6))
    uf = P.sb([T], F32, "s5_uf")
    ub = P.sb([T], BF16, "s5_ub")
    xrb = P.sb([T], BF16, "s5_xrb")
    xib = P.sb([T], BF16, "s5_xib")
    dsk = P.sb([16], F32, "s5_d")
    ystg = P.sb([512], F32, "s5_ystg")
    if Y2 is not None:
        P.dma("sp", dsk[:], prm["d"], w=[dsk])
    psB = [P.ps[0], P.ps[1]]
    psY = [P.ps[2], P.ps[3], P.ps[4], P.ps[5]]
    for m in range(64):
        kt = m // 4
        if m % 4 == 0:
            P.dma("sp", uf[:], ZF[6144 + kt * 128:6144 + (kt + 1) * 128, :], r=[ZF], w=[uf])
            P.op("pool", lambda e: e.tensor_copy(out=ub[:], in_=uf[:]), r=[uf], w=[ub])
        for ri, dst in ((0, B1), (1, B2)):
            for tc in range(NTC):
                c0, c1 = tc * 512, min(T, tc * 512 + 512)
                pbk = psB[(ri * NTC + tc) % 2]
                P.op("pe", lambda e, m=m, ri=ri, c0=c0, c1=c1, pbk=pbk: e.matmul(
                    pbk[:, 0:c1 - c0], lbu[:, m, ri, :], ub[:, c0:c1], start=True, stop=True), r=[lbu, ub], w=[pbk])
                P.op("act", lambda e, dst=dst, c0=c0, c1=c1, pbk=pbk: e.activation(
                    out=dst[:, c0:c1], in_=pbk[:, 0:c1 - c0], func=AF.Copy), r=[pbk], w=[dst])
        P.op("dve", lambda e, m=m: e.tensor_scalar(out=B5[:], in0=iota, scalar1=thn[:, m:m + 1], scalar2=MAGIC,
                                                   op0=ALU.mult, op1=ALU.add), r=[cb, thn], w=[B5])
        P.op("dve", lambda e: e.tensor_scalar(out=B5[:], in0=B5[:], scalar1=-MAGIC, scalar2=-TWO_PI,
                                              op0=ALU.add, op1=ALU.mult), r=[B5], w=[B5])
        P.op("dve", lambda e, m=m: e.scalar_tensor_tensor(out=B5[:], in0=iota, scalar=thp[:, m:m + 1], in1=B5[:],
                                                          op0=ALU.mult, op1=ALU.add), r=[cb, thp, B5], w=[B5])
        P.op("act", lambda e: e.activation(out=B4[:], in_=B5[:], func=AF.Sin), r=[B5], w=[B4])
        P.op("act", lambda e: e.activation(out=B6[:], in_=B5[:], func=AF.Abs), r=[B5], w=[B6])
        P.op("act", lambda e: e.activation(out=B3[:], in_=B6[:], func=AF.Sin, scale=-1.0, bias=C["halfpi"][:]),
             r=[B6, C["halfpi"]], w=[B3])
        P.op("dve", lambda e: e.tensor_tensor(out=B5[:], in0=B1[:], in1=B3[:], op=ALU.mult), r=[B1, B3], w=[B5])
        P.op("pool", lambda e: e.tensor_tensor(out=B6[:], in0=B2[:], in1=B4[:], op=ALU.mult), r=[B2, B4], w=[B6])
        P.op("dve", lambda e: e.tensor_tensor(out=B5[:], in0=B5[:], in1=B6[:], op=ALU.add), r=[B5, B6], w=[B5])
        P.op("pool", lambda e: e.tensor_tensor(out=B6[:], in0=B2[:], in1=B3[:], op=ALU.mult), r=[B2, B3], w=[B6])
        P.op("dve", lambda e: e.tensor_tensor(out=B2[:], in0=B1[:], in1=B4[:], op=ALU.mult), r=[B1, B4, B6], w=[B2])
        P.op("dve", lambda e: e.tensor_tensor(out=B6[:], in0=B6[:], in1=B2[:], op=ALU.subtract), r=[B6, B2], w=[B6])
        rb = rp[:, m:m + 1].to_broadcast([128, T])
        P.op("dve", lambda e, m=m, rb=rb: e.tensor_tensor_scan(out=B1[:], data0=rb, data1=B5[:], initial=xh0[:, m, 0:1],
                                                               op0=ALU.mult, op1=ALU.add), r=[rp, B5, xh0], w=[B1])
        P.op("dve", lambda e, m=m, rb=rb: e.tensor_tensor_scan(out=B2[:], data0=rb, data1=B6[:], initial=xh0[:, m, 1:2],
                                                               op0=ALU.mult, op1=ALU.add), r=[rp, B6, xh0], w=[B2])
        if SXL_d is not None:
            L = T - 1
            P.op("dve", lambda e, L=L: e.tensor_tensor(out=fq[0][:], in0=B1[:, L:L + 1], in1=B3[:, L:L + 1], op=ALU.mult),
                 r=[B1, B3], w=[fq[0]])
            P.op("dve", lambda e, L=L: e.tensor_tensor(out=fq[1][:], in0=B2[:, L:L + 1], in1=B4[:, L:L + 1], op=ALU.mult),
                 r=[B2, B4], w=[fq[1]])
            P.op("dve", lambda e, m=m: e.tensor_tensor(out=sxl[:, m, 0:1], in0=fq[0][:], in1=fq[1][:], op=ALU.subtract),
                 r=fq, w=[sxl])
            P.op("dve", lambda e, L=L: e.tensor_tensor(out=fq[0][:], in0=B2[:, L:L + 1], in1=B3[:, L:L + 1], op=ALU.mult),
                 r=[B2, B3], w=[fq[0]])
            P.op("dve", lambda e, L=L: e.tensor_tensor(out=fq[1][:], in0=B1[:, L:L + 1], in1=B4[:, L:L + 1], op=ALU.mult),
                 r=[B1, B4], w=[fq[1]])
            P.op("dve", lambda e, m=m: e.tensor_tensor(out=sxl[:, m, 1:2], in0=fq[0][:], in1=fq[1][:], op=ALU.add),
                 r=fq, w=[sxl])
        if Y2 is not None:
            P.op("dve", lambda e: e.tensor_tensor(out=B5[:], in0=B1[:], in1=B3[:], op=ALU.mult), r=[B1, B3], w=[B5])
            P.op("pool", lambda e: e.tensor_tensor(out=B6[:], in0=B2[:], in1=B4[:], op=ALU.mult), r=[B2, B4], w=[B6])
            P.op("dve", lambda e: e.tensor_tensor(out=xrb[:], in0=B5[:], in1=B6[:], op=ALU.subtract), r=[B5, B6], w=[xrb])
            P.op("pool", lambda e: e.tensor_tensor(out=B5[:], in0=B2[:], in1=B3[:], op=ALU.mult), r=[B2, B3], w=[B5])
            P.op("dve", lambda e: e.tensor_tensor(out=B6[:], in0=B1[:], in1=B4[:], op=ALU.mult), r=[B1, B4], w=[B6])
            P.op("dve", lambda e: e.tensor_tensor(out=xib[:], in0=B5[:], in1=B6[:], op=ALU.add), r=[B5, B6], w=[xib])
            for tc in range(NTC):
                c0, c1 = tc * 512, min(T, tc * 512 + 512)
                for ri, src in ((0, xrb), (1, xib)):
                    P.op("pe", lambda e, m=m, ri=ri, src=src, c0=c0, c1=c1, tc=tc: e.matmul(
                        psY[tc][:, 0:c1 - c0], lc[:, m, ri, :], src[:, c0:c1],
                        start=(m % 4 == 0 and ri == 0), stop=(m % 4 == 3 and ri == 1)), r=[lc, src], w=[psY[tc]])
            if m % 4 == 3:
                ft = m // 4
                for tc in range(NTC):
                    c0, c1 = tc * 512, min(T, tc * 512 + 512)
                    P.op("dve", lambda e, ft=ft, c0=c0, c1=c1, tc=tc: e.scalar_tensor_tensor(
                        out=ystg[:, 0:c1 - c0], in0=uf[:, c0:c1], scalar=dsk[:, ft:ft + 1], in1=psY[tc][:, 0:c1 - c0],
                        op0=ALU.mult, op1=ALU.add), r=[uf, dsk, psY[tc]], w=[ystg])
                    P.op("act", lambda e, c0=c0, c1=c1: e.activation(out=ystg[:, 0:c1 - c0], in_=ystg[:, 0:c1 - c0],
                                                                     func=AF.Gelu), r=[ystg], w=[ystg])
                    P.dma("sp", Y2[ft * 128:(ft + 1) * 128, c0:c1], ystg[:, 0:c1 - c0], r=[ystg], w=[Y2])
    if SXL_d is not None:
        P.dma("sp", SXL_d.t, sxl[:], r=[sxl], w=[SXL_d])
    P.release(m0)


def glu_stage(P, C, T, Y2, Wg, OT, row0):
    m0 = P.mark()
    yb = P.sb([16, T], BF16, "gl_yb")
    yf = P.sb([T], F32, "gl_yf")
    og = P.sb([T], BF16, "gl_og")
    sl = P.sb([16, 512], BF16, "gl_slab")
    sgm = P.sb([512], F32, "gl_sg")
    for kt in range(16):
        P.dma("sp", yf[:], Y2[kt * 128:(kt + 1) * 128, :], r=[Y2], w=[yf])
        P.op("dve", lambda e, kt=kt: e.tensor_copy(out=yb[:, kt, :], in_=yf[:]), r=[yf], w=[yb])
    banks = [P.ps[0], P.ps[1]]
    k = 0
    for s in range(4):
        load_wslab(P, sl, Wg, s * 512, (s + 1) * 512, 16)
        for nt in range(4):
            n = s * 4 + nt
            P.dma("sp", yf[:], Y2[n * 128:(n + 1) * 128, :], r=[Y2], w=[yf])
            for c0 in range(0, T, 512):
                c1 = min(T, c0 + 512)
                pb = banks[k % 2]
                k += 1
                for kt in range(16):
                    P.op("pe", lambda e, kt=kt, nt=nt, c0=c0, c1=c1, pb=pb: e.matmul(
                        pb[:, 0:c1 - c0], sl[:, kt, nt * 128:(nt + 1) * 128], yb[:, kt, c0:c1],
                        start=(kt == 0), stop=(kt == 15)), r=[sl, yb], w=[pb])
                P.op("act", lambda e, c0=c0, c1=c1, pb=pb: e.activation(out=sgm[:, 0:c1 - c0], in_=pb[:, 0:c1 - c0],
                                                                        func=AF.Sigmoid), r=[pb], w=[sgm])
                P.op("dve", lambda e, c0=c0, c1=c1: e.tensor_tensor(out=og[:, c0:c1], in0=yf[:, c0:c1],
                                                                    in1=sgm[:, 0:c1 - c0], op=ALU.mult),
                     r=[yf, sgm], w=[og])
            P.dma("sp", OT[row0 + n * 128:row0 + (n + 1) * 128, :], og[:], r=[og], w=[OT])
    P.release(m0)


def s5_host_params(a_re, a_im, log_dt, b_re, b_im, c_re, c_im, b_d):
    def bl(a):
        a = a.reshape(16, 8, 1, 64)
        a = np.broadcast_to(a, (16, 8, 16, 64)).transpose(1, 2, 0, 3)
        return np.ascontiguousarray(a.reshape(128, 16, 64))
    def pl(a):
        return np.ascontiguousarray(a.reshape(64, 2, 64).transpose(1, 2, 0).reshape(128, 64))
    out = {}
    out["are_b"] = bl(a_re)
    out["aim_b"] = bl(a_im)
    l = np.broadcast_to(log_dt.reshape(16, 8, 1), (16, 8, 16)).transpose(1, 2, 0)
    out["ldt_b"] = np.ascontiguousarray(l.reshape(128, 16))
    out["brT"] = np.ascontiguousarray(b_re.reshape(16, 8, 64, 16).transpose(1, 3, 0, 2).reshape(128, 16, 64))
    out["biT"] = np.ascontiguousarray(b_im.reshape(16, 8, 64, 16).transpose(1, 3, 0, 2).reshape(128, 16, 64))
    out["are_p"] = pl(a_re)
    out["aim_p"] = pl(a_im)
    out["ldt_p"] = pl(np.broadcast_to(log_dt[:, None], (128, 64)))
    out["crT"] = np.ascontiguousarray(c_re.reshape(64, 2, 16, 64).transpose(1, 3, 0, 2).reshape(128, 64, 16))
    out["ciT"] = np.ascontiguousarray(c_im.reshape(64, 2, 16, 64).transpose(1, 3, 0, 2).reshape(128, 64, 16))
    out["d"] = np.ascontiguousarray(b_d.reshape(16, 128).T)
    return {k: v.astype(np.float32) for k, v in out.items()}


S5_SHAPES = {"are_b": [128, 16, 64], "aim_b": [128, 16, 64], "ldt_b": [128, 16], "brT": [128, 16, 64],
             "biT": [128, 16, 64], "are_p": [128, 64], "aim_p": [128, 64], "ldt_p": [128, 64],
             "crT": [128, 64, 16], "ciT": [128, 64, 16], "d": [128, 16]}


def ret_gammas():
    return [1.0 - 2.0 ** (-5.0 - h) for h in range(16)]


def make_ret_consts():
    g = np.array(ret_gammas(), np.float64)
    s = np.arange(128)
    same = (s[:, None] // 64) == (s[None, :] // 64)
    dist = np.abs(s[:, None] - s[None, :]).astype(np.float64)
    decT = np.stack([np.where(same, gh ** dist, 0.0) / 16.0 for gh in g], axis=1)
    tl = np.arange(64, dtype=np.float64)
    qdec = np.broadcast_to((g[:, None] ** (tl[None, :] + 1.0))[None], (128, 16, 64))
    kdec = (g[None, :] ** (63.0 - (s % 64))[:, None]) / 16.0
    return (np.ascontiguousarray(decT.reshape(128, 16 * 128), np.float32),
            np.ascontiguousarray(qdec.reshape(128, 16 * 64), np.float32),
            np.ascontiguousarray(kdec, np.float32))


def make_rope(pos0, T):
    pos = np.arange(pos0, pos0 + T, dtype=np.float32)
    inv = (np.float32(10000.0) ** (-np.arange(0, 256, 2, dtype=np.float32) / np.float32(256))).astype(np.float32)
    ang = (pos[:, None] * inv[None, :]).astype(np.float32)
    c, s_ = np.cos(ang).astype(np.float32), np.sin(ang).astype(np.float32)
    NT = T // 128
    cT = c.reshape(NT, 128, 128).transpose(1, 0, 2).reshape(128, NT * 128)
    sT = s_.reshape(NT, 128, 128).transpose(1, 0, 2).reshape(128, NT * 128)
    return np.ascontiguousarray(np.concatenate([c.T, s_.T, cT, sT], axis=1), np.float32)


def ret_pass1(P, C, T, ZF, ZT, rope_d, rc, KD, VB, QR, KR, QD, RLd):
    NT = T // 128
    NCH = T // 64
    gam = ret_gammas()
    m0 = P.mark()
    rope = P.sb([4 * T], F32, "r_rope")
    P.dma("sp", rope[:], rope_d, w=[rope])
    cosF, sinF = rope[:, 0:T], rope[:, T:2 * T]
    cosT = rope[:, 2 * T:3 * T].rearrange("p (n d) -> p n d", n=NT)
    sinT = rope[:, 3 * T:4 * T].rearrange("p (n d) -> p n d", n=NT)
    qdec = P.sb([16, 64], F32, "r_qdec")
    kdec = P.sb([16], F32, "r_kdec")
    P.dma("sp", qdec[:].rearrange("p a b -> p (a b)"), rc["qdec"], w=[qdec])
    P.dma("sp", kdec[:], rc["kdec"], w=[kdec])
    kt_ = P.sb([256], F32, "r_kt")
    vt_ = P.sb([512], F32, "r_vt")
    u0 = P.sb([128], F32, "r_u0")
    u1 = P.sb([128], F32, "r_u1")
    kr_ = P.sb([256], F32, "r_kr")
    kd = P.sb([NT, 256], BF16, "r_kd")
    vb = P.sb([NT, 512], BF16, "r_vb")
    XA, XB, W0, W1 = (P.sb([T], F32, "r_X%d" % i) for i in range(4))
    oA, oB, dA, dB = (P.sb([T], BF16, "r_o%d" % i) for i in range(4))
    R = [P.sb([512], F32, "r_R%d" % i) for i in range(2)]
    KDv = KD.t.rearrange("(nt p) c -> p nt c", p=128)
    VBv = VB.t.rearrange("(nt p) c -> p nt c", p=128)
    psU = [[P.ps[0], P.ps[1]], [P.ps[2], P.ps[3]]]
    for h in range(16):
        for tt in range(NT):
            rows = slice(tt * 128, (tt + 1) * 128)
            P.dma("sp", kt_[:], ZT[rows, h * 256:(h + 1) * 256], r=[ZT], w=[kt_])
            P.dma("sp", vt_[:], ZT[rows, 4096 + h * 512:4096 + (h + 1) * 512], r=[ZT], w=[vt_])
            c_, s_ = cosT[:, tt, :], sinT[:, tt, :]
            P.op("dve", lambda e, c_=c_: e.tensor_tensor(out=u0[:], in0=kt_[:, 0:128], in1=c_, op=ALU.mult), r=[kt_, rope], w=[u0])
            P.op("pool", lambda e, s_=s_: e.tensor_tensor(out=u1[:], in0=kt_[:, 128:256], in1=s_, op=ALU.mult), r=[kt_, rope], w=[u1])
            P.op("dve", lambda e: e.tensor_tensor(out=kr_[:, 0:128], in0=u0[:], in1=u1[:], op=ALU.subtract), r=[u0, u1], w=[kr_])
            P.op("dve", lambda e, c_=c_: e.tensor_tensor(out=u0[:], in0=kt_[:, 128:256], in1=c_, op=ALU.mult), r=[kt_, rope], w=[u0])
            P.op("pool", lambda e, s_=s_: e.tensor_tensor(out=u1[:], in0=kt_[:, 0:128], in1=s_, op=ALU.mult), r=[kt_, rope], w=[u1])
            P.op("dve", lambda e: e.tensor_tensor(out=kr_[:, 128:256], in0=u0[:], in1=u1[:], op=ALU.add), r=[u0, u1], w=[kr_])
            P.op("dve", lambda e, tt=tt, h=h: e.tensor_scalar(out=kd[:, tt, :], in0=kr_[:], scalar1=kdec[:, h:h + 1], scalar2=None,
                                                              op0=ALU.mult), r=[kr_, kdec], w=[kd])
            P.op("pool", lambda e, tt=tt: e.tensor_copy(out=vb[:, tt, :], in_=vt_[:]), r=[vt_], w=[vb])
        P.dma("sp", KDv[:, :, h * 256:(h + 1) * 256], kd[:], r=[kd], w=[KD])
        P.dma("sp", VBv[:, :, h * 512:(h + 1) * 512], vb[:], r=[vb], w=[VB])
        for which, base, outs in (("q", 0, (oA, oB)), ("k", 4096, (oA, oB))):
            P.dma("sp", XA[:], ZF[base + h * 256:base + h * 256 + 128, :], r=[ZF], w=[XA])
            P.dma("sp", XB[:], ZF[base + h * 256 + 128:base + h * 256 + 256, :], r=[ZF], w=[XB])
            P.op("dve", lambda e: e.tensor_tensor(out=W0[:], in0=XA[:], in1=cosF, op=ALU.mult), r=[XA, rope], w=[W0])
            P.op("pool", lambda e: e.tensor_tensor(out=W1[:], in0=XB[:], in1=sinF, op=ALU.mult), r=[XB, rope], w=[W1])
            P.op("dve", lambda e: e.tensor_tensor(out=W0[:], in0=W0[:], in1=W1[:], op=ALU.subtract), r=[W0, W1], w=[W0])
            P.op("act", lambda e: e.activation(out=oA[:], in_=W0[:], func=AF.Copy), r=[W0], w=[oA])
            if which == "q":
                P.op("dve", lambda e, h=h: e.tensor_tensor(
                    out=dA[:].rearrange("p (c s) -> p c s", s=64), in0=W0[:].rearrange("p (c s) -> p c s", s=64),
                    in1=qdec[:, h, :].unsqueeze(1).to_broadcast([128, NCH, 64]), op=ALU.mult), r=[W0, qdec], w=[dA])
            P.op("pool", lambda e: e.tensor_tensor(out=W1[:], in0=XA[:], in1=sinF, op=ALU.mult), r=[XA, rope], w=[W1])
            P.op("dve", lambda e: e.tensor_tensor(out=W0[:], in0=XB[:], in1=cosF, op=ALU.mult), r=[XB, rope], w=[W0])
            P.op("dve", lambda e: e.tensor_tensor(out=W0[:], in0=W0[:], in1=W1[:], op=ALU.add), r=[W0, W1], w=[W0])
            P.op("act", lambda e: e.activation(out=oB[:], in_=W0[:], func=AF.Copy), r=[W0], w=[oB])
            if which == "q":
                P.op("dve", lambda e, h=h: e.tensor_tensor(
                    out=dB[:].rearrange("p (c s) -> p c s", s=64), in0=W0[:].rearrange("p (c s) -> p c s", s=64),
                    in1=qdec[:, h, :].unsqueeze(1).to_broadcast([128, NCH, 64]), op=ALU.mult), r=[W0, qdec], w=[dB])
            dst = QR if which == "q" else KR
            P.dma("sp", dst[h * 256:h * 256 + 128, :], oA[:], r=[oA], w=[dst])
            P.dma("sp", dst[h * 256 + 128:h * 256 + 256, :], oB[:], r=[oB], w=[dst])
            if which == "q":
                P.dma("sp", QD[h * 256:h * 256 + 128, :], dA[:], r=[dA], w=[QD])
                P.dma("sp", QD[h * 256 + 128:h * 256 + 256, :], dB[:], r=[dB], w=[QD])
        for dtl in range(2):
            P.op("dve", lambda e, dtl=dtl: e.memset(R[dtl][:], 0.0), w=[R[dtl]])
        g64 = float(gam[h] ** 64)
        for c in range(NCH):
            tt, b0 = c // 2, (c % 2) * 64
            for dtl in range(2):
                pu = psU[c % 2][dtl]
                P.op("pe", lambda e, tt=tt, b0=b0, dtl=dtl, pu=pu: e.matmul(
                    pu[:], kd[b0:b0 + 64, tt, dtl * 128:(dtl + 1) * 128], vb[b0:b0 + 64, tt, :], start=True, stop=True),
                    r=[kd, vb], w=[pu])
                P.op("dve", lambda e, dtl=dtl, pu=pu, g64=g64: e.scalar_tensor_tensor(
                    out=R[dtl][:], in0=R[dtl][:], scalar=g64, in1=pu[:], op0=ALU.mult, op1=ALU.add),
                    r=[R[dtl], pu], w=[R[dtl]])
        for dtl in range(2):
            P.dma("sp", RLd[:, h, dtl, :], R[dtl][:], r=[R[dtl]], w=[RLd])
    P.release(m0)


def ret_pass2(P, C, T, ZF, rc, KD, VB, QR, KR, QD, RId, OT):
    NT = T // 128
    gam = ret_gammas()
    cb = C["cb"]
    m0 = P.mark()
    decT = P.sb([16, 128], F32, "q_decT")
    P.dma("sp", decT[:].rearrange("p a b -> p (a b)"), rc["decT"], w=[decT])
    od512 = P.sb([128], F32, "q_od")
    P.op("dve", lambda e: e.memset(od512[:], 1.0 / 512), w=[od512])
    kd = P.sb([NT, 256], BF16, "q_kd")
    vb = P.sb([NT, 512], BF16, "q_vb")
    qr = P.sb([2, T], BF16, "q_qr")
    kr = P.sb([2, T], BF16, "q_kr")
    qd = P.sb([2, T], BF16, "q_qd")
    sg = P.sb([4, T], F32, "q_sg")
    og = P.sb([4, T], BF16, "q_og")
    R = [P.sb([512], F32, "q_R%d" % i) for i in range(2)]
    Rb = [P.sb([512], BF16, "q_Rb%d" % i) for i in range(2)]
    sm = P.sb([128], BF16, "q_sm")
    osb = P.sb([4, 128], F32, "q_osb")
    sq = P.sb([4, 128], F32, "q_sq")
    rsd = P.sb([128], F32, "q_rsd")
    KDv = KD.t.rearrange("(nt p) c -> p nt c", p=128)
    VBv = VB.t.rearrange("(nt p) c -> p nt c", p=128)
    psS, psO, psM = P.ps[4], P.ps[5], P.ps[6]
    psU = [[P.ps[0], P.ps[1]], [P.ps[2], P.ps[3]]]
    psOv = psO.t.rearrange("p (j t) -> p j t", j=4)
    for h in range(16):
        g64 = float(gam[h] ** 64)
        P.dma("sp", kd[:], KDv[:, :, h * 256:(h + 1) * 256], r=[KD], w=[kd])
        P.dma("sp", vb[:], VBv[:, :, h * 512:(h + 1) * 512], r=[VB], w=[vb])
        for dtl in range(2):
            rs_ = slice(h * 256 + dtl * 128, h * 256 + (dtl + 1) * 128)
            P.dma("sp", qr[:, dtl, :], QR[rs_, :], r=[QR], w=[qr])
            P.dma("sp", kr[:, dtl, :], KR[rs_, :], r=[KR], w=[kr])
            P.dma("sp", qd[:, dtl, :], QD[rs_, :], r=[QD], w=[qd])
            P.dma("sp", R[dtl][:], RId[:, h, dtl, :], r=[RId], w=[R[dtl]])
            P.op("act", lambda e, dtl=dtl: e.activation(out=Rb[dtl][:], in_=R[dtl][:], func=AF.Copy), r=[R[dtl]], w=[Rb[dtl]])
        for j in range(4):
            P.dma("sp", sg[:, j, :], ZF[8192 + h * 512 + j * 128:8192 + h * 512 + (j + 1) * 128, :], r=[ZF], w=[sg])
        for tt in range(NT):
            ts = slice(tt * 128, (tt + 1) * 128)
            for dtl in range(2):
                P.op("pe", lambda e, dtl=dtl, ts=ts: e.matmul(psS[:, 0:128], kr[:, dtl, ts], qr[:, dtl, ts],
                                                              start=(dtl == 0), stop=(dtl == 1)), r=[kr, qr], w=[psS])
            P.op("dve", lambda e, h=h: e.tensor_tensor(out=sm[:], in0=psS[:, 0:128], in1=decT[:, h, :], op=ALU.mult),
                 r=[psS, decT], w=[sm])
            for ci in range(2):
                c = 2 * tt + ci
                b0 = ci * 64
                for j in range(4):
                    P.op("pe", lambda e, j=j, tt=tt, b0=b0: e.matmul(
                        psOv[:, j, b0:b0 + 64], vb[b0:b0 + 64, tt, j * 128:(j + 1) * 128], sm[b0:b0 + 64, b0:b0 + 64],
                        start=True, stop=False), r=[vb, sm], w=[psO])
                    for dtl in range(2):
                        P.op("pe", lambda e, j=j, dtl=dtl, b0=b0, c=c: e.matmul(
                            psOv[:, j, b0:b0 + 64], Rb[dtl][:, j * 128:(j + 1) * 128], qd[:, dtl, c * 64:(c + 1) * 64],
                            start=False, stop=(dtl == 1)), r=[Rb[dtl], qd], w=[psO])
                for dtl in range(2):
                    pu = psU[c % 2][dtl]
                    P.op("pe", lambda e, tt=tt, b0=b0, dtl=dtl, pu=pu: e.matmul(
                        pu[:], kd[b0:b0 + 64, tt, dtl * 128:(dtl + 1) * 128], vb[b0:b0 + 64, tt, :], start=True, stop=True),
                        r=[kd, vb], w=[pu])
                    P.op("dve", lambda e, dtl=dtl, pu=pu, g64=g64: e.scalar_tensor_tensor(
                        out=R[dtl][:], in0=R[dtl][:], scalar=g64, in1=pu[:], op0=ALU.mult, op1=ALU.add),
                        r=[R[dtl], pu], w=[R[dtl]])
                    P.op("act", lambda e, dtl=dtl: e.activation(out=Rb[dtl][:], in_=R[dtl][:], func=AF.Copy),
                         r=[R[dtl]], w=[Rb[dtl]])
            P.op("act", lambda e: e.activation(out=osb[:], in_=psOv, func=AF.Copy), r=[psO], w=[osb])
            P.op("act", lambda e: e.activation(out=sq[:], in_=psOv, func=AF.Square), r=[psO], w=[sq])
            for j in range(4):
                P.op("pe", lambda e, j=j: e.matmul(psM[:, 0:128], od512[:], sq[:, j, :], start=(j == 0), stop=(j == 3)),
                     r=[od512, sq], w=[psM])
            P.op("act", lambda e: e.activation(out=rsd[:], in_=psM[:, 0:128], func=AF.Sqrt, bias=C["eps"][:], scale=1.0),
                 r=[psM, C["eps"]], w=[rsd])
            P.op("dve", lambda e: e.reciprocal(out=rsd[:], in_=rsd[:]), r=[rsd], w=[rsd])
            P.op("dve", lambda e: e.tensor_tensor(out=osb[:], in0=osb[:], in1=rsd[:].unsqueeze(1).to_broadcast([128, 4, 128]),
                                                  op=ALU.mult), r=[osb, rsd], w=[osb])
            P.op("dve", lambda e, ts=ts: e.tensor_tensor(out=og[:, :, ts], in0=osb[:], in1=sg[:, :, ts], op=ALU.mult),
                 r=[osb, sg], w=[og])
        for j in range(4):
            P.dma("sp", OT[h * 512 + j * 128:h * 512 + (j + 1) * 128, :], og[:, j, :], r=[og], w=[OT])
    P.release(m0)


def gather_states(P, src, dst, ncores):
    if ncores == 1:
        P.dma("sp", dst.t, src.t, r=[src], w=[dst])
    else:
        P.collective("AllGather", src.handle.ap().opt(), dst.handle.ap().opt(), [list(range(ncores))],
                     r=[src], w=[dst])


def hgrn2_combine(P, C, SLall, sel, SId, ncores, seg_per_batch):
    m0 = P.mark()
    F = P.sb([16, 128], F32, "hc_F")
    SI = P.sb([16, 128], F32, "hc_SI")
    L = P.sb([2064], F32, "hc_L")
    P.op("dve", lambda e: e.memset(F[:], 0.0), w=[F])
    P.op("dve", lambda e: e.memset(SI[:], 0.0), w=[SI])
    for r in range(ncores):
        P.op("dve", lambda e, r=r: e.scalar_tensor_tensor(out=SI[:], in0=F[:], scalar=sel[:, r:r + 1], in1=SI[:],
                                                          op0=ALU.mult, op1=ALU.add), r=[F, sel, SI], w=[SI])
        if r == ncores - 1:
            break
        if (r + 1) % seg_per_batch == 0:
            P.op("dve", lambda e: e.memset(F[:], 0.0), w=[F])
            continue
        P.dma("sp", L[:], SLall[r * 128:(r + 1) * 128, :], r=[SLall], w=[L])
        P.op("dve", lambda e: e.tensor_tensor(out=F[:], in0=F[:], in1=L[:, 2048:2064].unsqueeze(2).to_broadcast([128, 16, 128]),
                                              op=ALU.mult), r=[F, L], w=[F])
        P.op("dve", lambda e: e.tensor_tensor(out=F[:], in0=F[:], in1=L[:, 0:2048].rearrange("p (h d) -> p h d", h=16),
                                              op=ALU.add), r=[F, L], w=[F])
    P.dma("sp", SId.t, SI[:].rearrange("p h d -> p (h d)"), r=[SI], w=[SId])
    P.release(m0)


def ret_combine(P, C, T, RLall, sel, RId, ncores, seg_per_batch):
    gam = ret_gammas()
    m0 = P.mark()
    F = P.sb([512], F32, "rc_F")
    SI = P.sb([512], F32, "rc_SI")
    L = [P.sb([512], F32, "rc_L%d" % i) for i in range(2)]
    RLv = RLall.t.rearrange("r (h d e) -> r h d e", h=16, d=2)
    k = 0
    for h in range(16):
        gT = float(gam[h] ** T)
        for dtl in range(2):
            P.op("dve", lambda e: e.memset(F[:], 0.0), w=[F])
            P.op("dve", lambda e: e.memset(SI[:], 0.0), w=[SI])
            for r in range(ncores):
                P.op("dve", lambda e, r=r: e.scalar_tensor_tensor(out=SI[:], in0=F[:], scalar=sel[:, r:r + 1], in1=SI[:],
                                                                  op0=ALU.mult, op1=ALU.add), r=[F, sel, SI], w=[SI])
                if r == ncores - 1:
                    break
                if (r + 1) % seg_per_batch == 0:
                    P.op("dve", lambda e: e.memset(F[:], 0.0), w=[F])
                    continue
                Lb = L[k % 2]
                k += 1
                P.dma("sp", Lb[:], RLv[r * 128:(r + 1) * 128, h, dtl, :], r=[RLall], w=[Lb])
                P.op("dve", lambda e, gT=gT, Lb=Lb: e.scalar_tensor_tensor(out=F[:], in0=F[:], scalar=gT, in1=Lb[:],
                                                                          op0=ALU.mult, op1=ALU.add), r=[F, Lb], w=[F])
            P.dma("sp", RId[:, h, dtl, :], SI[:], r=[SI], w=[RId])
    P.release(m0)


def s5_combine(P, C, T, prm, SXall, sel, XId, ncores, seg_per_batch):
    m0 = P.mark()
    apr = P.sb([64], F32, "sc_apr")
    api = P.sb([64], F32, "sc_api")
    ldp = P.sb([64], F32, "sc_ldp")
    for b_, nm in ((apr, "are_p"), (api, "aim_p"), (ldp, "ldt_p")):
        P.dma("sp", b_[:], prm[nm], w=[b_])
    mg = P.sb([64], F32, "sc_mg")
    th = P.sb([64], F32, "sc_th")
    sn, cs, t0, t1 = (P.sb([64], F32, "sc_t%d" % i) for i in range(4))
    P.op("act", lambda e: e.activation(out=ldp[:], in_=ldp[:], func=AF.Exp), r=[ldp], w=[ldp])
    P.op("dve", lambda e: e.tensor_scalar(out=apr[:], in0=apr[:], scalar1=-1e-4, scalar2=None, op0=ALU.min), r=[apr], w=[apr])
    P.op("dve", lambda e: e.tensor_tensor(out=mg[:], in0=apr[:], in1=ldp[:], op=ALU.mult), r=[apr, ldp], w=[mg])
    P.op("act", lambda e: e.activation(out=mg[:], in_=mg[:], func=AF.Exp, scale=float(T)), r=[mg], w=[mg])
    P.op("dve", lambda e: e.tensor_tensor(out=th[:], in0=api[:], in1=ldp[:], op=ALU.mult), r=[api, ldp], w=[th])
    P.op("dve", lambda e: e.tensor_scalar(out=t0[:], in0=th[:], scalar1=1.0 / TWO_PI, scalar2=MAGIC, op0=ALU.mult, op1=ALU.add),
         r=[th], w=[t0])
    P.op("dve", lambda e: e.tensor_scalar(out=t0[:], in0=t0[:], scalar1=-MAGIC, scalar2=-TWO_PI, op0=ALU.add, op1=ALU.mult),
         r=[t0], w=[t0])
    P.op("dve", lambda e: e.tensor_tensor(out=th[:], in0=th[:], in1=t0[:], op=ALU.add), r=[th, t0], w=[th])
    P.op("dve", lambda e: e.tensor_scalar(out=th[:], in0=th[:], scalar1=float(T), scalar2=None, op0=ALU.mult), r=[th], w=[th])
    sincos(P, C, th, sn, cs, t0, t1)
    ar, ai = cs, sn
    P.op("dve", lambda e: e.tensor_tensor(out=ar[:], in0=cs[:], in1=mg[:], op=ALU.mult), r=[cs, mg], w=[ar])
    P.op("dve", lambda e: e.tensor_tensor(out=ai[:], in0=sn[:], in1=mg[:], op=ALU.mult), r=[sn, mg], w=[ai])
    F = P.sb([64, 2], F32, "sc_F")
    SI = P.sb([64, 2], F32, "sc_SI")
    L = P.sb([64, 2], F32, "sc_L")
    nr = P.sb([64], F32, "sc_nr")
    ni = P.sb([64], F32, "sc_ni")
    P.op("dve", lambda e: e.memset(F[:], 0.0), w=[F])
    P.op("dve", lambda e: e.memset(SI[:], 0.0), w=[SI])
    SXv = SXall.t.rearrange("r (m two) -> r m two", two=2)
    for r in range(ncores):
        P.op("dve", lambda e, r=r: e.scalar_tensor_tensor(out=SI[:], in0=F[:], scalar=sel[:, r:r + 1], in1=SI[:],
                                                          op0=ALU.mult, op1=ALU.add), r=[F, sel, SI], w=[SI])
        if r == ncores - 1:
            break
        if (r + 1) % seg_per_batch == 0:
            P.op("dve", lambda e: e.memset(F[:], 0.0), w=[F])
            continue
        P.dma("sp", L[:], SXv[r * 128:(r + 1) * 128, :, :], r=[SXall], w=[L])
        P.op("dve", lambda e: e.tensor_tensor(out=t0[:], in0=F[:, :, 0], in1=ar[:], op=ALU.mult), r=[F, ar], w=[t0])
        P.op("dve", lambda e: e.tensor_tensor(out=t1[:], in0=F[:, :, 1], in1=ai[:], op=ALU.mult), r=[F, ai], w=[t1])
        P.op("dve", lambda e: e.tensor_tensor(out=nr[:], in0=t0[:], in1=t1[:], op=ALU.subtract), r=[t0, t1], w=[nr])
        P.op("dve", lambda e: e.tensor_tensor(out=t0[:], in0=F[:, :, 1], in1=ar[:], op=ALU.mult), r=[F, ar], w=[t0])
        P.op("dve", lambda e: e.tensor_tensor(out=t1[:], in0=F[:, :, 0], in1=ai[:], op=ALU.mult), r=[F, ai], w=[t1])
        P.op("dve", lambda e: e.tensor_tensor(out=ni[:], in0=t0[:], in1=t1[:], op=ALU.add), r=[t0, t1], w=[ni])
        P.op("dve", lambda e: e.tensor_tensor(out=F[:, :, 0], in0=nr[:], in1=L[:, :, 0], op=ALU.add), r=[nr, L], w=[F])
        P.op("dve", lambda e: e.tensor_tensor(out=F[:, :, 1], in0=ni[:], in1=L[:, :, 1], op=ALU.add), r=[ni, L], w=[F])
    P.dma("sp", XId.t, SI[:], r=[SI], w=[XId])
    P.release(m0)


def final_norm_stage(P, C, Y, T, gfin_d):
    m0 = P.mark()
    grep_ = P.sb([4096], F32, "fn_g")
    P.dma("sp", grep_[:], gfin_d, w=[grep_])
    xt = [P.sb([4096], F32, "fn_x%d" % i) for i in range(2)]
    jk = P.sb([4096], BF16, "fn_j")
    ss = P.sb([1], F32, "fn_ss")
    rs = P.sb([1], F32, "fn_rs")
    for tt in range(T // 128):
        x_ = xt[tt % 2]
        rows = slice(tt * 128, (tt + 1) * 128)
        P.dma("sp", x_[:], Y[rows, :], r=[Y], w=[x_])
        P.op("dve", lambda e: e.memset(ss[:], 0.0), w=[ss])
        P.op("act", lambda e, x_=x_: e.activation(out=jk[:], in_=x_[:], func=AF.Square, accum_out=ss[:]), r=[x_, ss], w=[jk, ss])
        P.op("act", lambda e: e.activation(out=rs[:], in_=ss[:], func=AF.Sqrt, scale=1.0 / D, bias=C["eps"][:]), r=[ss, C["eps"]], w=[rs])
        P.op("dve", lambda e: e.reciprocal(out=rs[:], in_=rs[:]), r=[rs], w=[rs])
        P.op("dve", lambda e, x_=x_: e.scalar_tensor_tensor(out=x_[:], in0=x_[:], scalar=rs[:], in1=grep_[:], op0=ALU.mult, op1=ALU.mult),
             r=[x_, rs, grep_], w=[x_])
        P.dma("sp", Y[rows, :], x_[:], r=[x_], w=[Y])
    P.release(m0)


WEIGHTS = [("ab_w_in", 4096, 10240), ("b_w_glu", 2048, 2048), ("ab_w_out", 4096, 4096),
           ("wq0", 4096, 2048), ("ut0", 4096, 16384), ("v0", 16384, 4096),
           ("c_w_in", 4096, 24576), ("c_w_out", 8192, 4096),
           ("wq1", 4096, 2048), ("ut1", 4096, 16384), ("v1", 16384, 4096)]


PART_W = {0: ("ab_w_in", "b_w_glu", "ab_w_out", "wq0", "ut0", "v0"),
          1: ("c_w_in", "c_w_out", "wq1", "ut1", "v1"),
          "1a": ("c_w_in", "c_w_out"), "1b": ("wq1", "ut1", "v1")}


def part_weights(parts):
    names = [n for p in parts for n in PART_W[p]]
    return [w for w in WEIGHTS if w[0] in names]


def build_full(T, ncores, seg_per_batch, parts=(0, 1)):
    nc = bass.Bass("TRN2", target_bir_lowering=False)
    dt_in = lambda name, shape: nc.dram_tensor(name, list(shape), F32, kind="ExternalInput").ap()
    x_in = dt_in("x", [T, 4096])
    y = nc.dram_tensor("y", [T, 4096], F32, kind="ExternalOutput").ap()
    cnp, cols = make_consts(T)
    consts_d = dt_in("consts", cnp.shape)
    rope_d = dt_in("rope", [128, 4 * T])
    rc = {"decT": dt_in("decT", [128, 2048]), "qdec": dt_in("qdec", [128, 1024]), "kdec": dt_in("kdec", [128, 16])}
    sel_d = dt_in("sel", [128, 8])
    gams = [dt_in("gam%d" % i, [128, 32]) for i in range(4)]
    gfin_d = dt_in("gfin", [128, 4096])
    lbp_d = dt_in("lbp", [128, 2, 16])
    lbrow_d = dt_in("lbrow", [128, 2, 2048])
    prm = {k: dt_in("s5_" + k, v) for k, v in S5_SHAPES.items()}
    keys = [dt_in("keys%d" % i, [128, 16, 128]) for i in range(2)]
    with ExitStack() as st:
        P = Prog(nc, st)
        C = setup_consts(P)
        load_consts(P, C, consts_d, cols)
        sel = P.sb([8], F32, "sel")
        P.dma("sp", sel[:], sel_d, w=[sel])
        Y = Buf(y, "Y")
        for r0 in range(0, T, 256):
            P.dma("sp", y[r0:r0 + 256, :], x_in[r0:r0 + 256, :], w=[Y])
        W = {}
        for name, K, N in part_weights(parts):
            W[name] = Weight(P, name, K, N, ncores)
            W[name].distribute(P)
        NCH = T // 64
        if 0 in parts:
            build_layer0(P, C, Y, T, ncores, seg_per_batch, W, gams, lbp_d, lbrow_d, prm, keys, sel)
        if 1 in parts or "1a" in parts or "1b" in parts:
            build_layer1(P, C, Y, T, ncores, seg_per_batch, W, gams, rope_d, rc, keys, sel, gfin_d,
                         mixer=(1 in parts or "1a" in parts), ffn=(1 in parts or "1b" in parts))
        stats = P.emit()
    return nc, cnp, stats


def build_layer0(P, C, Y, T, ncores, seg_per_batch, W, gams, lbp_d, lbrow_d, prm, keys, sel):
    if True:
        NCH = T // 64
        ZF = P.dram("ZF0", [8192, T], F32)
        ZT = P.dram("ZT0", [T, 4096], F32)
        KH = P.dram("KH", [T, 2048], BF16)
        VB = P.dram("VB", [T, 2048], BF16)
        QT = P.dram("QT", [2048, T], BF16)
        KT = P.dram("KT", [2048, T], BF16)
        DECd = P.dram("DEC", [128, 16 * NCH], F32)
        SLd = P.dram("SL", [128, 2064], F32)
        SLall = P.dram("SLall", [ncores * 128, 2064], F32)
        SId = P.dram("SI", [128, 2048], F32)
        SXd = P.dram("SX", [128, 128], F32)
        SXall = P.dram("SXall", [ncores * 128, 128], F32)
        XId = P.dram("XI", [128, 128], F32)
        Y2 = P.dram("Y2", [2048, T], F32)
        OT0 = P.dram("OT0", [4096, T], BF16)
        inproj_stage(P, C, Y, T, gams[0], W["ab_w_in"].full,
                     [(0, 2048, ZF, 0, AF.Silu), (2048, 2048, ZF, 2048, AF.Sigmoid),
                      (6144, 2048, ZF, 4096, AF.Silu), (8192, 2048, ZF, 6144, AF.Copy)],
                     [(2048, 2048, ZT, 0, AF.Sigmoid), (4096, 2048, ZT, 2048, AF.Copy)])
        hgrn2_pass1(P, C, T, ZF, ZT, lbp_d, lbrow_d, KH, VB, QT, KT, DECd, SLd)
        SXd3 = Buf(SXd.t.rearrange("p (m two) -> p m two", two=2), "SX3")
        XId3 = Buf(XId.t.rearrange("p (m two) -> p m two", two=2), "XI3")
        s5_scan(P, C, T, ZF, prm, None, SXd3, None)
        P.barrier()
        SXd.w = SXd3.w
        gather_states(P, SLd, SLall, ncores)
        gather_states(P, SXd, SXall, ncores)
        hgrn2_combine(P, C, SLall, sel, SId, ncores, seg_per_batch)
        s5_combine(P, C, T, prm, SXall, sel, XId, ncores, seg_per_batch)
        XId3.w = XId.w
        hgrn2_pass2(P, C, T, ZF, KH, VB, QT, KT, DECd, SId, OT0)
        s5_scan(P, C, T, ZF, prm, XId3, None, Y2)
        glu_stage(P, C, T, Y2, W["b_w_glu"].full, OT0, 2048)
        outproj_stage(P, C, Y, T, OT0, W["ab_w_out"].full, 4096)
        peer_stage(P, C, Y, T, gams[1], W["wq0"].full, keys[0], W["ut0"].full, W["v0"].full)


def build_layer1(P, C, Y, T, ncores, seg_per_batch, W, gams, rope_d, rc, keys, sel, gfin_d, mixer=True, ffn=True):
    if mixer:
        ZF1 = P.dram("ZF1", [16384, T], F32)
        ZT1 = P.dram("ZT1", [T, 12288], F32)
        KD = P.dram("KD", [T, 4096], BF16)
        VB1 = P.dram("VB1", [T, 8192], BF16)
        QR = P.dram("QR", [4096, T], BF16)
        KR = P.dram("KR", [4096, T], BF16)
        QD = P.dram("QD", [4096, T], BF16)
        RL2 = P.dram("RL", [128, 16384], F32)
        RLall = P.dram("RLall", [ncores * 128, 16384], F32)
        RI2 = P.dram("RI", [128, 16384], F32)
        RLd = Buf(RL2.t.rearrange("p (h d e) -> p h d e", h=16, d=2), "RL4")
        RId = Buf(RI2.t.rearrange("p (h d e) -> p h d e", h=16, d=2), "RI4")
        OT1 = P.dram("OT1", [8192, T], BF16)
        inproj_stage(P, C, Y, T, gams[2], W["c_w_in"].full,
                     [(0, 4096, ZF1, 0, AF.Copy), (4096, 4096, ZF1, 4096, AF.Copy), (16384, 8192, ZF1, 8192, AF.Silu)],
                     [(4096, 4096, ZT1, 0, AF.Copy), (8192, 8192, ZT1, 4096, AF.Copy)])
        ret_pass1(P, C, T, ZF1, ZT1, rope_d, rc, KD, VB1, QR, KR, QD, RLd)
        P.barrier()
        RL2.w = RLd.w
        gather_states(P, RL2, RLall, ncores)
        ret_combine(P, C, T, RLall, sel, RId, ncores, seg_per_batch)
        ret_pass2(P, C, T, ZF1, rc, KD, VB1, QR, KR, QD, RId, OT1)
        outproj_stage(P, C, Y, T, OT1, W["c_w_out"].full, 8192)
    if ffn:
        peer_stage(P, C, Y, T, gams[3], W["wq1"].full, keys[1], W["ut1"].full, W["v1"].full)
        final_norm_stage(P, C, Y, T, gfin_d)


def host_inputs(inp, T, ncores, seg_per_batch, cnp, parts=(0, 1), x_override=None):
    f = lambda a: np.ascontiguousarray(np.asarray(a, dtype=np.float32))
    x = f(inp["x"] if x_override is None else x_override).reshape(-1, 4096)
    gl = lambda g: f(np.asarray(g).reshape(32, 128).T)
    decT, qdec, kdec = make_ret_consts()
    common = {"consts": cnp, "decT": decT, "qdec": qdec, "kdec": kdec,
              "gam0": gl(inp["mix_norm_g"][0]), "gam1": gl(inp["ffn_norm_g"][0]),
              "gam2": gl(inp["mix_norm_g"][1]), "gam3": gl(inp["ffn_norm_g"][1]),
              "gfin": f(np.broadcast_to(np.asarray(inp["final_norm_g"])[None, :], (128, 4096)))}
    lbparam = np.asarray(inp["a_lb_param"], np.float32)
    common["lbp"] = f(lbparam.reshape(2, 16, 128).transpose(2, 0, 1))
    common["lbrow"] = f(np.broadcast_to(lbparam[None], (128, 2, 2048)))
    hp = s5_host_params(*(np.asarray(inp[k][0], np.float32) for k in
                          ("b_a_re", "b_a_im", "b_log_dt", "b_b_re", "b_b_im", "b_c_re", "b_c_im", "b_d")))
    common.update({"s5_" + k: v for k, v in hp.items()})
    for l in range(2):
        keys = np.asarray(inp["peer_sub_keys"][l], np.float32)
        common["keys%d" % l] = f(keys.reshape(16, 128, 128).transpose(2, 0, 1))
    wsrc = {k: inp[k][0] for k in ("ab_w_in", "b_w_glu", "ab_w_out", "c_w_in", "c_w_out") if k in inp}
    shards = {}
    for name, K, N in part_weights(parts):
        if name.startswith("wq"):
            Wm = np.asarray(inp["peer_w_q"][int(name[2])], np.float32)
        elif name.startswith("ut"):
            Wm = np.ascontiguousarray(np.asarray(inp["peer_u"][int(name[2])], np.float32).T)
        elif name.startswith("v") and name[1:].isdigit():
            Wm = np.asarray(inp["peer_v"][int(name[1])], np.float32)
        else:
            Wm = np.asarray(wsrc[name], np.float32)
        assert Wm.shape == (K, N), (name, Wm.shape)
        shards[name] = shard_rows(Wm, ncores)
        del Wm
    maps = []
    for r in range(ncores):
        m = dict(common)
        m["x"] = f(x[r * T:(r + 1) * T])
        seg = r % seg_per_batch
        m["rope"] = make_rope(seg * T, T)
        s_ = np.zeros((128, 8), np.float32)
        s_[:, r] = 1.0
        m["sel"] = s_
        for name, _, _ in part_weights(parts):
            m[name] = shards[name][r]
        maps.append(m)
    return maps


W_NS = {"ab_w_in": 512, "b_w_glu": 512, "ab_w_out": 512, "wq0": 512, "ut0": 512, "v0": None,
        "c_w_in": 512, "c_w_out": 256, "wq1": 512, "ut1": 512, "v1": None}
LAUNCH_W = {"A0": ("ab_w_in",), "B0": ("ab_w_in", "b_w_glu", "ab_w_out", "wq0", "ut0", "v0"),
            "A1": ("c_w_in",), "B1a": ("c_w_in", "c_w_out"), "B1b": ("wq1", "ut1", "v1")}


def build_launch(T, mode, ncores=8, spb=4):
    nc = bass.Bass("TRN2", target_bir_lowering=False)
    dt_in = lambda name, shape: nc.dram_tensor(name, list(shape), F32, kind="ExternalInput").ap()
    dt_out = lambda name, shape: nc.dram_tensor(name, list(shape), F32, kind="ExternalOutput").ap()
    x_in = dt_in("x", [T, 4096])
    cnp, cols = make_consts(T)
    consts_d = dt_in("consts", cnp.shape)
    NCH = T // 64
    with ExitStack() as st:
        P = Prog(nc, st)
        C = setup_consts(P)
        load_consts(P, C, consts_d, cols)
        W = {}
        for name, K, N in WEIGHTS:
            if name in LAUNCH_W[mode]:
                W[name] = Weight(P, name, K, N, 1, W_NS[name])
                W[name].distribute(P)
        if mode in ("B0", "B1a", "B1b"):
            y = dt_out("y", [T, 4096])
            Y = Buf(y, "Y")
            for r0 in range(0, T, 256):
                P.dma("sp", y[r0:r0 + 256, :], x_in[r0:r0 + 256, :], w=[Y])
        else:
            Y = Buf(x_in, "Xin")
        if mode in ("B0", "B1a"):
            sel = P.sb([8], F32, "sel")
            P.dma("sp", sel[:], dt_in("sel", [128, 8]), w=[sel])
        if mode in ("A0", "B0"):
            gam0 = dt_in("gam0", [128, 32])
            lbp_d = dt_in("lbp", [128, 2, 16])
            lbrow_d = dt_in("lbrow", [128, 2, 2048])
            prm = {k: dt_in("s5_" + k, v) for k, v in S5_SHAPES.items()}
            ZF = P.dram("ZF0", [8192, T], F32)
            ZT = P.dram("ZT0", [T, 4096], F32)
            KH = P.dram("KH", [T, 2048], BF16)
            VB = P.dram("VB", [T, 2048], BF16)
            QT = P.dram("QT", [2048, T], BF16)
            KT = P.dram("KT", [2048, T], BF16)
            DECd = P.dram("DEC", [128, 16 * NCH], F32)
            if mode == "A0":
                SLd = Buf(dt_out("sl", [128, 2064]), "SL")
                SXd3 = Buf(dt_out("sx", [128, 64, 2]), "SX")
            else:
                SLd = P.dram("SL", [128, 2064], F32)
            inproj_stage(P, C, Y, T, gam0, W["ab_w_in"].full,
                         [(0, 2048, ZF, 0, AF.Silu), (2048, 2048, ZF, 2048, AF.Sigmoid),
                          (6144, 2048, ZF, 4096, AF.Silu), (8192, 2048, ZF, 6144, AF.Copy)],
                         [(2048, 2048, ZT, 0, AF.Sigmoid), (4096, 2048, ZT, 2048, AF.Copy)])
            hgrn2_pass1(P, C, T, ZF, ZT, lbp_d, lbrow_d, KH, VB, QT, KT, DECd, SLd)
            if mode == "A0":
                s5_scan(P, C, T, ZF, prm, None, SXd3, None)
            else:
                gam1 = dt_in("gam1", [128, 32])
                keys0 = dt_in("keys0", [128, 16, 128])
                SLall = Buf(dt_in("slall", [ncores * 128, 2064]), "SLall")
                SXall = Buf(dt_in("sxall", [ncores * 128, 128]), "SXall")
                SId = P.dram("SI", [128, 2048], F32)
                XId = P.dram("XI", [128, 128], F32)
                XId3 = Buf(XId.t.rearrange("p (m two) -> p m two", two=2), "XI3")
                Y2 = P.dram("Y2", [2048, T], F32)
                OT0 = P.dram("OT0", [4096, T], BF16)
                hgrn2_combine(P, C, SLall, sel, SId, ncores, spb)
                s5_combine(P, C, T, prm, SXall, sel, XId, ncores, spb)
                hgrn2_pass2(P, C, T, ZF, KH, VB, QT, KT, DECd, SId, OT0)
                s5_scan(P, C, T, ZF, prm, XId3, None, Y2)
                glu_stage(P, C, T, Y2, W["b_w_glu"].full, OT0, 2048)
                outproj_stage(P, C, Y, T, OT0, W["ab_w_out"].full, 4096)
                peer_stage(P, C, Y, T, gam1, W["wq0"].full, keys0, W["ut0"].full, W["v0"].full)
        if mode in ("A1", "B1a"):
            gam2 = dt_in("gam2", [128, 32])
            rope_d = dt_in("rope", [128, 4 * T])
            rc = {"decT": dt_in("decT", [128, 2048]), "qdec": dt_in("qdec", [128, 1024]), "kdec": dt_in("kdec", [128, 16])}
            ZF1 = P.dram("ZF1", [16384, T], F32)
            ZT1 = P.dram("ZT1", [T, 12288], F32)
            KD = P.dram("KD", [T, 4096], BF16)
            VB1 = P.dram("VB1", [T, 8192], BF16)
            QR = P.dram("QR", [4096, T], BF16)
            KR = P.dram("KR", [4096, T], BF16)
            QD = P.dram("QD", [4096, T], BF16)
            if mode == "A1":
                rl_ap = dt_out("rl", [128, 16384])
            else:
                rl_ap = P.dram("RL", [128, 16384], F32).t
            RLd = Buf(rl_ap.rearrange("p (h d e) -> p h d e", h=16, d=2), "RL4")
            inproj_stage(P, C, Y, T, gam2, W["c_w_in"].full,
                         [(0, 4096, ZF1, 0, AF.Copy), (4096, 4096, ZF1, 4096, AF.Copy), (16384, 8192, ZF1, 8192, AF.Silu)],
                         [(4096, 4096, ZT1, 0, AF.Copy), (8192, 8192, ZT1, 4096, AF.Copy)])
            ret_pass1(P, C, T, ZF1, ZT1, rope_d, rc, KD, VB1, QR, KR, QD, RLd)
            if mode == "B1a":
                RLall = Buf(dt_in("rlall", [ncores * 128, 16384]), "RLall")
                RI2 = P.dram("RI", [128, 16384], F32)
                RId = Buf(RI2.t.rearrange("p (h d e) -> p h d e", h=16, d=2), "RI4")
                OT1 = P.dram("OT1", [8192, T], BF16)
                ret_combine(P, C, T, RLall, sel, RId, ncores, spb)
                ret_pass2(P, C, T, ZF1, rc, KD, VB1, QR, KR, QD, RId, OT1)
                outproj_stage(P, C, Y, T, OT1, W["c_w_out"].full, 8192)
        if mode == "B1b":
            gam3 = dt_in("gam3", [128, 32])
            keys1 = dt_in("keys1", [128, 16, 128])
            gfin_d = dt_in("gfin", [128, 4096])
            peer_stage(P, C, Y, T, gam3, W["wq1"].full, keys1, W["ut1"].full, W["v1"].full)
            final_norm_stage(P, C, Y, T, gfin_d)
        P.barrier()
        stats = P.emit()
    return nc, cnp, stats


def launch_inputs(inp, T, mode, cnp, xcur, extra, ncores=8, spb=4):
    f = lambda a: np.ascontiguousarray(np.asarray(a, dtype=np.float32))
    gl = lambda g: f(np.asarray(g).reshape(32, 128).T)
    x = xcur.reshape(-1, 4096)
    common = {"consts": cnp}
    if mode in ("A0", "B0"):
        common["gam0"] = gl(inp["mix_norm_g"][0])
        lbparam = np.asarray(inp["a_lb_param"], np.float32)
        common["lbp"] = f(lbparam.reshape(2, 16, 128).transpose(2, 0, 1))
        common["lbrow"] = f(np.broadcast_to(lbparam[None], (128, 2, 2048)))
        hp = s5_host_params(*(np.asarray(inp[k][0], np.float32) for k in
                              ("b_a_re", "b_a_im", "b_log_dt", "b_b_re", "b_b_im", "b_c_re", "b_c_im", "b_d")))
        common.update({"s5_" + k: v for k, v in hp.items()})
        common["ab_w_in"] = tile_w(f(inp["ab_w_in"][0]), 512)
    if mode == "B0":
        common["gam1"] = gl(inp["ffn_norm_g"][0])
        common["keys0"] = f(np.asarray(inp["peer_sub_keys"][0], np.float32).reshape(16, 128, 128).transpose(2, 0, 1))
        common["b_w_glu"] = tile_w(f(inp["b_w_glu"][0]), 512)
        common["ab_w_out"] = tile_w(f(inp["ab_w_out"][0]), 512)
        common["wq0"] = tile_w(f(inp["peer_w_q"][0]), 512)
        common["ut0"] = tile_w(f(np.asarray(inp["peer_u"][0], np.float32).T), 512)
        common["v0"] = f(inp["peer_v"][0])
        common["slall"] = extra["slall"]
        common["sxall"] = extra["sxall"]
    if mode in ("A1", "B1a"):
        decT, qdec, kdec = make_ret_consts()
        common.update({"gam2": gl(inp["mix_norm_g"][1]), "decT": decT, "qdec": qdec, "kdec": kdec,
                       "c_w_in": tile_w(f(inp["c_w_in"][0]), 512)})
    if mode == "B1a":
        common["c_w_out"] = tile_w(f(inp["c_w_out"][0]), 256)
        common["rlall"] = extra["rlall"]
    if mode == "B1b":
        common["gam3"] = gl(inp["ffn_norm_g"][1])
        common["keys1"] = f(np.asarray(inp["peer_sub_keys"][1], np.float32).reshape(16, 128, 128).transpose(2, 0, 1))
        common["gfin"] = f(np.broadcast_to(np.asarray(inp["final_norm_g"])[None, :], (128, 4096)))
        common["wq1"] = tile_w(f(inp["peer_w_q"][1]), 512)
        common["ut1"] = tile_w(f(np.asarray(inp["peer_u"][1], np.float32).T), 512)
        common["v1"] = f(inp["peer_v"][1])
    maps = []
    for r in range(ncores):
        m = dict(common)
        m["x"] = f(x[r * T:(r + 1) * T])
        if mode in ("A1", "B1a"):
            m["rope"] = make_rope((r % spb) * T, T)
        if mode in ("B0", "B1a"):
            s_ = np.zeros((128, 8), np.float32)
            s_[:, r] = 1.0
            m["sel"] = s_
        maps.append(m)
    return maps


_CACHE = {}


def _run(mode, inputs, xcur, extra, T=2048):
    if mode not in _CACHE:
        _CACHE[mode] = build_launch(T, mode)
    nc, cnp, _ = _CACHE[mode]
    maps = launch_inputs(inputs, T, mode, cnp, xcur, extra)
    res = run_bass_kernel_spmd(nc, maps, core_ids=list(range(8)))
    del maps
    return res.results


def kernel(**inputs):
    x = np.ascontiguousarray(np.asarray(inputs["x"], np.float32)).reshape(-1, 4096)
    cat = lambda rs, k: np.ascontiguousarray(np.concatenate([np.asarray(r[k], np.float32).reshape(128, -1) for r in rs], axis=0))
    rs = _run("A0", inputs, x, None)
    extra = {"slall": cat(rs, "sl"), "sxall": cat(rs, "sx")}
    rs = _run("B0", inputs, x, extra)
    x = np.concatenate([np.asarray(r["y"], np.float32) for r in rs], axis=0)
    rs = _run("A1", inputs, x, None)
    extra = {"rlall": cat(rs, "rl")}
    rs = _run("B1a", inputs, x, extra)
    x = np.concatenate([np.asarray(r["y"], np.float32) for r in rs], axis=0)
    rs = _run("B1b", inputs, x, None)
    x = np.concatenate([np.asarray(r["y"], np.float32) for r in rs], axis=0)
    return x.reshape(2, 8192, 4096)
```

```python
from contextlib import ExitStack
import math
import numpy as np
import concourse.bass as bass
import concourse.mybir as mybir
from concourse.bass_utils import run_bass_kernel_spmd

F32 = mybir.dt.float32
BF16 = mybir.dt.bfloat16
ALU = mybir.AluOpType
AF = mybir.ActivationFunctionType
AX = mybir.AxisListType

D = 4096
EPS = 1e-6
N_DMA_SEMS = 20
SB_WORDS = 53000
NEG = -1.0e30


class Buf:
    def __init__(self, ap, name):
        self.t = ap
        self.name = name
        self.w = None
        self.r = []

    def __getitem__(self, idx):
        return self.t[idx]


class Op:
    __slots__ = ("eng", "fn", "deps", "kind", "sem", "val", "need_inc", "prev_same_sem")

    def __init__(self, eng, fn, kind):
        self.eng = eng
        self.fn = fn
        self.kind = kind
        self.deps = []
        self.sem = None
        self.val = None
        self.need_inc = False
        self.prev_same_sem = None


class Prog:
    ENGS = ("pe", "act", "dve", "pool", "sp")

    def __init__(self, nc, stack):
        self.nc = nc
        self.stack = stack
        self.ops = []
        self.eobj = {"pe": nc.tensor, "act": nc.scalar, "dve": nc.vector,
                     "pool": nc.gpsimd, "sp": nc.sync}
        self.esets = [{e: stack.enter_context(nc.semaphore("s%d_%s" % (i, e))) for e in self.ENGS[:4]}
                      for i in range(4)]
        self.nds = {"sp": 76, "pool": 8}
        self.dsem = {q: [stack.enter_context(nc.semaphore("d_%s%d" % (q, i)))
                         for i in range(n)] for q, n in self.nds.items()}
        self.ccsem = stack.enter_context(nc.semaphore("s_cc"))
        self.cccount = 0
        self.dcount = {q: 0 for q in self.dsem}
        self.dlast = {}
        self.last = {}
        self.pending_dma = []
        self.big = stack.enter_context(nc.sbuf_tensor("arena", [128, SB_WORDS], F32))
        self.off = 0
        self.nb = 0
        self.ps = [Buf(stack.enter_context(nc.psum_tensor("psb%d" % i, [128, 512], F32)), "ps%d" % i)
                   for i in range(8)]

    def sb(self, free, dtype, name=None):
        n = int(np.prod(free))
        words = (n * (2 if dtype == BF16 else 4) + 3) // 4
        words = (words + 7) // 8 * 8
        assert self.off + words <= SB_WORDS, ("SBUF arena overflow", self.off, words)
        ap = self.big[:, self.off:self.off + words]
        self.off += words
        if dtype != F32:
            ap = ap.bitcast(dtype)
        ap = ap[:, 0:n]
        if len(free) == 2:
            ap = ap.rearrange("p (a b) -> p a b", a=free[0])
        elif len(free) == 3:
            ap = ap.rearrange("p (a b c) -> p a b c", a=free[0], b=free[1])
        self.nb += 1
        return Buf(ap, name or "b%d" % self.nb)

    def mark(self):
        return self.off

    def release(self, m):
        self.barrier()
        self.off = m

    def dram(self, name, shape, dtype):
        t = self.nc.dram_tensor(name, list(shape), dtype)
        b = Buf(t.ap(), name)
        b.handle = t
        return b

    def _track(self, op, r, w):
        deps = []
        for b in r:
            if b.w is not None:
                deps.append(b.w)
        for b in w:
            if b.w is not None:
                deps.append(b.w)
            for x in b.r:
                if x.eng == op.eng and x.kind == "c" and op.kind == "c":
                    continue
                deps.append(x)
        op.deps = [d for d in deps if not (d.eng == "pe" and op.eng == "pe"
                                           and d.kind == "c" and op.kind == "c")]
        for d in op.deps:
            d.need_inc = True
        for b in r:
            if op.kind == "c":
                b.r = [x for x in b.r if not (x.kind == "c" and x.eng == op.eng)]
            b.r.append(op)
        for b in w:
            b.w = op
            b.r = []

    def op(self, eng, fn, r=(), w=()):
        o = Op(eng, fn, "c")
        self._track(o, r, w)
        self.ops.append(o)
        self.last[eng] = o
        return o

    def dma(self, q, out, in_, r=(), w=()):
        o = Op(q, (out, in_), "d")
        k = self.dcount[q]
        self.dcount[q] += 1
        slot = k % self.nds[q]
        o.sem = self.dsem[q][slot]
        o.val = 16 * (k // self.nds[q] + 1)
        o.prev_same_sem = self.dlast.get((q, slot))
        self.dlast[(q, slot)] = o
        o.need_inc = True
        self._track(o, r, w)
        self.ops.append(o)
        self.pending_dma.append(o)
        return o

    def collective(self, kind, in_ap, out_ap, groups, r=(), w=()):
        o = Op("pool", (kind, in_ap, out_ap, groups), "cc")
        self.cccount += 1
        o.sem = self.ccsem
        o.val = self.cccount
        o.need_inc = True
        self._track(o, r, w)
        self.ops.append(o)
        self.pending_dma.append(o)
        return o

    def barrier(self):
        srcs = list(self.last.values()) + self.pending_dma
        for s in srcs:
            s.need_inc = True
        for e in self.ENGS:
            o = Op(e, None, "b")
            o.deps = list(srcs)
            self.ops.append(o)
        self.pending_dma = []

    def emit(self, final_wait_ops=()):
        cnt = {e: 0 for e in self.ENGS}
        epoch = 0
        tot = {e: 0 for e in self.ENGS}
        prev_b = False
        for o in self.ops:
            if o.kind == "b":
                prev_b = True
                continue
            if prev_b:
                prev_b = False
                if max(cnt.values()) > 14000 and epoch + 1 < len(self.esets):
                    epoch += 1
                    cnt = {e: 0 for e in self.ENGS}
            if o.kind == "c" and o.need_inc:
                cnt[o.eng] += 1
                tot[o.eng] += 1
                o.sem = self.esets[epoch][o.eng]
                o.val = cnt[o.eng]
        cnt = tot
        seen = {e: {} for e in self.ENGS}
        nwait = 0
        for o in self.ops:
            e = self.eobj[o.eng]
            sn = seen[o.eng]
            deps = list(o.deps)
            if o.kind == "d" and o.prev_same_sem is not None:
                deps.append(o.prev_same_sem)
            need = {}
            for d in deps:
                key = id(d.sem)
                if sn.get(key, 0) >= d.val:
                    continue
                if key not in need or need[key][1] < d.val:
                    need[key] = (d.sem, d.val)
            for key, (sem, val) in need.items():
                e.wait_ge(sem, val)
                sn[key] = val
                nwait += 1
            if o.kind == "d":
                out, in_ = o.fn
                e.dma_start(out=out, in_=in_).then_inc(o.sem, 16)
            elif o.kind == "cc":
                kind, in_ap, out_ap, groups = o.fn
                e.collective_compute(kind, ALU.bypass, replica_groups=groups,
                                     ins=[in_ap], outs=[out_ap]).then_inc(o.sem)
            elif o.kind == "c":
                ins = o.fn(e)
                if o.need_inc:
                    ins.then_inc(o.sem, 1)
        for o in final_wait_ops:
            e = self.eobj[o.eng]
            if seen[o.eng].get(id(o.sem), 0) < o.val:
                e.wait_ge(o.sem, o.val)
                seen[o.eng][id(o.sem)] = o.val
        return dict(n_ops=len(self.ops), n_wait=nwait, cnt=cnt, dma=dict(self.dcount), cc=self.cccount)


def piece_rows(K, N, ncores):
    kp = K
    while kp * N * 2 > (16 << 20) and kp % 2 == 0 and (kp // 2) % ncores == 0 and (kp // 2) >= ncores:
        kp //= 2
    return kp


def shard_rows(W, ncores):
    K, N = W.shape
    if ncores == 1:
        return [np.ascontiguousarray(W)]
    kp = piece_rows(K, N, ncores)
    npc = K // kp
    sub = kp // ncores
    Wr = W.reshape(npc, ncores, sub, N)
    return [np.ascontiguousarray(Wr[:, r].reshape(npc * sub, N)) for r in range(ncores)]


def tile_w(W, NS):
    K, N = W.shape
    KT = K // 128
    return np.ascontiguousarray(W.reshape(KT, 128, N // NS, NS).transpose(2, 1, 0, 3).reshape((N // NS) * 128, KT * NS))


class Weight:
    def __init__(self, P, name, K, N, ncores=1, NS=None):
        self.K, self.N, self.name, self.NS = K, N, name, NS
        nc = P.nc
        if NS is None:
            self.shape = [K, N]
        else:
            self.shape = [(N // NS) * 128, (K // 128) * NS]
        self.src = nc.dram_tensor(name, self.shape, F32, kind="ExternalInput").ap()
        self.full = P.dram(name + "_bf", self.shape, BF16)
        self.full.NS = NS

    def distribute(self, P):
        rows, cols = self.shape
        step = max(1, min(rows, (4 << 20) // (4 * cols)))
        for r0 in range(0, rows, step):
            r1 = min(rows, r0 + step)
            P.dma("pool", self.full[r0:r1, :], self.src[r0:r1, :], w=[self.full])


def load_wslab(P, dst, W, n0, n1, KT, r_extra=()):
    NS = W.NS
    assert n0 % NS == 0 and n1 - n0 == NS, (n0, n1, NS)
    s_ = n0 // NS
    src = W.t[s_ * 128:(s_ + 1) * 128, :].rearrange("p (kt n) -> p kt n", kt=KT)
    h = KT // 2
    P.dma("sp", dst[:, 0:h, 0:NS], src[:, 0:h, :], r=[W], w=[dst])
    P.dma("sp", dst[:, h:KT, 0:NS], src[:, h:KT, :], r=[W], w=[dst])


def norm_to_T(P, C, Y, r0, gam_sb, xt, hn, hnT, col0, acc=None, stat=None):
    P.dma("sp", xt[:], Y[r0:r0 + 128, :], r=[Y], w=[xt])
    ss, rs = stat
    P.op("dve", lambda e: e.memset(ss[:], 0.0), w=[ss])
    P.op("act", lambda e: e.activation(out=hn[:], in_=xt[:], func=AF.Square, accum_out=ss[:]),
         r=[xt, ss], w=[hn, ss])
    P.op("act", lambda e: e.activation(out=rs[:], in_=ss[:], func=AF.Sqrt, scale=1.0 / D, bias=C["eps"][:]),
         r=[ss, C["eps"]], w=[rs])
    P.op("dve", lambda e: e.reciprocal(out=rs[:], in_=rs[:]), r=[rs], w=[rs])
    if acc is not None:
        P.op("pool", lambda e: e.tensor_copy(out=acc[:], in_=xt[:]), r=[xt], w=[acc])
    P.op("dve", lambda e: e.tensor_scalar(out=hn[:], in0=xt[:], scalar1=rs[:], scalar2=None, op0=ALU.mult),
         r=[xt, rs], w=[hn])
    ident = C["ident_bf"]
    for kb in range(4):
        pb = P.ps[kb % 2]
        pv = pb.t.bitcast(BF16).rearrange("p (a b) -> p a b", a=8)
        for k in range(8):
            kt = kb * 8 + k
            P.op("pe", lambda e, kt=kt, k=k, pv=pv: e.transpose(pv[:, k, :], hn[:, kt * 128:(kt + 1) * 128], ident[:]),
                 r=[hn, ident], w=[pb])
        for k in range(8):
            kt = kb * 8 + k
            P.op("act", lambda e, kt=kt, k=k, pv=pv: e.activation(
                out=hnT[:, kt, col0:col0 + 128], in_=pv[:, k, :], func=AF.Copy, scale=gam_sb[:, kt:kt + 1]),
                r=[pb, gam_sb], w=[hnT])


def setup_consts(P):
    C = {}
    idf = P.sb([128], F32, "idf")
    P.op("pool", lambda e: e.memset(idf[:], 0.0), w=[idf])
    P.op("pool", lambda e: e.affine_select(out=idf[:], in_=idf[:], pattern=[[-1, 128]],
                                           compare_op=ALU.not_equal, fill=1.0, base=0, channel_multiplier=1),
         r=[idf], w=[idf])
    ib = P.sb([128], BF16, "ident_bf")
    P.op("dve", lambda e: e.tensor_copy(out=ib[:], in_=idf[:]), r=[idf], w=[ib])
    eps = P.sb([1], F32, "eps")
    P.op("dve", lambda e: e.memset(eps[:], EPS), w=[eps])
    hp = P.sb([1], F32, "halfpi")
    P.op("dve", lambda e: e.memset(hp[:], float(np.pi / 2)), w=[hp])
    C["halfpi"] = hp
    C["ident_f"] = idf
    C["ident_bf"] = ib
    C["eps"] = eps
    return C


def peer_stage(P, C, Y, T, gam_d, Wq, keys_d, UT, V):
    TG = min(T, 256)
    NT = TG // 128
    m0 = P.mark()
    hnT = P.sb([32, TG], BF16, "hnT")
    acc = [P.sb([4096], F32, "acc%d" % i) for i in range(NT)]
    wA = P.sb([32, 512], BF16, "wA")
    wB = P.sb([4, 4096], BF16, "wB")
    qT = P.sb([16, TG], BF16, "qT")
    keysT = P.sb([16, 128], BF16, "keysT")
    gam = P.sb([32], F32, "gam")
    sc = [P.sb([16, 128], F32, "sc%d" % i) for i in range(NT)]
    tau = [P.sb([8], F32, "tau%d" % i) for i in range(NT)]
    nbias = [P.sb([8], F32, "nb%d" % i) for i in range(NT)]
    a16 = P.sb([16], F32, "a16")
    b16 = P.sb([16], F32, "b16")
    c16 = P.sb([16], F32, "c16")
    j16 = P.sb([16], F32, "j16")
    tmp128 = P.sb([128], F32, "tmp128")
    cand = P.sb([16, 16], F32, "cand")
    cand2 = P.sb([256], F32, "cand2")
    negm = P.sb([1], F32, "negm")
    zz = P.sb([1], F32, "zz")
    ss = P.sb([1], F32, "ss")
    rs = P.sb([1], F32, "rs")
    m1 = P.mark()
    xt = P.sb([4096], F32, "xt")
    hn = P.sb([4096], BF16, "hn")
    P.off = m1
    S = [P.sb([8, 2, 128], F32, "S%d" % i) for i in range(2)]
    E = [P.sb([8, 256], F32, "E%d" % i) for i in range(2)]
    Eh = [[Buf(E[i][:, h, :], "E%d_%d" % (i, h)) for h in range(8)] for i in range(2)]
    hidg = [P.sb([512], F32, "hidg%d" % i) for i in range(2)]
    G = [P.sb([256], F32, "G%d" % i) for i in range(2)]
    Wt = [P.sb([256], BF16, "W%d" % i) for i in range(2)]
    WT = [P.sb([2, 128], BF16, "WT%d" % i) for i in range(2 * NT)]

    P.dma("pool", keysT[:], keys_d, w=[keysT])
    P.dma("sp", gam[:], gam_d, w=[gam])
    ident = C["ident_bf"]
    psH = [P.ps[0], P.ps[1]]
    psT = P.ps[2]
    psT_v = psT.t.bitcast(BF16)[:, 0:256].rearrange("p (a b) -> p a b", a=2)
    psO = [[P.ps[3], P.ps[4]], [P.ps[5], P.ps[6]]]

    for g0 in range(0, T, TG):
        for tt in range(NT):
            norm_to_T(P, C, Y, g0 + tt * 128, gam, xt, hn, hnT, tt * 128, acc=acc[tt], stat=(ss, rs))
        for s in range(4):
            load_wslab(P, wA, Wq, s * 512, (s + 1) * 512, 32)
            for ft in range(4):
                pb = P.ps[3 + (s * 4 + ft) % 2]
                for kt in range(32):
                    P.op("pe", lambda e, kt=kt, ft=ft, pb=pb: e.matmul(
                        pb[:, 0:TG], wA[:, kt, ft * 128:(ft + 1) * 128], hnT[:, kt, :],
                        start=(kt == 0), stop=(kt == 31)), r=[wA, hnT], w=[pb])
                P.op("act", lambda e, s=s, ft=ft, pb=pb: e.activation(
                    out=qT[:, s * 4 + ft, :], in_=pb[:, 0:TG], func=AF.Copy), r=[pb], w=[qT])
        for tt in range(NT):
            for q4 in range(4):
                pb = P.ps[5 + q4 % 2]
                for k in range(4):
                    hp = q4 * 4 + k
                    P.op("pe", lambda e, hp=hp, k=k, pb=pb, tt=tt: e.matmul(
                        pb[:, k * 128:(k + 1) * 128], qT[:, hp, tt * 128:(tt + 1) * 128], keysT[:, hp, :],
                        start=True, stop=True), r=[qT, keysT], w=[pb])
                P.op("dve", lambda e, q4=q4, pb=pb, tt=tt: e.tensor_copy(
                    out=sc[tt][:, q4 * 4:(q4 + 1) * 4, :], in_=pb[:].rearrange("p (a b) -> p a b", a=4)),
                    r=[pb], w=[sc[tt]])
            for h in range(8):
                s1 = sc[tt][:, 2 * h, :]
                s2 = sc[tt][:, 2 * h + 1, :]
                for src, dst in ((s1, a16), (s2, b16)):
                    P.op("dve", lambda e, src=src, dst=dst: e.max(out=dst[:, 0:8], in_=src), r=[sc[tt]], w=[dst])
                    P.op("dve", lambda e, src=src, dst=dst: e.match_replace(
                        out=tmp128[:], in_to_replace=dst[:, 0:8], in_values=src, imm_value=NEG),
                        r=[sc[tt], dst], w=[tmp128])
                    P.op("dve", lambda e, dst=dst: e.max(out=dst[:, 8:16], in_=tmp128[:]), r=[tmp128], w=[dst])
                P.op("dve", lambda e: e.tensor_tensor(
                    out=cand[:], in0=a16[:].unsqueeze(2).to_broadcast([128, 16, 16]),
                    in1=b16[:].unsqueeze(1).to_broadcast([128, 16, 16]), op=ALU.add), r=[a16, b16], w=[cand])
                cf = cand[:].rearrange("p a b -> p (a b)")
                P.op("dve", lambda e, cf=cf: e.max(out=c16[:, 0:8], in_=cf), r=[cand], w=[c16])
                P.op("dve", lambda e, cf=cf: e.match_replace(out=cand2[:], in_to_replace=c16[:, 0:8],
                                                             in_values=cf, imm_value=NEG), r=[cand, c16], w=[cand2])
                P.op("dve", lambda e: e.max(out=c16[:, 8:16], in_=cand2[:]), r=[cand2], w=[c16])
                P.op("dve", lambda e, h=h, tt=tt: e.tensor_copy(out=tau[tt][:, h:h + 1], in_=c16[:, 15:16]),
                     r=[c16], w=[tau[tt]])
                P.op("dve", lambda e: e.tensor_scalar(out=negm[:], in0=c16[:, 0:1], scalar1=-1.0, scalar2=None,
                                                      op0=ALU.mult), r=[c16], w=[negm])
                P.op("dve", lambda e: e.memset(zz[:], 0.0), w=[zz])
                P.op("act", lambda e: e.activation(out=j16[:], in_=c16[:], func=AF.Exp, bias=negm[:], scale=1.0,
                                                   accum_out=zz[:]), r=[c16, negm, zz], w=[j16, zz])
                P.op("act", lambda e: e.activation(out=zz[:], in_=zz[:], func=AF.Ln), r=[zz], w=[zz])
                P.op("dve", lambda e, h=h, tt=tt: e.tensor_tensor(out=nbias[tt][:, h:h + 1], in0=negm[:], in1=zz[:],
                                                                  op=ALU.subtract), r=[negm, zz], w=[nbias[tt]])
        P.barrier()
        for c in range(32):
            e0 = c * 512
            load_wslab(P, wA, UT, e0, e0 + 512, 32)
            Vv = V.t.rearrange("(et p) d -> p et d", p=128)
            for et in range(4):
                P.dma("sp", wB[:, et, :], Vv[:, c * 4 + et, :], r=[V], w=[wB])
            for tt in range(NT):
                ph = psH[(c * NT + tt) % 2]
                hg = hidg[(c * NT + tt) % 2]
                for kt in range(32):
                    P.op("pe", lambda e, kt=kt, tt=tt, ph=ph: e.matmul(
                        ph[:], hnT[:, kt, tt * 128:(tt + 1) * 128], wA[:, kt, :],
                        start=(kt == 0), stop=(kt == 31)), r=[hnT, wA], w=[ph])
                P.op("act", lambda e, ph=ph, hg=hg: e.activation(out=hg[:], in_=ph[:], func=AF.Gelu), r=[ph], w=[hg])
            for tt in range(NT):
                hg = hidg[(c * NT + tt) % 2]
                scv = sc[tt][:].rearrange("p (h two) n -> p h two n", two=2)
                for hf in range(2):
                    i0 = c * 4 + hf * 2
                    Sb, Eb, Gb, Wb, WTb = S[hf], E[hf], G[hf], Wt[hf], WT[tt * 2 + hf]
                    Ehs = Eh[hf]
                    P.op("dve", lambda e, Sb=Sb, i0=i0, scv=scv: e.tensor_tensor(
                        out=Sb[:],
                        in0=scv[:, :, 0, i0:i0 + 2].unsqueeze(3).to_broadcast([128, 8, 2, 128]),
                        in1=scv[:, :, 1, :].unsqueeze(2).to_broadcast([128, 8, 2, 128]),
                        op=ALU.add), r=[sc[tt]], w=[Sb])
                    for h in range(8):
                        sv = Sb[:, h, :, :].rearrange("p a b -> p (a b)")
                        P.op("act", lambda e, sv=sv, Eb=Eb, h=h, tt=tt: e.activation(
                            out=Eb[:, h, :], in_=sv, func=AF.Exp, bias=nbias[tt][:, h:h + 1], scale=1.0),
                            r=[Sb, nbias[tt]], w=[Ehs[h]])
                        P.op("dve", lambda e, sv=sv, Eb=Eb, h=h, tt=tt: e.scalar_tensor_tensor(
                            out=Eb[:, h, :], in0=sv, scalar=tau[tt][:, h:h + 1], in1=Eb[:, h, :],
                            op0=ALU.is_ge, op1=ALU.mult), r=[Sb, Ehs[h], tau[tt]], w=[Ehs[h]])
                    P.op("pool", lambda e, Eb=Eb: e.tensor_tensor(
                        out=Eb[:, 0:4, :], in0=Eb[:, 0:4, :], in1=Eb[:, 4:8, :], op=ALU.add), r=Ehs, w=Ehs[0:4])
                    P.op("pool", lambda e, Eb=Eb: e.tensor_tensor(
                        out=Eb[:, 0:2, :], in0=Eb[:, 0:2, :], in1=Eb[:, 2:4, :], op=ALU.add), r=Ehs[0:4], w=Ehs[0:2])
                    P.op("pool", lambda e, Eb=Eb, Gb=Gb: e.tensor_tensor(
                        out=Gb[:], in0=Eb[:, 0, :], in1=Eb[:, 1, :], op=ALU.add), r=Ehs[0:2], w=[Gb])
                    P.op("dve", lambda e, Gb=Gb, Wb=Wb, hg=hg, hf=hf: e.tensor_tensor(
                        out=Wb[:], in0=Gb[:], in1=hg[:, hf * 256:(hf + 1) * 256], op=ALU.mult), r=[Gb, hg], w=[Wb])
                    for k in range(2):
                        P.op("pe", lambda e, k=k, Wb=Wb: e.transpose(psT_v[:, k, :], Wb[:, k * 128:(k + 1) * 128],
                                                                    ident[:]), r=[Wb, ident], w=[psT])
                    P.op("act", lambda e, WTb=WTb: e.activation(out=WTb[:], in_=psT_v, func=AF.Copy), r=[psT], w=[WTb])
            for tt in range(NT):
                for dq in range(4):
                    pair = psO[dq % 2]
                    for dc in range(2):
                        d0 = (dq * 2 + dc) * 512
                        for et in range(4):
                            wt_ = WT[tt * 2 + et // 2]
                            P.op("pe", lambda e, et=et, d0=d0, pb=pair[dc], wt_=wt_: e.matmul(
                                pb[:], wt_[:, et % 2, :], wB[:, et, d0:d0 + 512],
                                start=(et == 0), stop=(et == 3)), r=[wt_, wB], w=[pair[dc]])
                    for dc in range(2):
                        d0 = (dq * 2 + dc) * 512
                        P.op("dve", lambda e, d0=d0, pb=pair[dc], tt=tt: e.tensor_tensor(
                            out=acc[tt][:, d0:d0 + 512], in0=acc[tt][:, d0:d0 + 512], in1=pb[:], op=ALU.add),
                            r=[acc[tt], pair[dc]], w=[acc[tt]])
        for tt in range(NT):
            P.dma("sp", Y[g0 + tt * 128:g0 + (tt + 1) * 128, :], acc[tt][:], r=[acc[tt]], w=[Y])
        P.barrier()
    P.release(m0)


def inproj_stage(P, C, Y, T, gam_d, W, fm_specs, tm_specs):
    TB = min(T, 1024)
    m0 = P.mark()
    hnT = P.sb([32, TB], BF16, "ip_hnT")
    gam = P.sb([32], F32, "ip_gam")
    xt = P.sb([4096], F32, "ip_xt")
    hn = P.sb([4096], BF16, "ip_hn")
    ss = P.sb([1], F32, "ip_ss")
    rs = P.sb([1], F32, "ip_rs")
    slab = [P.sb([32, 512], BF16, "ip_slab%d" % i) for i in range(2)]
    stg = [P.sb([512], F32, "ip_stg%d" % i) for i in range(3)]
    P.dma("sp", gam[:], gam_d, w=[gam])
    banks = [P.ps[3], P.ps[4], P.ps[5], P.ps[6]]
    cnt = [0, 0, 0]
    for t0 in range(0, T, TB):
        for tt in range(TB // 128):
            norm_to_T(P, C, Y, t0 + tt * 128, gam, xt, hn, hnT, tt * 128, stat=(ss, rs))
        for (col0, ncols, dst, row0, func) in fm_specs:
            for s0 in range(0, ncols, 512):
                sl = slab[cnt[0] % 2]
                cnt[0] += 1
                load_wslab(P, sl, W, col0 + s0, col0 + s0 + 512, 32)
                for nt in range(4):
                    for tc in range(0, TB, 512):
                        tw = min(512, TB - tc)
                        pb = banks[cnt[1] % 4]
                        cnt[1] += 1
                        for kt in range(32):
                            P.op("pe", lambda e, kt=kt, nt=nt, pb=pb, sl=sl, tc=tc, tw=tw: e.matmul(
                                pb[:, 0:tw], sl[:, kt, nt * 128:(nt + 1) * 128], hnT[:, kt, tc:tc + tw],
                                start=(kt == 0), stop=(kt == 31)), r=[sl, hnT], w=[pb])
                        sg = stg[cnt[2] % 3]
                        cnt[2] += 1
                        P.op("act", lambda e, pb=pb, sg=sg, tw=tw, func=func: e.activation(
                            out=sg[:, 0:tw], in_=pb[:, 0:tw], func=func), r=[pb], w=[sg])
                        rr = row0 + s0 + nt * 128
                        P.dma("sp", dst[rr:rr + 128, t0 + tc:t0 + tc + tw], sg[:, 0:tw], r=[sg], w=[dst])
        for (col0, ncols, dst, dcol0, func) in tm_specs:
            for s0 in range(0, ncols, 512):
                sl = slab[cnt[0] % 2]
                cnt[0] += 1
                load_wslab(P, sl, W, col0 + s0, col0 + s0 + 512, 32)
                for tt in range(TB // 128):
                    pb = banks[cnt[1] % 4]
                    cnt[1] += 1
                    for kt in range(32):
                        P.op("pe", lambda e, kt=kt, tt=tt, pb=pb, sl=sl: e.matmul(
                            pb[:], hnT[:, kt, tt * 128:(tt + 1) * 128], sl[:, kt, :],
                            start=(kt == 0), stop=(kt == 31)), r=[sl, hnT], w=[pb])
                    sg = stg[cnt[2] % 3]
                    cnt[2] += 1
                    P.op("act", lambda e, pb=pb, sg=sg, func=func: e.activation(
                        out=sg[:], in_=pb[:], func=func), r=[pb], w=[sg])
                    r0 = t0 + tt * 128
                    P.dma("sp", dst[r0:r0 + 128, dcol0 + s0:dcol0 + s0 + 512], sg[:], r=[sg], w=[dst])
    P.release(m0)


def outproj_stage(P, C, Y, T, OT, W, K):
    KT = K // 128
    TB = min(T, 512 if KT <= 32 else 256)
    NS = 512 if KT <= 32 else 256
    m0 = P.mark()
    oT = P.sb([KT, TB], BF16, "op_oT")
    yt = [P.sb([4096], F32, "op_y%d" % i) for i in range(TB // 128)]
    slab = [P.sb([KT, NS], BF16, "op_slab%d" % i) for i in range(2)]
    banks = [P.ps[3], P.ps[4], P.ps[5], P.ps[6]]
    OTv = OT.t.rearrange("(kt p) t -> p kt t", p=128)
    cnt = [0, 0]
    for t0 in range(0, T, TB):
        for k0 in range(0, KT, 8):
            P.dma("sp", oT[:, k0:k0 + 8, :], OTv[:, k0:k0 + 8, t0:t0 + TB], r=[OT], w=[oT])
        for tt in range(TB // 128):
            P.dma("sp", yt[tt][:], Y[t0 + tt * 128:t0 + (tt + 1) * 128, :], r=[Y], w=[yt[tt]])
        for s0 in range(0, 4096, NS):
            sl = slab[cnt[0] % 2]
            cnt[0] += 1
            load_wslab(P, sl, W, s0, s0 + NS, KT)
            for tt in range(TB // 128):
                pb = banks[cnt[1] % 4]
                cnt[1] += 1
                for kt in range(KT):
                    P.op("pe", lambda e, kt=kt, tt=tt, pb=pb, sl=sl: e.matmul(
                        pb[:, 0:NS], oT[:, kt, tt * 128:(tt + 1) * 128], sl[:, kt, :],
                        start=(kt == 0), stop=(kt == KT - 1)), r=[sl, oT], w=[pb])
                P.op("dve", lambda e, pb=pb, tt=tt, s0=s0: e.tensor_tensor(
                    out=yt[tt][:, s0:s0 + NS], in0=yt[tt][:, s0:s0 + NS], in1=pb[:, 0:NS], op=ALU.add),
                    r=[yt[tt], pb], w=[yt[tt]])
        for tt in range(TB // 128):
            P.dma("sp", Y[t0 + tt * 128:t0 + (tt + 1) * 128, :], yt[tt][:], r=[yt[tt]], w=[Y])
    P.release(m0)


def make_consts(T):
    cols = {}
    parts = []
    off = 0

    def add(name, arr):
        nonlocal off
        arr = np.asarray(arr, np.float32)
        cols[name] = (off, arr.shape[1])
        parts.append(arr)
        off += arr.shape[1]

    t = np.arange(T)
    add("reset", np.broadcast_to((t % 64 != 0).astype(np.float32)[None, :], (128, T)))
    s = np.arange(128)
    same = (s[:, None] // 64) == (s[None, :] // 64)
    add("mc2", (same & (s[:, None] <= s[None, :])).astype(np.float32))
    add("usuf", (same & (s[:, None] > s[None, :])).astype(np.float32))
    add("onesdiv", np.full((128, 128), 1.0 / 128, np.float32))
    add("same", same.astype(np.float32))
    add("iota", np.broadcast_to(t.astype(np.float32)[None, :], (128, T)))
    add("mask8", ((s[:, None] // 16) == np.arange(8)[None, :]).astype(np.float32))
    return np.ascontiguousarray(np.concatenate(parts, axis=1)), cols


def load_consts(P, C, consts_d, cols):
    n = sum(v[1] for v in cols.values())
    cb = P.sb([n], F32, "consts")
    P.dma("sp", cb[:], consts_d, w=[cb])
    C["cb"] = cb
    C["cols"] = cols


def cview(C, name):
    o, n = C["cols"][name]
    return C["cb"][:, o:o + n]


def hgrn2_pass1(P, C, T, ZF, ZT, lbp_d, lbrow_d, KH, VB, QT, KT, DECd, SLd):
    NT = T // 128
    NCH = T // 64
    cb = C["cb"]
    m0 = P.mark()
    lbrow = P.sb([2048], F32, "lbrow")
    omlrow = P.sb([2048], F32, "omlrow")
    lbp = P.sb([16], F32, "lbp")
    omlp = P.sb([16], F32, "omlp")
    tmp2 = P.sb([2, 2048], F32, "lbtmp")
    tmpp = P.sb([2, 16], F32, "lbtmpp")
    P.dma("sp", tmp2[:], lbrow_d, w=[tmp2])
    P.dma("sp", tmpp[:], lbp_d, w=[tmpp])
    P.op("dve", lambda e: e.tensor_tensor(out=lbrow[:], in0=tmp2[:, 0, :], in1=tmp2[:, 1, :], op=ALU.subtract),
         r=[tmp2], w=[lbrow])
    P.op("act", lambda e: e.activation(out=lbrow[:], in_=lbrow[:], func=AF.Sigmoid), r=[lbrow], w=[lbrow])
    P.op("dve", lambda e: e.tensor_scalar(out=omlrow[:], in0=lbrow[:], scalar1=-1.0, scalar2=1.0,
                                          op0=ALU.mult, op1=ALU.add), r=[lbrow], w=[omlrow])
    P.op("dve", lambda e: e.tensor_tensor(out=lbp[:], in0=tmpp[:, 0, :], in1=tmpp[:, 1, :], op=ALU.subtract),
         r=[tmpp], w=[lbp])
    P.op("act", lambda e: e.activation(out=lbp[:], in_=lbp[:], func=AF.Sigmoid), r=[lbp], w=[lbp])
    P.op("dve", lambda e: e.tensor_scalar(out=omlp[:], in0=lbp[:], scalar1=-1.0, scalar2=1.0,
                                          op0=ALU.mult, op1=ALU.add), r=[lbp], w=[omlp])
    ft = P.sb([512], F32, "h_ft")
    vt = P.sb([512], F32, "h_vt")
    lg = P.sb([512], F32, "h_lg")
    eR = P.sb([512], F32, "h_eR")
    k1 = P.sb([512], F32, "h_k1")
    khat = P.sb([NT, 512], BF16, "h_khat")
    vb = P.sb([NT, 512], BF16, "h_vb")
    A = P.sb([T], F32, "h_A")
    B = P.sb([T], F32, "h_B")
    Cb = P.sb([T], F32, "h_C")
    Dd = P.sb([T], F32, "h_D")
    E = P.sb([T], F32, "h_E")
    qt = P.sb([T], BF16, "h_qt")
    kt = P.sb([T], BF16, "h_kt")
    dec = P.sb([16, NCH], F32, "h_dec")
    dsum = P.sb([1], F32, "h_dsum")
    SL = P.sb([16 * 128 + 16], F32, "h_SL")
    S = P.sb([128], F32, "h_S")
    usuf = cview(C, "usuf")
    reset = cview(C, "reset")
    psR = [P.ps[0], P.ps[1]]
    psU = [P.ps[2], P.ps[3]]
    KHv = KH.t.rearrange("(nt p) c -> p nt c", p=128)
    VBv = VB.t.rearrange("(nt p) c -> p nt c", p=128)
    for hg in range(4):
        h0 = hg * 4
        for tt in range(NT):
            pr = psR[tt % 2]
            P.dma("sp", ft[:], ZT[tt * 128:(tt + 1) * 128, h0 * 128:h0 * 128 + 512], r=[ZT], w=[ft])
            P.dma("sp", vt[:], ZT[tt * 128:(tt + 1) * 128, 2048 + h0 * 128:2048 + h0 * 128 + 512], r=[ZT], w=[vt])
            P.op("dve", lambda e, h0=h0: e.tensor_tensor(out=ft[:], in0=ft[:], in1=omlrow[:, h0 * 128:h0 * 128 + 512],
                                                         op=ALU.mult), r=[ft, omlrow], w=[ft])
            P.op("dve", lambda e, h0=h0: e.tensor_tensor(out=ft[:], in0=ft[:], in1=lbrow[:, h0 * 128:h0 * 128 + 512],
                                                         op=ALU.add), r=[ft, lbrow], w=[ft])
            P.op("act", lambda e: e.activation(out=lg[:], in_=ft[:], func=AF.Ln), r=[ft], w=[lg])
            P.op("pe", lambda e, pr=pr: e.matmul(pr[:], usuf, lg[:], start=True, stop=True), r=[cb, lg], w=[pr])
            P.op("act", lambda e, pr=pr: e.activation(out=eR[:], in_=pr[:], func=AF.Exp), r=[pr], w=[eR])
            P.op("dve", lambda e: e.tensor_scalar(out=k1[:], in0=ft[:], scalar1=-1.0, scalar2=1.0,
                                                  op0=ALU.mult, op1=ALU.add), r=[ft], w=[k1])
            P.op("dve", lambda e, tt=tt: e.tensor_tensor(out=khat[:, tt, :], in0=k1[:], in1=eR[:], op=ALU.mult),
                 r=[k1, eR], w=[khat])
            P.op("pool", lambda e, tt=tt: e.tensor_copy(out=vb[:, tt, :], in_=vt[:]), r=[vt], w=[vb])
        P.dma("sp", KHv[:, :, h0 * 128:h0 * 128 + 512], khat[:], r=[khat], w=[KH])
        P.dma("sp", VBv[:, :, h0 * 128:h0 * 128 + 512], vb[:], r=[vb], w=[VB])
        for hl in range(4):
            h = h0 + hl
            P.dma("sp", A[:], ZF[h * 128:(h + 1) * 128, :], r=[ZF], w=[A])
            P.dma("sp", B[:], ZF[2048 + h * 128:2048 + (h + 1) * 128, :], r=[ZF], w=[B])
            P.op("dve", lambda e, h=h: e.tensor_scalar(out=B[:], in0=B[:], scalar1=omlp[:, h:h + 1],
                                                       scalar2=lbp[:, h:h + 1], op0=ALU.mult, op1=ALU.add),
                 r=[B, omlp, lbp], w=[B])
            P.op("act", lambda e: e.activation(out=Cb[:], in_=B[:], func=AF.Ln), r=[B], w=[Cb])
            P.op("dve", lambda e: e.tensor_tensor_scan(out=Dd[:], data0=reset[:, 0:T], data1=Cb[:], initial=0.0,
                                                       op0=ALU.mult, op1=ALU.add), r=[cb, Cb], w=[Dd])
            P.op("act", lambda e: e.activation(out=E[:], in_=Dd[:], func=AF.Exp), r=[Dd], w=[E])
            P.op("dve", lambda e: e.tensor_tensor(out=qt[:], in0=A[:], in1=E[:], op=ALU.mult), r=[A, E], w=[qt])
            P.op("act", lambda e: e.activation(out=A[:], in_=Dd[:], func=AF.Exp, scale=-1.0), r=[Dd, qt], w=[A])
            P.op("dve", lambda e: e.tensor_scalar(out=B[:], in0=B[:], scalar1=-1.0, scalar2=1.0,
                                                  op0=ALU.mult, op1=ALU.add), r=[B, Cb], w=[B])
            P.op("dve", lambda e: e.tensor_tensor(out=kt[:], in0=B[:], in1=A[:], op=ALU.mult), r=[A, B], w=[kt])
            Ev = E[:].rearrange("p (c s) -> p c s", s=64)
            Dv = Dd[:].rearrange("p (c s) -> p c s", s=64)
            P.op("dve", lambda e, h=h, Ev=Ev: e.tensor_copy(out=dec[:, h, :], in_=Ev[:, :, 63]), r=[E], w=[dec])
            P.op("dve", lambda e, Dv=Dv: e.tensor_reduce(out=dsum[:], in_=Dv[:, :, 63], axis=AX.X, op=ALU.add),
                 r=[Dd], w=[dsum])
            P.op("act", lambda e, h=h: e.activation(out=SL[:, 2048 + h:2048 + h + 1], in_=dsum[:], func=AF.Exp),
                 r=[dsum], w=[SL])
            P.dma("sp", QT[h * 128:(h + 1) * 128, :], qt[:], r=[qt], w=[QT])
            P.dma("sp", KT[h * 128:(h + 1) * 128, :], kt[:], r=[kt], w=[KT])
            P.op("dve", lambda e: e.memset(S[:], 0.0), w=[S])
            for c in range(NCH):
                tt, b0 = c // 2, (c % 2) * 64
                pu = psU[c % 2]
                P.op("pe", lambda e, tt=tt, b0=b0, hl=hl, pu=pu: e.matmul(
                    pu[:, 0:128], khat[b0:b0 + 64, tt, hl * 128:(hl + 1) * 128],
                    vb[b0:b0 + 64, tt, hl * 128:(hl + 1) * 128], start=True, stop=True), r=[khat, vb], w=[pu])
                P.op("dve", lambda e, h=h, c=c, pu=pu: e.scalar_tensor_tensor(
                    out=S[:], in0=S[:], scalar=dec[:, h, c:c + 1], in1=pu[:, 0:128], op0=ALU.mult, op1=ALU.add),
                    r=[S, dec, pu], w=[S])
            P.op("dve", lambda e, h=h: e.tensor_copy(out=SL[:, h * 128:(h + 1) * 128], in_=S[:]), r=[S], w=[SL])
    P.dma("sp", DECd.t, dec[:].rearrange("p a b -> p (a b)"), r=[dec], w=[DECd])
    P.dma("sp", SLd.t, SL[:], r=[SL], w=[SLd])
    P.release(m0)


def hgrn2_pass2(P, C, T, ZF, KH, VB, QT, KT, DECd, SId, OT):
    NT = T // 128
    NCH = T // 64
    cb = C["cb"]
    m0 = P.mark()
    khat = P.sb([NT, 512], BF16, "g_khat")
    vb = P.sb([NT, 512], BF16, "g_vb")
    qt = P.sb([T], BF16, "g_qt")
    kt = P.sb([T], BF16, "g_kt")
    sg = P.sb([T], F32, "g_sg")
    og = P.sb([T], BF16, "g_og")
    dec = P.sb([16, NCH], F32, "g_dec")
    S = P.sb([128], F32, "g_S")
    Sb = P.sb([128], BF16, "g_Sb")
    am = P.sb([128], BF16, "g_am")
    osb = P.sb([128], F32, "g_osb")
    sq = P.sb([128], F32, "g_sq")
    rsd = P.sb([128], F32, "g_rsd")
    mc2 = cview(C, "mc2")
    onesdiv = cview(C, "onesdiv")
    KHv = KH.t.rearrange("(nt p) c -> p nt c", p=128)
    VBv = VB.t.rearrange("(nt p) c -> p nt c", p=128)
    P.dma("sp", dec[:].rearrange("p a b -> p (a b)"), DECd.t, r=[DECd], w=[dec])
    psA, psO, psM = P.ps[0], P.ps[1], P.ps[4]
    psU = [P.ps[2], P.ps[3]]
    for hg in range(4):
        h0 = hg * 4
        P.dma("sp", khat[:], KHv[:, :, h0 * 128:h0 * 128 + 512], r=[KH], w=[khat])
        P.dma("sp", vb[:], VBv[:, :, h0 * 128:h0 * 128 + 512], r=[VB], w=[vb])
        for hl in range(4):
            h = h0 + hl
            P.dma("sp", qt[:], QT[h * 128:(h + 1) * 128, :], r=[QT], w=[qt])
            P.dma("sp", kt[:], KT[h * 128:(h + 1) * 128, :], r=[KT], w=[kt])
            P.dma("sp", sg[:], ZF[4096 + h * 128:4096 + (h + 1) * 128, :], r=[ZF], w=[sg])
            P.dma("sp", S[:], SId[:, h * 128:(h + 1) * 128], r=[SId], w=[S])
            P.op("act", lambda e: e.activation(out=Sb[:], in_=S[:], func=AF.Copy), r=[S], w=[Sb])
            for tt in range(NT):
                ts = slice(tt * 128, (tt + 1) * 128)
                P.op("pe", lambda e, ts=ts: e.matmul(psA[:, 0:128], kt[:, ts], qt[:, ts], start=True, stop=True),
                     r=[kt, qt], w=[psA])
                P.op("dve", lambda e: e.tensor_tensor(out=am[:], in0=psA[:, 0:128], in1=mc2, op=ALU.mult),
                     r=[psA, cb], w=[am])
                P.op("pe", lambda e, tt=tt, hl=hl: e.matmul(psO[:, 0:128], vb[:, tt, hl * 128:(hl + 1) * 128], am[:],
                                                            start=True, stop=False), r=[vb, am], w=[psO])
                for ci in range(2):
                    c = 2 * tt + ci
                    b0 = ci * 64
                    pu = psU[c % 2]
                    P.op("pe", lambda e, b0=b0, c=c: e.matmul(psO[:, b0:b0 + 64], Sb[:], qt[:, c * 64:(c + 1) * 64],
                                                              start=False, stop=True), r=[Sb, qt], w=[psO])
                    P.op("pe", lambda e, tt=tt, b0=b0, hl=hl, pu=pu: e.matmul(
                        pu[:, 0:128], khat[b0:b0 + 64, tt, hl * 128:(hl + 1) * 128],
                        vb[b0:b0 + 64, tt, hl * 128:(hl + 1) * 128], start=True, stop=True), r=[khat, vb], w=[pu])
                    P.op("dve", lambda e, h=h, c=c, pu=pu: e.scalar_tensor_tensor(
                        out=S[:], in0=S[:], scalar=dec[:, h, c:c + 1], in1=pu[:, 0:128], op0=ALU.mult, op1=ALU.add),
                        r=[S, dec, pu], w=[S])
                    P.op("act", lambda e: e.activation(out=Sb[:], in_=S[:], func=AF.Copy), r=[S], w=[Sb])
                P.op("act", lambda e: e.activation(out=osb[:], in_=psO[:, 0:128], func=AF.Copy), r=[psO], w=[osb])
                P.op("act", lambda e: e.activation(out=sq[:], in_=psO[:, 0:128], func=AF.Square), r=[psO], w=[sq])
                P.op("pe", lambda e: e.matmul(psM[:, 0:128], onesdiv, sq[:], start=True, stop=True), r=[cb, sq], w=[psM])
                P.op("act", lambda e: e.activation(out=rsd[:], in_=psM[:, 0:128], func=AF.Sqrt, bias=C["eps"][:], scale=1.0),
                     r=[psM, C["eps"]], w=[rsd])
                P.op("dve", lambda e: e.reciprocal(out=rsd[:], in_=rsd[:]), r=[rsd], w=[rsd])
                P.op("dve", lambda e: e.tensor_tensor(out=osb[:], in0=osb[:], in1=rsd[:], op=ALU.mult), r=[osb, rsd], w=[osb])
                P.op("dve", lambda e, ts=ts: e.tensor_tensor(out=og[:, ts], in0=osb[:], in1=sg[:, ts], op=ALU.mult),
                     r=[osb, sg], w=[og])
            P.dma("sp", OT[h * 128:(h + 1) * 128, :], og[:], r=[og], w=[OT])
    P.release(m0)


MAGIC = 12582912.0
TWO_PI = float(2.0 * np.pi)


def sincos(P, C, arg, s_out, c_out, t0, t1):
    P.op("dve", lambda e: e.tensor_scalar(out=t0[:], in0=arg[:], scalar1=1.0 / TWO_PI, scalar2=MAGIC,
                                          op0=ALU.mult, op1=ALU.add), r=[arg], w=[t0])
    P.op("dve", lambda e: e.tensor_scalar(out=t0[:], in0=t0[:], scalar1=-MAGIC, scalar2=-TWO_PI,
                                          op0=ALU.add, op1=ALU.mult), r=[t0], w=[t0])
    P.op("dve", lambda e: e.tensor_tensor(out=t1[:], in0=arg[:], in1=t0[:], op=ALU.add), r=[arg, t0], w=[t1])
    P.op("act", lambda e: e.activation(out=s_out[:], in_=t1[:], func=AF.Sin), r=[t1], w=[s_out])
    P.op("act", lambda e: e.activation(out=t0[:], in_=t1[:], func=AF.Abs), r=[t1], w=[t0])
    P.op("act", lambda e: e.activation(out=c_out[:], in_=t0[:], func=AF.Sin, scale=-1.0, bias=C["halfpi"][:]),
         r=[t0, C["halfpi"]], w=[c_out])


def s5_scan(P, C, T, ZF, prm, XI_d, SXL_d, Y2):
    cb = C["cb"]
    NTC = (T + 511) // 512
    m0 = P.mark()
    iota = cview(C, "iota")[:, 0:T]
    mask8 = cview(C, "mask8")
    lbu = P.sb([64, 2, 128], BF16, "s5_lbu")
    lc = P.sb([64, 2, 128], BF16, "s5_lc") if Y2 is not None else None
    mp = P.mark()
    def pb(name, free=(16, 64)):
        return P.sb(list(free), F32, "s5_" + name)
    are, aim, brT, biT = pb("are"), pb("aim"), pb("brT"), pb("biT")
    ldt = P.sb([16], F32, "s5_ldt")
    for b_, nm in ((are, "are_b"), (aim, "aim_b"), (brT, "brT"), (biT, "biT"), (ldt, "ldt_b")):
        P.dma("sp", b_[:], prm[nm], w=[b_])
    dt = P.sb([16], F32, "s5_dt")
    P.op("act", lambda e: e.activation(out=dt[:], in_=ldt[:], func=AF.Exp), r=[ldt], w=[dt])
    dtb = dt[:].unsqueeze(2).to_broadcast([128, 16, 64])
    P.op("dve", lambda e: e.tensor_scalar(out=are[:], in0=are[:], scalar1=-1e-4, scalar2=None, op0=ALU.min),
         r=[are], w=[are])
    x1, th, sn, cs, t0, t1 = pb("x1"), pb("th"), pb("sn"), pb("cs"), pb("t0"), pb("t1")
    P.op("dve", lambda e: e.tensor_tensor(out=x1[:], in0=are[:], in1=dtb, op=ALU.mult), r=[are, dt], w=[x1])
    P.op("act", lambda e: e.activation(out=x1[:], in_=x1[:], func=AF.Exp), r=[x1], w=[x1])
    P.op("dve", lambda e: e.tensor_tensor(out=th[:], in0=aim[:], in1=dtb, op=ALU.mult), r=[aim, dt], w=[th])
    sincos(P, C, th, sn, cs, t0, t1)
    abr, abi = cs, sn
    P.op("dve", lambda e: e.tensor_tensor(out=abr[:], in0=cs[:], in1=x1[:], op=ALU.mult), r=[cs, x1], w=[abr])
    P.op("dve", lambda e: e.tensor_tensor(out=abi[:], in0=sn[:], in1=x1[:], op=ALU.mult), r=[sn, x1], w=[abi])
    P.op("dve", lambda e: e.tensor_scalar(out=abr[:], in0=abr[:], scalar1=-1.0, scalar2=None, op0=ALU.add),
         r=[abr], w=[abr])
    den = x1
    P.op("dve", lambda e: e.tensor_tensor(out=t0[:], in0=are[:], in1=are[:], op=ALU.mult), r=[are], w=[t0])
    P.op("dve", lambda e: e.tensor_tensor(out=t1[:], in0=aim[:], in1=aim[:], op=ALU.mult), r=[aim], w=[t1])
    P.op("dve", lambda e: e.tensor_tensor(out=den[:], in0=t0[:], in1=t1[:], op=ALU.add), r=[t0, t1], w=[den])
    P.op("dve", lambda e: e.reciprocal(out=den[:], in_=den[:]), r=[den], w=[den])
    zr, zi = th, pb("zi")
    P.op("dve", lambda e: e.tensor_tensor(out=t0[:], in0=abr[:], in1=are[:], op=ALU.mult), r=[abr, are], w=[t0])
    P.op("dve", lambda e: e.tensor_tensor(out=t1[:], in0=abi[:], in1=aim[:], op=ALU.mult), r=[abi, aim], w=[t1])
    P.op("dve", lambda e: e.tensor_tensor(out=t0[:], in0=t0[:], in1=t1[:], op=ALU.add), r=[t0, t1], w=[t0])
    P.op("dve", lambda e: e.tensor_tensor(out=zr[:], in0=t0[:], in1=den[:], op=ALU.mult), r=[t0, den], w=[zr])
    P.op("dve", lambda e: e.tensor_tensor(out=t0[:], in0=abi[:], in1=are[:], op=ALU.mult), r=[abi, are], w=[t0])
    P.op("dve", lambda e: e.tensor_tensor(out=t1[:], in0=abr[:], in1=aim[:], op=ALU.mult), r=[abr, aim], w=[t1])
    P.op("dve", lambda e: e.tensor_tensor(out=t0[:], in0=t0[:], in1=t1[:], op=ALU.subtract), r=[t0, t1], w=[t0])
    P.op("dve", lambda e: e.tensor_tensor(out=zi[:], in0=t0[:], in1=den[:], op=ALU.mult), r=[t0, den], w=[zi])
    bbr, bbi = are, aim
    P.op("dve", lambda e: e.tensor_tensor(out=t0[:], in0=zr[:], in1=brT[:], op=ALU.mult), r=[zr, brT], w=[t0])
    P.op("dve", lambda e: e.tensor_tensor(out=t1[:], in0=zi[:], in1=biT[:], op=ALU.mult), r=[zi, biT], w=[t1])
    P.op("dve", lambda e: e.tensor_tensor(out=bbr[:], in0=t0[:], in1=t1[:], op=ALU.subtract), r=[t0, t1], w=[bbr])
    P.op("dve", lambda e: e.tensor_tensor(out=t0[:], in0=zr[:], in1=biT[:], op=ALU.mult), r=[zr, biT], w=[t0])
    P.op("dve", lambda e: e.tensor_tensor(out=t1[:], in0=zi[:], in1=brT[:], op=ALU.mult), r=[zi, brT], w=[t1])
    P.op("dve", lambda e: e.tensor_tensor(out=bbi[:], in0=t0[:], in1=t1[:], op=ALU.add), r=[t0, t1], w=[bbi])
    for m in range(64):
        kt, ga = m // 4, (2 * m) % 8
        for ri, src in ((0, bbr), (1, bbi)):
            for g2 in range(2):
                P.op("dve", lambda e, m=m, ri=ri, src=src, g2=g2, kt=kt, ga=ga: e.tensor_scalar(
                    out=lbu[:, m, ri, g2 * 64:(g2 + 1) * 64], in0=src[:, kt, :],
                    scalar1=mask8[:, ga + g2:ga + g2 + 1], scalar2=None, op0=ALU.mult), r=[src, cb], w=[lbu])
    if Y2 is not None:
        crT = P.sb([64, 16], F32, "s5_crT")
        ciT = P.sb([64, 16], F32, "s5_ciT")
        P.dma("sp", crT[:], prm["crT"], w=[crT])
        P.dma("sp", ciT[:], prm["ciT"], w=[ciT])
        P.op("pool", lambda e: e.memset(lc[:], 0.0), w=[lc])
        for m in range(64):
            for g2 in range(2):
                lg_ = (2 * m + g2) % 8
                ps_ = slice(g2 * 64, (g2 + 1) * 64)
                P.op("act", lambda e, m=m, ps_=ps_, lg_=lg_: e.activation(
                    out=lc[ps_, m, 0, lg_ * 16:(lg_ + 1) * 16], in_=crT[ps_, m, :], func=AF.Copy),
                    r=[crT], w=[lc])
                P.op("act", lambda e, m=m, ps_=ps_, lg_=lg_: e.activation(
                    out=lc[ps_, m, 1, lg_ * 16:(lg_ + 1) * 16], in_=ciT[ps_, m, :], func=AF.Copy, scale=-1.0),
                    r=[ciT], w=[lc])
    P.barrier()
    P.off = mp
    apr = P.sb([64], F32, "s5_apr")
    api = P.sb([64], F32, "s5_api")
    ldp = P.sb([64], F32, "s5_ldp")
    for b_, nm in ((apr, "are_p"), (api, "aim_p"), (ldp, "ldt_p")):
        P.dma("sp", b_[:], prm[nm], w=[b_])
    rp = P.sb([64], F32, "s5_rp")
    thp = P.sb([64], F32, "s5_thp")
    thn = P.sb([64], F32, "s5_thn")
    P.op("act", lambda e: e.activation(out=ldp[:], in_=ldp[:], func=AF.Exp), r=[ldp], w=[ldp])
    P.op("dve", lambda e: e.tensor_scalar(out=apr[:], in0=apr[:], scalar1=-1e-4, scalar2=None, op0=ALU.min),
         r=[apr], w=[apr])
    P.op("dve", lambda e: e.tensor_tensor(out=rp[:], in0=apr[:], in1=ldp[:], op=ALU.mult), r=[apr, ldp], w=[rp])
    P.op("act", lambda e: e.activation(out=rp[:], in_=rp[:], func=AF.Exp), r=[rp], w=[rp])
    P.op("dve", lambda e: e.tensor_tensor(out=thp[:], in0=api[:], in1=ldp[:], op=ALU.mult), r=[api, ldp], w=[thp])
    P.op("dve", lambda e: e.tensor_scalar(out=thn[:], in0=thp[:], scalar1=1.0 / TWO_PI, scalar2=None, op0=ALU.mult),
         r=[thp], w=[thn])
    xh0 = P.sb([64, 2], F32, "s5_xh0")
    if XI_d is not None:
        xi = P.sb([64, 2], F32, "s5_xi")
        P.dma("sp", xi[:], XI_d.t, r=[XI_d], w=[xi])
        s1_, c1_, q0, q1 = (P.sb([64], F32, "s5_i%d" % i) for i in range(4))
        sincos(P, C, thp, s1_, c1_, q0, q1)
        P.op("dve", lambda e: e.tensor_tensor(out=q0[:], in0=xi[:, :, 0], in1=c1_[:], op=ALU.mult), r=[xi, c1_], w=[q0])
        P.op("dve", lambda e: e.tensor_tensor(out=q1[:], in0=xi[:, :, 1], in1=s1_[:], op=ALU.mult), r=[xi, s1_], w=[q1])
        P.op("dve", lambda e: e.tensor_tensor(out=xh0[:, :, 0], in0=q0[:], in1=q1[:], op=ALU.subtract), r=[q0, q1], w=[xh0])
        P.op("dve", lambda e: e.tensor_tensor(out=q0[:], in0=xi[:, :, 1], in1=c1_[:], op=ALU.mult), r=[xi, c1_], w=[q0])
        P.op("dve", lambda e: e.tensor_tensor(out=q1[:], in0=xi[:, :, 0], in1=s1_[:], op=ALU.mult), r=[xi, s1_], w=[q1])
        P.op("dve", lambda e: e.tensor_tensor(out=xh0[:, :, 1], in0=q0[:], in1=q1[:], op=ALU.add), r=[q0, q1], w=[xh0])
    else:
        P.op("dve", lambda e: e.memset(xh0[:], 0.0), w=[xh0])
    sxl = P.sb([64, 2], F32, "s5_sxl")
    fq = [P.sb([1], F32, "s5_fq%d" % i) for i in range(2)]
    B1, B2, B3, B4, B5, B6 = (P.sb([T], F32, "s5_B%d" % i) for i in range(6))
    uf = P.sb([T], F32, "s5_uf")
    ub = P.sb([T], BF16, "s5_ub")
    xrb = P.sb([T], BF16, "s5_xrb")
    xib = P.sb([T], BF16, "s5_xib")
    dsk = P.sb([16], F32, "s5_d")
    ystg = P.sb([512], F32, "s5_ystg")
    if Y2 is not None:
        P.dma("sp", dsk[:], prm["d"], w=[dsk])
    psB = [P.ps[0], P.ps[1]]
    psY = [P.ps[2], P.ps[3], P.ps[4], P.ps[5]]
    for m in range(64):
        kt = m // 4
        if m % 4 == 0:
            P.dma("sp", uf[:], ZF[6144 + kt * 128:6144 + (kt + 1) * 128, :], r=[ZF], w=[uf])
            P.op("pool", lambda e: e.tensor_copy(out=ub[:], in_=uf[:]), r=[uf], w=[ub])
        for ri, dst in ((0, B1), (1, B2)):
            for tc in range(NTC):
                c0, c1 = tc * 512, min(T, tc * 512 + 512)
                pbk = psB[(ri * NTC + tc) % 2]
                P.op("pe", lambda e, m=m, ri=ri, c0=c0, c1=c1, pbk=pbk: e.matmul(
                    pbk[:, 0:c1 - c0], lbu[:, m, ri, :], ub[:, c0:c1], start=True, stop=True), r=[lbu, ub], w=[pbk])
                P.op("act", lambda e, dst=dst, c0=c0, c1=c1, pbk=pbk: e.activation(
                    out=dst[:, c0:c1], in_=pbk[:, 0:c1 - c0], func=AF.Copy), r=[pbk], w=[dst])
        P.op("dve", lambda e, m=m: e.tensor_scalar(out=B5[:], in0=iota, scalar1=thn[:, m:m + 1], scalar2=MAGIC,
                                                   op0=ALU.mult, op1=ALU.add), r=[cb, thn], w=[B5])
        P.op("dve", lambda e: e.tensor_scalar(out=B5[:], in0=B5[:], scalar1=-MAGIC, scalar2=-TWO_PI,
                                              op0=ALU.add, op1=ALU.mult), r=[B5], w=[B5])
        P.op("dve", lambda e, m=m: e.scalar_tensor_tensor(out=B5[:], in0=iota, scalar=thp[:, m:m + 1], in1=B5[:],
                                                          op0=ALU.mult, op1=ALU.add), r=[cb, thp, B5], w=[B5])
        P.op("act", lambda e: e.activation(out=B4[:], in_=B5[:], func=AF.Sin), r=[B5], w=[B4])
        P.op("act", lambda e: e.activation(out=B6[:], in_=B5[:], func=AF.Abs), r=[B5], w=[B6])
        P.op("act", lambda e: e.activation(out=B3[:], in_=B6[:], func=AF.Sin, scale=-1.0, bias=C["halfpi"][:]),
             r=[B6, C["halfpi"]], w=[B3])
        P.op("dve", lambda e: e.tensor_tensor(out=B5[:], in0=B1[:], in1=B3[:], op=ALU.mult), r=[B1, B3], w=[B5])
        P.op("pool", lambda e: e.tensor_tensor(out=B6[:], in0=B2[:], in1=B4[:], op=ALU.mult), r=[B2, B4], w=[B6])
        P.op("dve", lambda e: e.tensor_tensor(out=B5[:], in0=B5[:], in1=B6[:], op=ALU.add), r=[B5, B6], w=[B5])
        P.op("pool", lambda e: e.tensor_tensor(out=B6[:], in0=B2[:], in1=B3[:], op=ALU.mult), r=[B2, B3], w=[B6])
        P.op("dve", lambda e: e.tensor_tensor(out=B2[:], in0=B1[:], in1=B4[:], op=ALU.mult), r=[B1, B4, B6], w=[B2])
        P.op("dve", lambda e: e.tensor_tensor(out=B6[:], in0=B6[:], in1=B2[:], op=ALU.subtract), r=[B6, B2], w=[B6])
        rb = rp[:, m:m + 1].to_broadcast([128, T])
        P.op("dve", lambda e, m=m, rb=rb: e.tensor_tensor_scan(out=B1[:], data0=rb, data1=B5[:], initial=xh0[:, m, 0:1],
                                                               op0=ALU.mult, op1=ALU.add), r=[rp, B5, xh0], w=[B1])
        P.op("dve", lambda e, m=m, rb=rb: e.tensor_tensor_scan(out=B2[:], data0=rb, data1=B6[:], initial=xh0[:, m, 1:2],
                                                               op0=ALU.mult, op1=ALU.add), r=[rp, B6, xh0], w=[B2])
        if SXL_d is not None:
            L = T - 1
            P.op("dve", lambda e, L=L: e.tensor_tensor(out=fq[0][:], in0=B1[:, L:L + 1], in1=B3[:, L:L + 1], op=ALU.mult),
                 r=[B1, B3], w=[fq[0]])
            P.op("dve", lambda e, L=L: e.tensor_tensor(out=fq[1][:], in0=B2[:, L:L + 1], in1=B4[:, L:L + 1], op=ALU.mult),
                 r=[B2, B4], w=[fq[1]])
            P.op("dve", lambda e, m=m: e.tensor_tensor(out=sxl[:, m, 0:1], in0=fq[0][:], in1=fq[1][:], op=ALU.subtract),
                 r=fq, w=[sxl])
            P.op("dve", lambda e, L=L: e.tensor_tensor(out=fq[0][:], in0=B2[:, L:L + 1], in1=B3[:, L:L + 1], op=ALU.mult),
                 r=[B2, B3], w=[fq[0]])
            P.op("dve", lambda e, L=L: e.tensor_tensor(out=fq[1][:], in0=B1[:, L:L + 1], in1=B4[:, L:L + 1], op=ALU.mult),
                 r=[B1, B4], w=[fq[1]])
            P.op("dve", lambda e, m=m: e.tensor_tensor(out=sxl[:, m, 1:2], in0=fq[0][:], in1=fq[1][:], op=ALU.add),
                 r=fq, w=[sxl])
        if Y2 is not None:
            P.op("dve", lambda e: e.tensor_tensor(out=B5[:], in0=B1[:], in1=B3[:], op=ALU.mult), r=[B1, B3], w=[B5])
            P.op("pool", lambda e: e.tensor_tensor(out=B6[:], in0=B2[:], in1=B4[:], op=ALU.mult), r=[B2, B4], w=[B6])
            P.op("dve", lambda e: e.tensor_tensor(out=xrb[:], in0=B5[:], in1=B6[:], op=ALU.subtract), r=[B5, B6], w=[xrb])
            P.op("pool", lambda e: e.tensor_tensor(out=B5[:], in0=B2[:], in1=B3[:], op=ALU.mult), r=[B2, B3], w=[B5])
            P.op("dve", lambda e: e.tensor_tensor(out=B6[:], in0=B1[:], in1=B4[:], op=ALU.mult), r=[B1, B4], w=[B6])
            P.op("dve", lambda e: e.tensor_tensor(out=xib[:], in0=B5[:], in1=B6[:], op=ALU.add), r=[B5, B6], w=[xib])
            for tc in range(NTC):
                c0, c1 = tc * 512, min(T, tc * 512 + 512)
                for ri, src in ((0, xrb), (1, xib)):
                    P.op("pe", lambda e, m=m, ri=ri, src=src, c0=c0, c1=c1, tc=tc: e.matmul(
                        psY[tc][:, 0:c1 - c0], lc[:, m, ri, :], src[:, c0:c1],
                        start=(m % 4 == 0 and ri == 0), stop=(m % 4 == 3 and ri == 1)), r=[lc, src], w=[psY[tc]])
            if m % 4 == 3:
                ft = m // 4
                for tc in range(NTC):
                    c0, c1 = tc * 512, min(T, tc * 512 + 512)
                    P.op("dve", lambda e, ft=ft, c0=c0, c1=c1, tc=tc: e.scalar_tensor_tensor(
                        out=ystg[:, 0:c1 - c0], in0=uf[:, c0:c1], scalar=dsk[:, ft:ft + 1], in1=psY[tc][:, 0:c1 - c0],
                        op0=ALU.mult, op1=ALU.add), r=[uf, dsk, psY[tc]], w=[ystg])
                    P.op("act", lambda e, c0=c0, c1=c1: e.activation(out=ystg[:, 0:c1 - c0], in_=ystg[:, 0:c1 - c0],
                                                                     func=AF.Gelu), r=[ystg], w=[ystg])
                    P.dma("sp", Y2[ft * 128:(ft + 1) * 128, c0:c1], ystg[:, 0:c1 - c0], r=[ystg], w=[Y2])
    if SXL_d is not None:
        P.dma("sp", SXL_d.t, sxl[:], r=[sxl], w=[SXL_d])
    P.release(m0)


def glu_stage(P, C, T, Y2, Wg, OT, row0):
    m0 = P.mark()
    yb = P.sb([16, T], BF16, "gl_yb")
    yf = P.sb([T], F32, "gl_yf")
    og = P.sb([T], BF16, "gl_og")
    sl = P.sb([16, 512], BF16, "gl_slab")
    sgm = P.sb([512], F32, "gl_sg")
    for kt in range(16):
        P.dma("sp", yf[:], Y2[kt * 128:(kt + 1) * 128, :], r=[Y2], w=[yf])
        P.op("dve", lambda e, kt=kt: e.tensor_copy(out=yb[:, kt, :], in_=yf[:]), r=[yf], w=[yb])
    banks = [P.ps[0], P.ps[1]]
    k = 0
    for s in range(4):
        load_wslab(P, sl, Wg, s * 512, (s + 1) * 512, 16)
        for nt in range(4):
            n = s * 4 + nt
            P.dma("sp", yf[:], Y2[n * 128:(n + 1) * 128, :], r=[Y2], w=[yf])
            for c0 in range(0, T, 512):
                c1 = min(T, c0 + 512)
                pb = banks[k % 2]
                k += 1
                for kt in range(16):
                    P.op("pe", lambda e, kt=kt, nt=nt, c0=c0, c1=c1, pb=pb: e.matmul(
                        pb[:, 0:c1 - c0], sl[:, kt, nt * 128:(nt + 1) * 128], yb[:, kt, c0:c1],
                        start=(kt == 0), stop=(kt == 15)), r=[sl, yb], w=[pb])
                P.op("act", lambda e, c0=c0, c1=c1, pb=pb: e.activation(out=sgm[:, 0:c1 - c0], in_=pb[:, 0:c1 - c0],
                                                                        func=AF.Sigmoid), r=[pb], w=[sgm])
                P.op("dve", lambda e, c0=c0, c1=c1: e.tensor_tensor(out=og[:, c0:c1], in0=yf[:, c0:c1],
                                                                    in1=sgm[:, 0:c1 - c0], op=ALU.mult),
                     r=[yf, sgm], w=[og])
            P.dma("sp", OT[row0 + n * 128:row0 + (n + 1) * 128, :], og[:], r=[og], w=[OT])
    P.release(m0)


def s5_host_params(a_re, a_im, log_dt, b_re, b_im, c_re, c_im, b_d):
    def bl(a):
        a = a.reshape(16, 8, 1, 64)
        a = np.broadcast_to(a, (16, 8, 16, 64)).transpose(1, 2, 0, 3)
        return np.ascontiguousarray(a.reshape(128, 16, 64))
    def pl(a):
        return np.ascontiguousarray(a.reshape(64, 2, 64).transpose(1, 2, 0).reshape(128, 64))
    out = {}
    out["are_b"] = bl(a_re)
    out["aim_b"] = bl(a_im)
    l = np.broadcast_to(log_dt.reshape(16, 8, 1), (16, 8, 16)).transpose(1, 2, 0)
    out["ldt_b"] = np.ascontiguousarray(l.reshape(128, 16))
    out["brT"] = np.ascontiguousarray(b_re.reshape(16, 8, 64, 16).transpose(1, 3, 0, 2).reshape(128, 16, 64))
    out["biT"] = np.ascontiguousarray(b_im.reshape(16, 8, 64, 16).transpose(1, 3, 0, 2).reshape(128, 16, 64))
    out["are_p"] = pl(a_re)
    out["aim_p"] = pl(a_im)
    out["ldt_p"] = pl(np.broadcast_to(log_dt[:, None], (128, 64)))
    out["crT"] = np.ascontiguousarray(c_re.reshape(64, 2, 16, 64).transpose(1, 3, 0, 2).reshape(128, 64, 16))
    out["ciT"] = np.ascontiguousarray(c_im.reshape(64, 2, 16, 64).transpose(1, 3, 0, 2).reshape(128, 64, 16))
    out["d"] = np.ascontiguousarray(b_d.reshape(16, 128).T)
    return {k: v.astype(np.float32) for k, v in out.items()}


S5_SHAPES = {"are_b": [128, 16, 64], "aim_b": [128, 16, 64], "ldt_b": [128, 16], "brT": [128, 16, 64],
             "biT": [128, 16, 64], "are_p": [128, 64], "aim_p": [128, 64], "ldt_p": [128, 64],
             "crT": [128, 64, 16], "ciT": [128, 64, 16], "d": [128, 16]}


def ret_gammas():
    return [1.0 - 2.0 ** (-5.0 - h) for h in range(16)]


def make_ret_consts():
    g = np.array(ret_gammas(), np.float64)
    s = np.arange(128)
    same = (s[:, None] // 64) == (s[None, :] // 64)
    dist = np.abs(s[:, None] - s[None, :]).astype(np.float64)
    decT = np.stack([np.where(same, gh ** dist, 0.0) / 16.0 for gh in g], axis=1)
    tl = np.arange(64, dtype=np.float64)
    qdec = np.broadcast_to((g[:, None] ** (tl[None, :] + 1.0))[None], (128, 16, 64))
    kdec = (g[None, :] ** (63.0 - (s % 64))[:, None]) / 16.0
    return (np.ascontiguousarray(decT.reshape(128, 16 * 128), np.float32),
            np.ascontiguousarray(qdec.reshape(128, 16 * 64), np.float32),
            np.ascontiguousarray(kdec, np.float32))


def make_rope(pos0, T):
    pos = np.arange(pos0, pos0 + T, dtype=np.float32)
    inv = (np.float32(10000.0) ** (-np.arange(0, 256, 2, dtype=np.float32) / np.float32(256))).astype(np.float32)
    ang = (pos[:, None] * inv[None, :]).astype(np.float32)
    c, s_ = np.cos(ang).astype(np.float32), np.sin(ang).astype(np.float32)
    NT = T // 128
    cT = c.reshape(NT, 128, 128).transpose(1, 0, 2).reshape(128, NT * 128)
    sT = s_.reshape(NT, 128, 128).transpose(1, 0, 2).reshape(128, NT * 128)
    return np.ascontiguousarray(np.concatenate([c.T, s_.T, cT, sT], axis=1), np.float32)


def ret_pass1(P, C, T, ZF, ZT, rope_d, rc, KD, VB, QR, KR, QD, RLd):
    NT = T // 128
    NCH = T // 64
    gam = ret_gammas()
    m0 = P.mark()
    rope = P.sb([4 * T], F32, "r_rope")
    P.dma("sp", rope[:], rope_d, w=[rope])
    cosF, sinF = rope[:, 0:T], rope[:, T:2 * T]
    cosT = rope[:, 2 * T:3 * T].rearrange("p (n d) -> p n d", n=NT)
    sinT = rope[:, 3 * T:4 * T].rearrange("p (n d) -> p n d", n=NT)
    qdec = P.sb([16, 64], F32, "r_qdec")
    kdec = P.sb([16], F32, "r_kdec")
    P.dma("sp", qdec[:].rearrange("p a b -> p (a b)"), rc["qdec"], w=[qdec])
    P.dma("sp", kdec[:], rc["kdec"], w=[kdec])
    kt_ = P.sb([256], F32, "r_kt")
    vt_ = P.sb([512], F32, "r_vt")
    u0 = P.sb([128], F32, "r_u0")
    u1 = P.sb([128], F32, "r_u1")
    kr_ = P.sb([256], F32, "r_kr")
    kd = P.sb([NT, 256], BF16, "r_kd")
    vb = P.sb([NT, 512], BF16, "r_vb")
    XA, XB, W0, W1 = (P.sb([T], F32, "r_X%d" % i) for i in range(4))
    oA, oB, dA, dB = (P.sb([T], BF16, "r_o%d" % i) for i in range(4))
    R = [P.sb([512], F32, "r_R%d" % i) for i in range(2)]
    KDv = KD.t.rearrange("(nt p) c -> p nt c", p=128)
    VBv = VB.t.rearrange("(nt p) c -> p nt c", p=128)
    psU = [[P.ps[0], P.ps[1]], [P.ps[2], P.ps[3]]]
    for h in range(16):
        for tt in range(NT):
            rows = slice(tt * 128, (tt + 1) * 128)
            P.dma("sp", kt_[:], ZT[rows, h * 256:(h + 1) * 256], r=[ZT], w=[kt_])
            P.dma("sp", vt_[:], ZT[rows, 4096 + h * 512:4096 + (h + 1) * 512], r=[ZT], w=[vt_])
            c_, s_ = cosT[:, tt, :], sinT[:, tt, :]
            P.op("dve", lambda e, c_=c_: e.tensor_tensor(out=u0[:], in0=kt_[:, 0:128], in1=c_, op=ALU.mult), r=[kt_, rope], w=[u0])
            P.op("pool", lambda e, s_=s_: e.tensor_tensor(out=u1[:], in0=kt_[:, 128:256], in1=s_, op=ALU.mult), r=[kt_, rope], w=[u1])
            P.op("dve", lambda e: e.tensor_tensor(out=kr_[:, 0:128], in0=u0[:], in1=u1[:], op=ALU.subtract), r=[u0, u1], w=[kr_])
            P.op("dve", lambda e, c_=c_: e.tensor_tensor(out=u0[:], in0=kt_[:, 128:256], in1=c_, op=ALU.mult), r=[kt_, rope], w=[u0])
            P.op("pool", lambda e, s_=s_: e.tensor_tensor(out=u1[:], in0=kt_[:, 0:128], in1=s_, op=ALU.mult), r=[kt_, rope], w=[u1])
            P.op("dve", lambda e: e.tensor_tensor(out=kr_[:, 128:256], in0=u0[:], in1=u1[:], op=ALU.add), r=[u0, u1], w=[kr_])
            P.op("dve", lambda e, tt=tt, h=h: e.tensor_scalar(out=kd[:, tt, :], in0=kr_[:], scalar1=kdec[:, h:h + 1], scalar2=None,
                                                              op0=ALU.mult), r=[kr_, kdec], w=[kd])
            P.op("pool", lambda e, tt=tt: e.tensor_copy(out=vb[:, tt, :], in_=vt_[:]), r=[vt_], w=[vb])
        P.dma("sp", KDv[:, :, h * 256:(h + 1) * 256], kd[:], r=[kd], w=[KD])
        P.dma("sp", VBv[:, :, h * 512:(h + 1) * 512], vb[:], r=[vb], w=[VB])
        for which, base, outs in (("q", 0, (oA, oB)), ("k", 4096, (oA, oB))):
            P.dma("sp", XA[:], ZF[base + h * 256:base + h * 256 + 128, :], r=[ZF], w=[XA])
            P.dma("sp", XB[:], ZF[base + h * 256 + 128:base + h * 256 + 256, :], r=[ZF], w=[XB])
            P.op("dve", lambda e: e.tensor_tensor(out=W0[:], in0=XA[:], in1=cosF, op=ALU.mult), r=[XA, rope], w=[W0])
            P.op("pool", lambda e: e.tensor_tensor(out=W1[:], in0=XB[:], in1=sinF, op=ALU.mult), r=[XB, rope], w=[W1])
            P.op("dve", lambda e: e.tensor_tensor(out=W0[:], in0=W0[:], in1=W1[:], op=ALU.subtract), r=[W0, W1], w=[W0])
            P.op("act", lambda e: e.activation(out=oA[:], in_=W0[:], func=AF.Copy), r=[W0], w=[oA])
            if which == "q":
                P.op("dve", lambda e, h=h: e.tensor_tensor(
                    out=dA[:].rearrange("p (c s) -> p c s", s=64), in0=W0[:].rearrange("p (c s) -> p c s", s=64),
                    in1=qdec[:, h, :].unsqueeze(1).to_broadcast([128, NCH, 64]), op=ALU.mult), r=[W0, qdec], w=[dA])
            P.op("pool", lambda e: e.tensor_tensor(out=W1[:], in0=XA[:], in1=sinF, op=ALU.mult), r=[XA, rope], w=[W1])
            P.op("dve", lambda e: e.tensor_tensor(out=W0[:], in0=XB[:], in1=cosF, op=ALU.mult), r=[XB, rope], w=[W0])
            P.op("dve", lambda e: e.tensor_tensor(out=W0[:], in0=W0[:], in1=W1[:], op=ALU.add), r=[W0, W1], w=[W0])
            P.op("act", lambda e: e.activation(out=oB[:], in_=W0[:], func=AF.Copy), r=[W0], w=[oB])
            if which == "q":
                P.op("dve", lambda e, h=h: e.tensor_tensor(
                    out=dB[:].rearrange("p (c s) -> p c s", s=64), in0=W0[:].rearrange("p (c s) -> p c s", s=64),
                    in1=qdec[:, h, :].unsqueeze(1).to_broadcast([128, NCH, 64]), op=ALU.mult), r=[W0, qdec], w=[dB])
            dst = QR if which == "q" else KR
            P.dma("sp", dst[h * 256:h * 256 + 128, :], oA[:], r=[oA], w=[dst])
            P.dma("sp", dst[h * 256 + 128:h * 256 + 256, :], oB[:], r=[oB], w=[dst])
            if which == "q":
                P.dma("sp", QD[h * 256:h * 256 + 128, :], dA[:], r=[dA], w=[QD])
                P.dma("sp", QD[h * 256 + 128:h * 256 + 256, :], dB[:], r=[dB], w=[QD])
        for dtl in range(2):
            P.op("dve", lambda e, dtl=dtl: e.memset(R[dtl][:], 0.0), w=[R[dtl]])
        g64 = float(gam[h] ** 64)
        for c in range(NCH):
            tt, b0 = c // 2, (c % 2) * 64
            for dtl in range(2):
                pu = psU[c % 2][dtl]
                P.op("pe", lambda e, tt=tt, b0=b0, dtl=dtl, pu=pu: e.matmul(
                    pu[:], kd[b0:b0 + 64, tt, dtl * 128:(dtl + 1) * 128], vb[b0:b0 + 64, tt, :], start=True, stop=True),
                    r=[kd, vb], w=[pu])
                P.op("dve", lambda e, dtl=dtl, pu=pu, g64=g64: e.scalar_tensor_tensor(
                    out=R[dtl][:], in0=R[dtl][:], scalar=g64, in1=pu[:], op0=ALU.mult, op1=ALU.add),
                    r=[R[dtl], pu], w=[R[dtl]])
        for dtl in range(2):
            P.dma("sp", RLd[:, h, dtl, :], R[dtl][:], r=[R[dtl]], w=[RLd])
    P.release(m0)


def ret_pass2(P, C, T, ZF, rc, KD, VB, QR, KR, QD, RId, OT):
    NT = T // 128
    gam = ret_gammas()
    cb = C["cb"]
    m0 = P.mark()
    decT = P.sb([16, 128], F32, "q_decT")
    P.dma("sp", decT[:].rearrange("p a b -> p (a b)"), rc["decT"], w=[decT])
    od512 = P.sb([128], F32, "q_od")
    P.op("dve", lambda e: e.memset(od512[:], 1.0 / 512), w=[od512])
    kd = P.sb([NT, 256], BF16, "q_kd")
    vb = P.sb([NT, 512], BF16, "q_vb")
    qr = P.sb([2, T], BF16, "q_qr")
    kr = P.sb([2, T], BF16, "q_kr")
    qd = P.sb([2, T], BF16, "q_qd")
    sg = P.sb([4, T], F32, "q_sg")
    og = P.sb([4, T], BF16, "q_og")
    R = [P.sb([512], F32, "q_R%d" % i) for i in range(2)]
    Rb = [P.sb([512], BF16, "q_Rb%d" % i) for i in range(2)]
    sm = P.sb([128], BF16, "q_sm")
    osb = P.sb([4, 128], F32, "q_osb")
    sq = P.sb([4, 128], F32, "q_sq")
    rsd = P.sb([128], F32, "q_rsd")
    KDv = KD.t.rearrange("(nt p) c -> p nt c", p=128)
    VBv = VB.t.rearrange("(nt p) c -> p nt c", p=128)
    psS, psO, psM = P.ps[4], P.ps[5], P.ps[6]
    psU = [[P.ps[0], P.ps[1]], [P.ps[2], P.ps[3]]]
    psOv = psO.t.rearrange("p (j t) -> p j t", j=4)
    for h in range(16):
        g64 = float(gam[h] ** 64)
        P.dma("sp", kd[:], KDv[:, :, h * 256:(h + 1) * 256], r=[KD], w=[kd])
        P.dma("sp", vb[:], VBv[:, :, h * 512:(h + 1) * 512], r=[VB], w=[vb])
        for dtl in range(2):
            rs_ = slice(h * 256 + dtl * 128, h * 256 + (dtl + 1) * 128)
            P.dma("sp", qr[:, dtl, :], QR[rs_, :], r=[QR], w=[qr])
            P.dma("sp", kr[:, dtl, :], KR[rs_, :], r=[KR], w=[kr])
            P.dma("sp", qd[:, dtl, :], QD[rs_, :], r=[QD], w=[qd])
            P.dma("sp", R[dtl][:], RId[:, h, dtl, :], r=[RId], w=[R[dtl]])
            P.op("act", lambda e, dtl=dtl: e.activation(out=Rb[dtl][:], in_=R[dtl][:], func=AF.Copy), r=[R[dtl]], w=[Rb[dtl]])
        for j in range(4):
            P.dma("sp", sg[:, j, :], ZF[8192 + h * 512 + j * 128:8192 + h * 512 + (j + 1) * 128, :], r=[ZF], w=[sg])
        for tt in range(NT):
            ts = slice(tt * 128, (tt + 1) * 128)
            for dtl in range(2):
                P.op("pe", lambda e, dtl=dtl, ts=ts: e.matmul(psS[:, 0:128], kr[:, dtl, ts], qr[:, dtl, ts],
                                                              start=(dtl == 0), stop=(dtl == 1)), r=[kr, qr], w=[psS])
            P.op("dve", lambda e, h=h: e.tensor_tensor(out=sm[:], in0=psS[:, 0:128], in1=decT[:, h, :], op=ALU.mult),
                 r=[psS, decT], w=[sm])
            for ci in range(2):
                c = 2 * tt + ci
                b0 = ci * 64
                for j in range(4):
                    P.op("pe", lambda e, j=j, tt=tt, b0=b0: e.matmul(
                        psOv[:, j, b0:b0 + 64], vb[b0:b0 + 64, tt, j * 128:(j + 1) * 128], sm[b0:b0 + 64, b0:b0 + 64],
                        start=True, stop=False), r=[vb, sm], w=[psO])
                    for dtl in range(2):
                        P.op("pe", lambda e, j=j, dtl=dtl, b0=b0, c=c: e.matmul(
                            psOv[:, j, b0:b0 + 64], Rb[dtl][:, j * 128:(j + 1) * 128], qd[:, dtl, c * 64:(c + 1) * 64],
                            start=False, stop=(dtl == 1)), r=[Rb[dtl], qd], w=[psO])
                for dtl in range(2):
                    pu = psU[c % 2][dtl]
                    P.op("pe", lambda e, tt=tt, b0=b0, dtl=dtl, pu=pu: e.matmul(
                        pu[:], kd[b0:b0 + 64, tt, dtl * 128:(dtl + 1) * 128], vb[b0:b0 + 64, tt, :], start=True, stop=True),
                        r=[kd, vb], w=[pu])
                    P.op("dve", lambda e, dtl=dtl, pu=pu, g64=g64: e.scalar_tensor_tensor(
                        out=R[dtl][:], in0=R[dtl][:], scalar=g64, in1=pu[:], op0=ALU.mult, op1=ALU.add),
                        r=[R[dtl], pu], w=[R[dtl]])
                    P.op("act", lambda e, dtl=dtl: e.activation(out=Rb[dtl][:], in_=R[dtl][:], func=AF.Copy),
                         r=[R[dtl]], w=[Rb[dtl]])
            P.op("act", lambda e: e.activation(out=osb[:], in_=psOv, func=AF.Copy), r=[psO], w=[osb])
            P.op("act", lambda e: e.activation(out=sq[:], in_=psOv, func=AF.Square), r=[psO], w=[sq])
            for j in range(4):
                P.op("pe", lambda e, j=j: e.matmul(psM[:, 0:128], od512[:], sq[:, j, :], start=(j == 0), stop=(j == 3)),
                     r=[od512, sq], w=[psM])
            P.op("act", lambda e: e.activation(out=rsd[:], in_=psM[:, 0:128], func=AF.Sqrt, bias=C["eps"][:], scale=1.0),
                 r=[psM, C["eps"]], w=[rsd])
            P.op("dve", lambda e: e.reciprocal(out=rsd[:], in_=rsd[:]), r=[rsd], w=[rsd])
            P.op("dve", lambda e: e.tensor_tensor(out=osb[:], in0=osb[:], in1=rsd[:].unsqueeze(1).to_broadcast([128, 4, 128]),
                                                  op=ALU.mult), r=[osb, rsd], w=[osb])
            P.op("dve", lambda e, ts=ts: e.tensor_tensor(out=og[:, :, ts], in0=osb[:], in1=sg[:, :, ts], op=ALU.mult),
                 r=[osb, sg], w=[og])
        for j in range(4):
            P.dma("sp", OT[h * 512 + j * 128:h * 512 + (j + 1) * 128, :], og[:, j, :], r=[og], w=[OT])
    P.release(m0)


def gather_states(P, src, dst, ncores):
    if ncores == 1:
        P.dma("sp", dst.t, src.t, r=[src], w=[dst])
    else:
        P.collective("AllGather", src.handle.ap().opt(), dst.handle.ap().opt(), [list(range(ncores))],
                     r=[src], w=[dst])


def hgrn2_combine(P, C, SLall, sel, SId, ncores, seg_per_batch):
    m0 = P.mark()
    F = P.sb([16, 128], F32, "hc_F")
    SI = P.sb([16, 128], F32, "hc_SI")
    L = P.sb([2064], F32, "hc_L")
    P.op("dve", lambda e: e.memset(F[:], 0.0), w=[F])
    P.op("dve", lambda e: e.memset(SI[:], 0.0), w=[SI])
    for r in range(ncores):
        P.op("dve", lambda e, r=r: e.scalar_tensor_tensor(out=SI[:], in0=F[:], scalar=sel[:, r:r + 1], in1=SI[:],
                                                          op0=ALU.mult, op1=ALU.add), r=[F, sel, SI], w=[SI])
        if r == ncores - 1:
            break
        if (r + 1) % seg_per_batch == 0:
            P.op("dve", lambda e: e.memset(F[:], 0.0), w=[F])
            continue
        P.dma("sp", L[:], SLall[r * 128:(r + 1) * 128, :], r=[SLall], w=[L])
        P.op("dve", lambda e: e.tensor_tensor(out=F[:], in0=F[:], in1=L[:, 2048:2064].unsqueeze(2).to_broadcast([128, 16, 128]),
                                              op=ALU.mult), r=[F, L], w=[F])
        P.op("dve", lambda e: e.tensor_tensor(out=F[:], in0=F[:], in1=L[:, 0:2048].rearrange("p (h d) -> p h d", h=16),
                                              op=ALU.add), r=[F, L], w=[F])
    P.dma("sp", SId.t, SI[:].rearrange("p h d -> p (h d)"), r=[SI], w=[SId])
    P.release(m0)


def ret_combine(P, C, T, RLall, sel, RId, ncores, seg_per_batch):
    gam = ret_gammas()
    m0 = P.mark()
    F = P.sb([512], F32, "rc_F")
    SI = P.sb([512], F32, "rc_SI")
    L = [P.sb([512], F32, "rc_L%d" % i) for i in range(2)]
    RLv = RLall.t.rearrange("r (h d e) -> r h d e", h=16, d=2)
    k = 0
    for h in range(16):
        gT = float(gam[h] ** T)
        for dtl in range(2):
            P.op("dve", lambda e: e.memset(F[:], 0.0), w=[F])
            P.op("dve", lambda e: e.memset(SI[:], 0.0), w=[SI])
            for r in range(ncores):
                P.op("dve", lambda e, r=r: e.scalar_tensor_tensor(out=SI[:], in0=F[:], scalar=sel[:, r:r + 1], in1=SI[:],
                                                                  op0=ALU.mult, op1=ALU.add), r=[F, sel, SI], w=[SI])
                if r == ncores - 1:
                    break
                if (r + 1) % seg_per_batch == 0:
                    P.op("dve", lambda e: e.memset(F[:], 0.0), w=[F])
                    continue
                Lb = L[k % 2]
                k += 1
                P.dma("sp", Lb[:], RLv[r * 128:(r + 1) * 128, h, dtl, :], r=[RLall], w=[Lb])
                P.op("dve", lambda e, gT=gT, Lb=Lb: e.scalar_tensor_tensor(out=F[:], in0=F[:], scalar=gT, in1=Lb[:],
                                                                          op0=ALU.mult, op1=ALU.add), r=[F, Lb], w=[F])
            P.dma("sp", RId[:, h, dtl, :], SI[:], r=[SI], w=[RId])
    P.release(m0)


def s5_combine(P, C, T, prm, SXall, sel, XId, ncores, seg_per_batch):
    m0 = P.mark()
    apr = P.sb([64], F32, "sc_apr")
    api = P.sb([64], F32, "sc_api")
    ldp = P.sb([64], F32, "sc_ldp")
    for b_, nm in ((apr, "are_p"), (api, "aim_p"), (ldp, "ldt_p")):
        P.dma("sp", b_[:], prm[nm], w=[b_])
    mg = P.sb([64], F32, "sc_mg")
    th = P.sb([64], F32, "sc_th")
    sn, cs, t0, t1 = (P.sb([64], F32, "sc_t%d" % i) for i in range(4))
    P.op("act", lambda e: e.activation(out=ldp[:], in_=ldp[:], func=AF.Exp), r=[ldp], w=[ldp])
    P.op("dve", lambda e: e.tensor_scalar(out=apr[:], in0=apr[:], scalar1=-1e-4, scalar2=None, op0=ALU.min), r=[apr], w=[apr])
    P.op("dve", lambda e: e.tensor_tensor(out=mg[:], in0=apr[:], in1=ldp[:], op=ALU.mult), r=[apr, ldp], w=[mg])
    P.op("act", lambda e: e.activation(out=mg[:], in_=mg[:], func=AF.Exp, scale=float(T)), r=[mg], w=[mg])
    P.op("dve", lambda e: e.tensor_tensor(out=th[:], in0=api[:], in1=ldp[:], op=ALU.mult), r=[api, ldp], w=[th])
    P.op("dve", lambda e: e.tensor_scalar(out=t0[:], in0=th[:], scalar1=1.0 / TWO_PI, scalar2=MAGIC, op0=ALU.mult, op1=ALU.add),
         r=[th], w=[t0])
    P.op("dve", lambda e: e.tensor_scalar(out=t0[:], in0=t0[:], scalar1=-MAGIC, scalar2=-TWO_PI, op0=ALU.add, op1=ALU.mult),
         r=[t0], w=[t0])
    P.op("dve", lambda e: e.tensor_tensor(out=th[:], in0=th[:], in1=t0[:], op=ALU.add), r=[th, t0], w=[th])
    P.op("dve", lambda e: e.tensor_scalar(out=th[:], in0=th[:], scalar1=float(T), scalar2=None, op0=ALU.mult), r=[th], w=[th])
    sincos(P, C, th, sn, cs, t0, t1)
    ar, ai = cs, sn
    P.op("dve", lambda e: e.tensor_tensor(out=ar[:], in0=cs[:], in1=mg[:], op=ALU.mult), r=[cs, mg], w=[ar])
    P.op("dve", lambda e: e.tensor_tensor(out=ai[:], in0=sn[:], in1=mg[:], op=ALU.mult), r=[sn, mg], w=[ai])
    F = P.sb([64, 2], F32, "sc_F")
    SI = P.sb([64, 2], F32, "sc_SI")
    L = P.sb([64, 2], F32, "sc_L")
    nr = P.sb([64], F32, "sc_nr")
    ni = P.sb([64], F32, "sc_ni")
    P.op("dve", lambda e: e.memset(F[:], 0.0), w=[F])
    P.op("dve", lambda e: e.memset(SI[:], 0.0), w=[SI])
    SXv = SXall.t.rearrange("r (m two) -> r m two", two=2)
    for r in range(ncores):
        P.op("dve", lambda e, r=r: e.scalar_tensor_tensor(out=SI[:], in0=F[:], scalar=sel[:, r:r + 1], in1=SI[:],
                                                          op0=ALU.mult, op1=ALU.add), r=[F, sel, SI], w=[SI])
        if r == ncores - 1:
            break
        if (r + 1) % seg_per_batch == 0:
            P.op("dve", lambda e: e.memset(F[:], 0.0), w=[F])
            continue
        P.dma("sp", L[:], SXv[r * 128:(r + 1) * 128, :, :], r=[SXall], w=[L])
        P.op("dve", lambda e: e.tensor_tensor(out=t0[:], in0=F[:, :, 0], in1=ar[:], op=ALU.mult), r=[F, ar], w=[t0])
        P.op("dve", lambda e: e.tensor_tensor(out=t1[:], in0=F[:, :, 1], in1=ai[:], op=ALU.mult), r=[F, ai], w=[t1])
        P.op("dve", lambda e: e.tensor_tensor(out=nr[:], in0=t0[:], in1=t1[:], op=ALU.subtract), r=[t0, t1], w=[nr])
        P.op("dve", lambda e: e.tensor_tensor(out=t0[:], in0=F[:, :, 1], in1=ar[:], op=ALU.mult), r=[F, ar], w=[t0])
        P.op("dve", lambda e: e.tensor_tensor(out=t1[:], in0=F[:, :, 0], in1=ai[:], op=ALU.mult), r=[F, ai], w=[t1])
        P.op("dve", lambda e: e.tensor_tensor(out=ni[:], in0=t0[:], in1=t1[:], op=ALU.add), r=[t0, t1], w=[ni])
        P.op("dve", lambda e: e.tensor_tensor(out=F[:, :, 0], in0=nr[:], in1=L[:, :, 0], op=ALU.add), r=[nr, L], w=[F])
        P.op("dve", lambda e: e.tensor_tensor(out=F[:, :, 1], in0=ni[:], in1=L[:, :, 1], op=ALU.add), r=[ni, L], w=[F])
    P.dma("sp", XId.t, SI[:], r=[SI], w=[XId])
    P.release(m0)


def final_norm_stage(P, C, Y, T, gfin_d):
    m0 = P.mark()
    grep_ = P.sb([4096], F32, "fn_g")
    P.dma("sp", grep_[:], gfin_d, w=[grep_])
    xt = [P.sb([4096], F32, "fn_x%d" % i) for i in range(2)]
    jk = P.sb([4096], BF16, "fn_j")
    ss = P.sb([1], F32, "fn_ss")
    rs = P.sb([1], F32, "fn_rs")
    for tt in range(T // 128):
        x_ = xt[tt % 2]
        rows = slice(tt * 128, (tt + 1) * 128)
        P.dma("sp", x_[:], Y[rows, :], r=[Y], w=[x_])
        P.op("dve", lambda e: e.memset(ss[:], 0.0), w=[ss])
        P.op("act", lambda e, x_=x_: e.activation(out=jk[:], in_=x_[:], func=AF.Square, accum_out=ss[:]), r=[x_, ss], w=[jk, ss])
        P.op("act", lambda e: e.activation(out=rs[:], in_=ss[:], func=AF.Sqrt, scale=1.0 / D, bias=C["eps"][:]), r=[ss, C["eps"]], w=[rs])
        P.op("dve", lambda e: e.reciprocal(out=rs[:], in_=rs[:]), r=[rs], w=[rs])
        P.op("dve", lambda e, x_=x_: e.scalar_tensor_tensor(out=x_[:], in0=x_[:], scalar=rs[:], in1=grep_[:], op0=ALU.mult, op1=ALU.mult),
             r=[x_, rs, grep_], w=[x_])
        P.dma("sp", Y[rows, :], x_[:], r=[x_], w=[Y])
    P.release(m0)


WEIGHTS = [("ab_w_in", 4096, 10240), ("b_w_glu", 2048, 2048), ("ab_w_out", 4096, 4096),
           ("wq0", 4096, 2048), ("ut0", 4096, 16384), ("v0", 16384, 4096),
           ("c_w_in", 4096, 24576), ("c_w_out", 8192, 4096),
           ("wq1", 4096, 2048), ("ut1", 4096, 16384), ("v1", 16384, 4096)]


PART_W = {0: ("ab_w_in", "b_w_glu", "ab_w_out", "wq0", "ut0", "v0"),
          1: ("c_w_in", "c_w_out", "wq1", "ut1", "v1"),
          "1a": ("c_w_in", "c_w_out"), "1b": ("wq1", "ut1", "v1")}


def part_weights(parts):
    names = [n for p in parts for n in PART_W[p]]
    return [w for w in WEIGHTS if w[0] in names]


def build_full(T, ncores, seg_per_batch, parts=(0, 1)):
    nc = bass.Bass("TRN2", target_bir_lowering=False)
    dt_in = lambda name, shape: nc.dram_tensor(name, list(shape), F32, kind="ExternalInput").ap()
    x_in = dt_in("x", [T, 4096])
    y = nc.dram_tensor("y", [T, 4096], F32, kind="ExternalOutput").ap()
    cnp, cols = make_consts(T)
    consts_d = dt_in("consts", cnp.shape)
    rope_d = dt_in("rope", [128, 4 * T])
    rc = {"decT": dt_in("decT", [128, 2048]), "qdec": dt_in("qdec", [128, 1024]), "kdec": dt_in("kdec", [128, 16])}
    sel_d = dt_in("sel", [128, 8])
    gams = [dt_in("gam%d" % i, [128, 32]) for i in range(4)]
    gfin_d = dt_in("gfin", [128, 4096])
    lbp_d = dt_in("lbp", [128, 2, 16])
    lbrow_d = dt_in("lbrow", [128, 2, 2048])
    prm = {k: dt_in("s5_" + k, v) for k, v in S5_SHAPES.items()}
    keys = [dt_in("keys%d" % i, [128, 16, 128]) for i in range(2)]
    with ExitStack() as st:
        P = Prog(nc, st)
        C = setup_consts(P)
        load_consts(P, C, consts_d, cols)
        sel = P.sb([8], F32, "sel")
        P.dma("sp", sel[:], sel_d, w=[sel])
        Y = Buf(y, "Y")
        for r0 in range(0, T, 256):
            P.dma("sp", y[r0:r0 + 256, :], x_in[r0:r0 + 256, :], w=[Y])
        W = {}
        for name, K, N in part_weights(parts):
            W[name] = Weight(P, name, K, N, ncores)
            W[name].distribute(P)
        NCH = T // 64
        if 0 in parts:
            build_layer0(P, C, Y, T, ncores, seg_per_batch, W, gams, lbp_d, lbrow_d, prm, keys, sel)
        if 1 in parts or "1a" in parts or "1b" in parts:
            build_layer1(P, C, Y, T, ncores, seg_per_batch, W, gams, rope_d, rc, keys, sel, gfin_d,
                         mixer=(1 in parts or "1a" in parts), ffn=(1 in parts or "1b" in parts))
        stats = P.emit()
    return nc, cnp, stats


def build_layer0(P, C, Y, T, ncores, seg_per_batch, W, gams, lbp_d, lbrow_d, prm, keys, sel):
    if True:
        NCH = T // 64
        ZF = P.dram("ZF0", [8192, T], F32)
        ZT = P.dram("ZT0", [T, 4096], F32)
        KH = P.dram("KH", [T, 2048], BF16)
        VB = P.dram("VB", [T, 2048], BF16)
        QT = P.dram("QT", [2048, T], BF16)
        KT = P.dram("KT", [2048, T], BF16)
        DECd = P.dram("DEC", [128, 16 * NCH], F32)
        SLd = P.dram("SL", [128, 2064], F32)
        SLall = P.dram("SLall", [ncores * 128, 2064], F32)
        SId = P.dram("SI", [128, 2048], F32)
        SXd = P.dram("SX", [128, 128], F32)
        SXall = P.dram("SXall", [ncores * 128, 128], F32)
        XId = P.dram("XI", [128, 128], F32)
        Y2 = P.dram("Y2", [2048, T], F32)
        OT0 = P.dram("OT0", [4096, T], BF16)
        inproj_stage(P, C, Y, T, gams[0], W["ab_w_in"].full,
                     [(0, 2048, ZF, 0, AF.Silu), (2048, 2048, ZF, 2048, AF.Sigmoid),
                      (6144, 2048, ZF, 4096, AF.Silu), (8192, 2048, ZF, 6144, AF.Copy)],
                     [(2048, 2048, ZT, 0, AF.Sigmoid), (4096, 2048, ZT, 2048, AF.Copy)])
        hgrn2_pass1(P, C, T, ZF, ZT, lbp_d, lbrow_d, KH, VB, QT, KT, DECd, SLd)
        SXd3 = Buf(SXd.t.rearrange("p (m two) -> p m two", two=2), "SX3")
        XId3 = Buf(XId.t.rearrange("p (m two) -> p m two", two=2), "XI3")
        s5_scan(P, C, T, ZF, prm, None, SXd3, None)
        P.barrier()
        SXd.w = SXd3.w
        gather_states(P, SLd, SLall, ncores)
        gather_states(P, SXd, SXall, ncores)
        hgrn2_combine(P, C, SLall, sel, SId, ncores, seg_per_batch)
        s5_combine(P, C, T, prm, SXall, sel, XId, ncores, seg_per_batch)
        XId3.w = XId.w
        hgrn2_pass2(P, C, T, ZF, KH, VB, QT, KT, DECd, SId, OT0)
        s5_scan(P, C, T, ZF, prm, XId3, None, Y2)
        glu_stage(P, C, T, Y2, W["b_w_glu"].full, OT0, 2048)
        outproj_stage(P, C, Y, T, OT0, W["ab_w_out"].full, 4096)
        peer_stage(P, C, Y, T, gams[1], W["wq0"].full, keys[0], W["ut0"].full, W["v0"].full)


def build_layer1(P, C, Y, T, ncores, seg_per_batch, W, gams, rope_d, rc, keys, sel, gfin_d, mixer=True, ffn=True):
    if mixer:
        ZF1 = P.dram("ZF1", [16384, T], F32)
        ZT1 = P.dram("ZT1", [T, 12288], F32)
        KD = P.dram("KD", [T, 4096], BF16)
        VB1 = P.dram("VB1", [T, 8192], BF16)
        QR = P.dram("QR", [4096, T], BF16)
        KR = P.dram("KR", [4096, T], BF16)
        QD = P.dram("QD", [4096, T], BF16)
        RL2 = P.dram("RL", [128, 16384], F32)
        RLall = P.dram("RLall", [ncores * 128, 16384], F32)
        RI2 = P.dram("RI", [128, 16384], F32)
        RLd = Buf(RL2.t.rearrange("p (h d e) -> p h d e", h=16, d=2), "RL4")
        RId = Buf(RI2.t.rearrange("p (h d e) -> p h d e", h=16, d=2), "RI4")
        OT1 = P.dram("OT1", [8192, T], BF16)
        inproj_stage(P, C, Y, T, gams[2], W["c_w_in"].full,
                     [(0, 4096, ZF1, 0, AF.Copy), (4096, 4096, ZF1, 4096, AF.Copy), (16384, 8192, ZF1, 8192, AF.Silu)],
                     [(4096, 4096, ZT1, 0, AF.Copy), (8192, 8192, ZT1, 4096, AF.Copy)])
        ret_pass1(P, C, T, ZF1, ZT1, rope_d, rc, KD, VB1, QR, KR, QD, RLd)
        P.barrier()
        RL2.w = RLd.w
        gather_states(P, RL2, RLall, ncores)
        ret_combine(P, C, T, RLall, sel, RId, ncores, seg_per_batch)
        ret_pass2(P, C, T, ZF1, rc, KD, VB1, QR, KR, QD, RId, OT1)
        outproj_stage(P, C, Y, T, OT1, W["c_w_out"].full, 8192)
    if ffn:
        peer_stage(P, C, Y, T, gams[3], W["wq1"].full, keys[1], W["ut1"].full, W["v1"].full)
        final_norm_stage(P, C, Y, T, gfin_d)


def host_inputs(inp, T, ncores, seg_per_batch, cnp, parts=(0, 1), x_override=None):
    f = lambda a: np.ascontiguousarray(np.asarray(a, dtype=np.float32))
    x = f(inp["x"] if x_override is None else x_override).reshape(-1, 4096)
    gl = lambda g: f(np.asarray(g).reshape(32, 128).T)
    decT, qdec, kdec = make_ret_consts()
    common = {"consts": cnp, "decT": decT, "qdec": qdec, "kdec": kdec,
              "gam0": gl(inp["mix_norm_g"][0]), "gam1": gl(inp["ffn_norm_g"][0]),
              "gam2": gl(inp["mix_norm_g"][1]), "gam3": gl(inp["ffn_norm_g"][1]),
              "gfin": f(np.broadcast_to(np.asarray(inp["final_norm_g"])[None, :], (128, 4096)))}
    lbparam = np.asarray(inp["a_lb_param"], np.float32)
    common["lbp"] = f(lbparam.reshape(2, 16, 128).transpose(2, 0, 1))
    common["lbrow"] = f(np.broadcast_to(lbparam[None], (128, 2, 2048)))
    hp = s5_host_params(*(np.asarray(inp[k][0], np.float32) for k in
                          ("b_a_re", "b_a_im", "b_log_dt", "b_b_re", "b_b_im", "b_c_re", "b_c_im", "b_d")))
    common.update({"s5_" + k: v for k, v in hp.items()})
    for l in range(2):
        keys = np.asarray(inp["peer_sub_keys"][l], np.float32)
        common["keys%d" % l] = f(keys.reshape(16, 128, 128).transpose(2, 0, 1))
    wsrc = {k: inp[k][0] for k in ("ab_w_in", "b_w_glu", "ab_w_out", "c_w_in", "c_w_out") if k in inp}
    shards = {}
    for name, K, N in part_weights(parts):
        if name.startswith("wq"):
            Wm = np.asarray(inp["peer_w_q"][int(name[2])], np.float32)
        elif name.startswith("ut"):
            Wm = np.ascontiguousarray(np.asarray(inp["peer_u"][int(name[2])], np.float32).T)
        elif name.startswith("v") and name[1:].isdigit():
            Wm = np.asarray(inp["peer_v"][int(name[1])], np.float32)
        else:
            Wm = np.asarray(wsrc[name], np.float32)
        assert Wm.shape == (K, N), (name, Wm.shape)
        shards[name] = shard_rows(Wm, ncores)
        del Wm
    maps = []
    for r in range(ncores):
        m = dict(common)
        m["x"] = f(x[r * T:(r + 1) * T])
        seg = r % seg_per_batch
        m["rope"] = make_rope(seg * T, T)
        s_ = np.zeros((128, 8), np.float32)
        s_[:, r] = 1.0
        m["sel"] = s_
        for name, _, _ in part_weights(parts):
            m[name] = shards[name][r]
        maps.append(m)
    return maps


W_NS = {"ab_w_in": 512, "b_w_glu": 512, "ab_w_out": 512, "wq0": 512, "ut0": 512, "v0": None,
        "c_w_in": 512, "c_w_out": 256, "wq1": 512, "ut1": 512, "v1": None}
LAUNCH_W = {"A0": ("ab_w_in",), "B0": ("ab_w_in", "b_w_glu", "ab_w_out", "wq0", "ut0", "v0"),
            "A1": ("c_w_in",), "B1a": ("c_w_in", "c_w_out"), "B1b": ("wq1", "ut1", "v1")}


def build_launch(T, mode, ncores=8, spb=4):
    nc = bass.Bass("TRN2", target_bir_lowering=False)
    dt_in = lambda name, shape: nc.dram_tensor(name, list(shape), F32, kind="ExternalInput").ap()
    dt_out = lambda name, shape: nc.dram_tensor(name, list(shape), F32, kind="ExternalOutput").ap()
    x_in = dt_in("x", [T, 4096])
    cnp, cols = make_consts(T)
    consts_d = dt_in("consts", cnp.shape)
    NCH = T // 64
    with ExitStack() as st:
        P = Prog(nc, st)
        C = setup_consts(P)
        load_consts(P, C, consts_d, cols)
        W = {}
        for name, K, N in WEIGHTS:
            if name in LAUNCH_W[mode]:
                W[name] = Weight(P, name, K, N, 1, W_NS[name])
                W[name].distribute(P)
        if mode in ("B0", "B1a", "B1b"):
            y = dt_out("y", [T, 4096])
            Y = Buf(y, "Y")
            for r0 in range(0, T, 256):
                P.dma("sp", y[r0:r0 + 256, :], x_in[r0:r0 + 256, :], w=[Y])
        else:
            Y = Buf(x_in, "Xin")
        if mode in ("B0", "B1a"):
            sel = P.sb([8], F32, "sel")
            P.dma("sp", sel[:], dt_in("sel", [128, 8]), w=[sel])
        if mode in ("A0", "B0"):
            gam0 = dt_in("gam0", [128, 32])
            lbp_d = dt_in("lbp", [128, 2, 16])
            lbrow_d = dt_in("lbrow", [128, 2, 2048])
            prm = {k: dt_in("s5_" + k, v) for k, v in S5_SHAPES.items()}
            ZF = P.dram("ZF0", [8192, T], F32)
            ZT = P.dram("ZT0", [T, 4096], F32)
            KH = P.dram("KH", [T, 2048], BF16)
            VB = P.dram("VB", [T, 2048], BF16)
            QT = P.dram("QT", [2048, T], BF16)
            KT = P.dram("KT", [2048, T], BF16)
            DECd = P.dram("DEC", [128, 16 * NCH], F32)
            if mode == "A0":
                SLd = Buf(dt_out("sl", [128, 2064]), "SL")
                SXd3 = Buf(dt_out("sx", [128, 64, 2]), "SX")
            else:
                SLd = P.dram("SL", [128, 2064], F32)
            inproj_stage(P, C, Y, T, gam0, W["ab_w_in"].full,
                         [(0, 2048, ZF, 0, AF.Silu), (2048, 2048, ZF, 2048, AF.Sigmoid),
                          (6144, 2048, ZF, 4096, AF.Silu), (8192, 2048, ZF, 6144, AF.Copy)],
                         [(2048, 2048, ZT, 0, AF.Sigmoid), (4096, 2048, ZT, 2048, AF.Copy)])
            hgrn2_pass1(P, C, T, ZF, ZT, lbp_d, lbrow_d, KH, VB, QT, KT, DECd, SLd)
            if mode == "A0":
                s5_scan(P, C, T, ZF, prm, None, SXd3, None)
            else:
                gam1 = dt_in("gam1", [128, 32])
                keys0 = dt_in("keys0", [128, 16, 128])
                SLall = Buf(dt_in("slall", [ncores * 128, 2064]), "SLall")
                SXall = Buf(dt_in("sxall", [ncores * 128, 128]), "SXall")
                SId = P.dram("SI", [128, 2048], F32)
                XId = P.dram("XI", [128, 128], F32)
                XId3 = Buf(XId.t.rearrange("p (m two) -> p m two", two=2), "XI3")
                Y2 = P.dram("Y2", [2048, T], F32)
                OT0 = P.dram("OT0", [4096, T], BF16)
                hgrn2_combine(P, C, SLall, sel, SId, ncores, spb)
                s5_combine(P, C, T, prm, SXall, sel, XId, ncores, spb)
                hgrn2_pass2(P, C, T, ZF, KH, VB, QT, KT, DECd, SId, OT0)
                s5_scan(P, C, T, ZF, prm, XId3, None, Y2)
                glu_stage(P, C, T, Y2, W["b_w_glu"].full, OT0, 2048)
                outproj_stage(P, C, Y, T, OT0, W["ab_w_out"].full, 4096)
                peer_stage(P, C, Y, T, gam1, W["wq0"].full, keys0, W["ut0"].full, W["v0"].full)
        if mode in ("A1", "B1a"):
            gam2 = dt_in("gam2", [128, 32])
            rope_d = dt_in("rope", [128, 4 * T])
            rc = {"decT": dt_in("decT", [128, 2048]), "qdec": dt_in("qdec", [128, 1024]), "kdec": dt_in("kdec", [128, 16])}
            ZF1 = P.dram("ZF1", [16384, T], F32)
            ZT1 = P.dram("ZT1", [T, 12288], F32)
            KD = P.dram("KD", [T, 4096], BF16)
            VB1 = P.dram("VB1", [T, 8192], BF16)
            QR = P.dram("QR", [4096, T], BF16)
            KR = P.dram("KR", [4096, T], BF16)
            QD = P.dram("QD", [4096, T], BF16)
            if mode == "A1":
                rl_ap = dt_out("rl", [128, 16384])
            else:
                rl_ap = P.dram("RL", [128, 16384], F32).t
            RLd = Buf(rl_ap.rearrange("p (h d e) -> p h d e", h=16, d=2), "RL4")
            inproj_stage(P, C, Y, T, gam2, W["c_w_in"].full,
                         [(0, 4096, ZF1, 0, AF.Copy), (4096, 4096, ZF1, 4096, AF.Copy), (16384, 8192, ZF1, 8192, AF.Silu)],
                         [(4096, 4096, ZT1, 0, AF.Copy), (8192, 8192, ZT1, 4096, AF.Copy)])
            ret_pass1(P, C, T, ZF1, ZT1, rope_d, rc, KD, VB1, QR, KR, QD, RLd)
            if mode == "B1a":
                RLall = Buf(dt_in("rlall", [ncores * 128, 16384]), "RLall")
                RI2 = P.dram("RI", [128, 16384], F32)
                RId = Buf(RI2.t.rearrange("p (h d e) -> p h d e", h=16, d=2), "RI4")
                OT1 = P.dram("OT1", [8192, T], BF16)
                ret_combine(P, C, T, RLall, sel, RId, ncores, spb)
                ret_pass2(P, C, T, ZF1, rc, KD, VB1, QR, KR, QD, RId, OT1)
                outproj_stage(P, C, Y, T, OT1, W["c_w_out"].full, 8192)
        if mode == "B1b":
            gam3 = dt_in("gam3", [128, 32])
            keys1 = dt_in("keys1", [128, 16, 128])
            gfin_d = dt_in("gfin", [128, 4096])
            peer_stage(P, C, Y, T, gam3, W["wq1"].full, keys1, W["ut1"].full, W["v1"].full)
            final_norm_stage(P, C, Y, T, gfin_d)
        P.barrier()
        stats = P.emit()
    return nc, cnp, stats


def launch_inputs(inp, T, mode, cnp, xcur, extra, ncores=8, spb=4):
    f = lambda a: np.ascontiguousarray(np.asarray(a, dtype=np.float32))
    gl = lambda g: f(np.asarray(g).reshape(32, 128).T)
    x = xcur.reshape(-1, 4096)
    common = {"consts": cnp}
    if mode in ("A0", "B0"):
        common["gam0"] = gl(inp["mix_norm_g"][0])
        lbparam = np.asarray(inp["a_lb_param"], np.float32)
        common["lbp"] = f(lbparam.reshape(2, 16, 128).transpose(2, 0, 1))
        common["lbrow"] = f(np.broadcast_to(lbparam[None], (128, 2, 2048)))
        hp = s5_host_params(*(np.asarray(inp[k][0], np.float32) for k in
                              ("b_a_re", "b_a_im", "b_log_dt", "b_b_re", "b_b_im", "b_c_re", "b_c_im", "b_d")))
        common.update({"s5_" + k: v for k, v in hp.items()})
        common["ab_w_in"] = tile_w(f(inp["ab_w_in"][0]), 512)
    if mode == "B0":
        common["gam1"] = gl(inp["ffn_norm_g"][0])
        common["keys0"] = f(np.asarray(inp["peer_sub_keys"][0], np.float32).reshape(16, 128, 128).transpose(2, 0, 1))
        common["b_w_glu"] = tile_w(f(inp["b_w_glu"][0]), 512)
        common["ab_w_out"] = tile_w(f(inp["ab_w_out"][0]), 512)
        common["wq0"] = tile_w(f(inp["peer_w_q"][0]), 512)
        common["ut0"] = tile_w(f(np.asarray(inp["peer_u"][0], np.float32).T), 512)
        common["v0"] = f(inp["peer_v"][0])
        common["slall"] = extra["slall"]
        common["sxall"] = extra["sxall"]
    if mode in ("A1", "B1a"):
        decT, qdec, kdec = make_ret_consts()
        common.update({"gam2": gl(inp["mix_norm_g"][1]), "decT": decT, "qdec": qdec, "kdec": kdec,
                       "c_w_in": tile_w(f(inp["c_w_in"][0]), 512)})
    if mode == "B1a":
        common["c_w_out"] = tile_w(f(inp["c_w_out"][0]), 256)
        common["rlall"] = extra["rlall"]
    if mode == "B1b":
        common["gam3"] = gl(inp["ffn_norm_g"][1])
        common["keys1"] = f(np.asarray(inp["peer_sub_keys"][1], np.float32).reshape(16, 128, 128).transpose(2, 0, 1))
        common["gfin"] = f(np.broadcast_to(np.asarray(inp["final_norm_g"])[None, :], (128, 4096)))
        common["wq1"] = tile_w(f(inp["peer_w_q"][1]), 512)
        common["ut1"] = tile_w(f(np.asarray(inp["peer_u"][1], np.float32).T), 512)
        common["v1"] = f(inp["peer_v"][1])
    maps = []
    for r in range(ncores):
        m = dict(common)
        m["x"] = f(x[r * T:(r + 1) * T])
        if mode in ("A1", "B1a"):
            m["rope"] = make_rope((r % spb) * T, T)
        if mode in ("B0", "B1a"):
            s_ = np.zeros((128, 8), np.float32)
            s_[:, r] = 1.0
            m["sel"] = s_
        maps.append(m)
    return maps


_CACHE = {}


def _run(mode, inputs, xcur, extra, T=2048):
    if mode not in _CACHE:
        _CACHE[mode] = build_launch(T, mode)
    nc, cnp, _ = _CACHE[mode]
    maps = launch_inputs(inputs, T, mode, cnp, xcur, extra)
    res = run_bass_kernel_spmd(nc, maps, core_ids=list(range(8)))
    del maps
    return res.results


def kernel(**inputs):
    x = np.ascontiguousarray(np.asarray(inputs["x"], np.float32)).reshape(-1, 4096)
    cat = lambda rs, k: np.ascontiguousarray(np.concatenate([np.asarray(r[k], np.float32).reshape(128, -1) for r in rs], axis=0))
    rs = _run("A0", inputs, x, None)
    extra = {"slall": cat(rs, "sl"), "sxall": cat(rs, "sx")}
    rs = _run("B0", inputs, x, extra)
    x = np.concatenate([np.asarray(r["y"], np.float32) for r in rs], axis=0)
    rs = _run("A1", inputs, x, None)
    extra = {"rlall": cat(rs, "rl")}
    rs = _run("B1a", inputs, x, extra)
    x = np.concatenate([np.asarray(r["y"], np.float32) for r in rs], axis=0)
    rs = _run("B1b", inputs, x, None)
    x = np.concatenate([np.asarray(r["y"], np.float32) for r in rs], axis=0)
    return x.reshape(2, 8192, 4096)
```
